# Optimizing a Trainium2 kernel written in Bass

```python
import jax, jax.numpy as jnp
from jax import lax
import numpy as np

D_MODEL = 1024
BATCH = 8
SEQ = 2048
DEPTH = 1
DEC_BATCH = 16
DEC_SEQ = 64
PAST_LEN = 1024

CHUNK = 64
N_HEADS = 16
HEAD_DIM = 64
ATTN_DIM = N_HEADS * HEAD_DIM
CONV_DIM = 1024
CONV_WIDTH = 3
D_FF = 2816
Q_BLOCK = 128
N_MOD = 9
EPS = 1e-6

OFF_Q = 0
OFF_K = OFF_Q + ATTN_DIM
OFF_V = OFF_K + ATTN_DIM
OFF_F = OFF_V + ATTN_DIM
OFF_B = OFF_F + N_HEADS
OFF_C = OFF_B + CONV_DIM
OFF_X = OFF_C + CONV_DIM
OFF_GA = OFF_X + CONV_DIM
OFF_GC = OFF_GA + ATTN_DIM
IN_COLS = OFF_GC + CONV_DIM
MIX_DIM = ATTN_DIM

kernel_name = "fox_shortconv_macaron_adaln_stream"


def rmsnorm(x, g):
    xf = x.astype(jnp.float32)
    y = xf * lax.rsqrt(jnp.mean(xf * xf, axis=-1, keepdims=True) + EPS)
    return (y * g.astype(jnp.float32)).astype(x.dtype)


def modulate(h, shift, scale):
    return h * (1 + scale[:, None, :]) + shift[:, None, :]


def swiglu(h, w_in, w_out):
    a, b = jnp.split(h @ w_in, 2, axis=-1)
    return (jax.nn.silu(a) * b) @ w_out


def fox_attention(q, k, v, logf, past_k, past_v, past_logf):
    B, H, T, _ = q.shape
    P = past_k.shape[2]
    k_all = jnp.concatenate([past_k.astype(k.dtype), k], axis=2)
    v_all = jnp.concatenate([past_v.astype(v.dtype), v], axis=2)
    F = jnp.cumsum(jnp.concatenate([past_logf.astype(jnp.float32), logf], axis=-1), axis=-1)
    k_pos = jnp.arange(P + T)
    q_pos = P + jnp.arange(T)
    Fq = F[..., P:]
    scale = HEAD_DIM ** -0.5

    def attend(args):
        qb, Fqb, qpb = args
        s = jnp.einsum('bhqd,bhkd->bhqk', qb, k_all, preferred_element_type=jnp.float32) * scale
        s = s + (Fqb[..., :, None] - F[..., None, :])
        s = jnp.where(k_pos[None, :] <= qpb[:, None], s, -jnp.inf)
        p = jax.nn.softmax(s, axis=-1)
        return jnp.einsum('bhqk,bhkd->bhqd', p.astype(v_all.dtype), v_all)

    if T <= Q_BLOCK:
        return attend((q, Fq, q_pos))
    nb = T // Q_BLOCK
    qb = q.reshape(B, H, nb, Q_BLOCK, HEAD_DIM).transpose(2, 0, 1, 3, 4)
    Fqb = Fq.reshape(B, H, nb, Q_BLOCK).transpose(2, 0, 1, 3)
    qpb = q_pos.reshape(nb, Q_BLOCK)
    o = lax.map(attend, (qb, Fqb, qpb))
    return o.transpose(1, 2, 0, 3, 4).reshape(B, H, T, HEAD_DIM)


def short_conv(u, past_u, w):
    T = u.shape[1]
    up = jnp.concatenate([past_u.astype(u.dtype), u], axis=1)
    y = w[0] * up[:, 0:T]
    for j in range(1, CONV_WIDTH):
        y = y + w[j] * up[:, j:j + T]
    return y, up[:, -(CONV_WIDTH - 1):]


def encoder_layer(x, c, past_k, past_v, past_logf, past_u,
                  w_ada, b_ada, g_ffn1, w_ffn1_in, w_ffn1_out, g_mix, w_in, b_f,
                  conv_w, w_out, g_ffn2, w_ffn2_in, w_ffn2_out):
    B, T, _ = x.shape
    mod = jax.nn.silu(c) @ w_ada + b_ada
    sh1, sc1, gt1, sh2, sc2, gt2, sh3, sc3, gt3 = jnp.split(mod, N_MOD, axis=-1)

    h = modulate(rmsnorm(x, g_ffn1), sh1, sc1)
    x = x + 0.5 * gt1[:, None, :] * swiglu(h, w_ffn1_in, w_ffn1_out)

    h = modulate(rmsnorm(x, g_mix), sh2, sc2)
    z = h @ w_in

    def heads(a):
        return a.reshape(B, T, N_HEADS, HEAD_DIM).transpose(0, 2, 1, 3)

    q = heads(z[..., OFF_Q:OFF_K])
    k = heads(z[..., OFF_K:OFF_V])
    v = heads(z[..., OFF_V:OFF_F])
    logf = jax.nn.log_sigmoid((z[..., OFF_F:OFF_B] + b_f).astype(jnp.float32)).transpose(0, 2, 1)
    o_attn = fox_attention(q, k, v, logf, past_k, past_v, past_logf)
    o_attn = o_attn.transpose(0, 2, 1, 3).reshape(B, T, ATTN_DIM)

    u = z[..., OFF_C:OFF_X] * z[..., OFF_X:OFF_GA]
    conv_y, new_u = short_conv(u, past_u, conv_w)
    o_conv = z[..., OFF_B:OFF_C] * conv_y

    m = (jax.nn.sigmoid(z[..., OFF_GA:OFF_GC]) * o_attn
         + jax.nn.sigmoid(z[..., OFF_GC:IN_COLS]) * o_conv)
    x = x + gt2[:, None, :] * (m @ w_out)

    h = modulate(rmsnorm(x, g_ffn2), sh3, sc3)
    x = x + 0.5 * gt3[:, None, :] * swiglu(h, w_ffn2_in, w_ffn2_out)
    return x, k, v, logf, new_u


def setup_inputs(seed: int = 0) -> dict:
    key = jax.random.key(seed)
    ks = jax.random.split(key, 24)
    f32 = jnp.float32
    nrm = lambda k, s, sc: jax.random.normal(k, s, f32) * sc
    gain = lambda k, s: 1.0 + 0.05 * jax.random.normal(k, s, f32)
    return {
        "x_prompt": nrm(ks[0], (BATCH, SEQ, D_MODEL), 1.0),
        "x_sample": nrm(ks[1], (DEC_BATCH, DEC_SEQ, D_MODEL), 1.0),
        "cache_k": nrm(ks[2], (DEPTH, DEC_BATCH, N_HEADS, PAST_LEN, HEAD_DIM), 1.0),
        "cache_v": nrm(ks[3], (DEPTH, DEC_BATCH, N_HEADS, PAST_LEN, HEAD_DIM), 1.0),
        "cache_logf": jax.nn.log_sigmoid(2.5 + jax.random.normal(ks[4], (DEPTH, DEC_BATCH, N_HEADS, PAST_LEN), f32)),
        "state_conv": nrm(ks[5], (DEPTH, DEC_BATCH, CONV_WIDTH - 1, CONV_DIM), 1.0),
        "c_prompt": nrm(ks[6], (BATCH, D_MODEL), 1.0),
        "c_sample": nrm(ks[7], (DEC_BATCH, D_MODEL), 1.0),
        "w_ada": nrm(ks[8], (DEPTH, D_MODEL, N_MOD * D_MODEL), D_MODEL ** -0.5),
        "b_ada": nrm(ks[9], (DEPTH, N_MOD * D_MODEL), 0.02),
        "g_ffn1": gain(ks[10], (DEPTH, D_MODEL)),
        "w_ffn1_in": nrm(ks[11], (DEPTH, D_MODEL, 2 * D_FF), D_MODEL ** -0.5),
        "w_ffn1_out": nrm(ks[12], (DEPTH, D_FF, D_MODEL), D_FF ** -0.5),
        "g_mix": gain(ks[13], (DEPTH, D_MODEL)),
        "w_in": nrm(ks[14], (DEPTH, D_MODEL, IN_COLS), D_MODEL ** -0.5),
        "b_f": jax.random.uniform(ks[15], (DEPTH, N_HEADS), f32, 1.0, 4.0),
        "conv_w": nrm(ks[16], (DEPTH, CONV_WIDTH, CONV_DIM), CONV_WIDTH ** -0.5),
        "w_out": nrm(ks[17], (DEPTH, MIX_DIM, D_MODEL), MIX_DIM ** -0.5),
        "g_ffn2": gain(ks[18], (DEPTH, D_MODEL)),
        "w_ffn2_in": nrm(ks[19], (DEPTH, D_MODEL, 2 * D_FF), D_MODEL ** -0.5),
        "w_ffn2_out": nrm(ks[20], (DEPTH, D_FF, D_MODEL), D_FF ** -0.5),
        "g_final": gain(ks[21], (D_MODEL,)),
    }


def reference(x_prompt, x_sample, cache_k, cache_v, cache_logf, state_conv, c_prompt, c_sample,
              w_ada, b_ada, g_ffn1, w_ffn1_in, w_ffn1_out, g_mix, w_in, b_f, conv_w, w_out,
              g_ffn2, w_ffn2_in, w_ffn2_out, g_final):
    Bp = x_prompt.shape[0]
    empty_kv = jnp.zeros((Bp, N_HEADS, 0, HEAD_DIM), x_prompt.dtype)
    empty_f = jnp.zeros((Bp, N_HEADS, 0), jnp.float32)
    zero_u = jnp.zeros((Bp, CONV_WIDTH - 1, CONV_DIM), x_prompt.dtype)
    xp, xs = x_prompt, x_sample
    kp, vp, fp, up, ksm, vsm, fsm, usm = [], [], [], [], [], [], [], []
    for l in range(DEPTH):
        w = (w_ada[l], b_ada[l], g_ffn1[l], w_ffn1_in[l], w_ffn1_out[l], g_mix[l], w_in[l], b_f[l],
             conv_w[l], w_out[l], g_ffn2[l], w_ffn2_in[l], w_ffn2_out[l])
        xp, k_, v_, f_, u_ = encoder_layer(xp, c_prompt, empty_kv, empty_kv, empty_f, zero_u, *w)
        kp.append(k_); vp.append(v_); fp.append(f_); up.append(u_)
        xs, k_, v_, f_, u_ = encoder_layer(xs, c_sample, cache_k[l], cache_v[l], cache_logf[l],
                                           state_conv[l], *w)
        ksm.append(k_); vsm.append(v_); fsm.append(f_); usm.append(u_)
    y_prompt = rmsnorm(xp, g_final)
    y_sample = rmsnorm(xs, g_final)
    return (y_prompt, y_sample,
            jnp.stack(kp), jnp.stack(vp), jnp.stack(fp), jnp.stack(up),
            jnp.stack(ksm), jnp.stack(vsm), jnp.stack(fsm), jnp.stack(usm))
```

```python
import struct
import numpy as np
import concourse.bass as bass
import concourse.mybir as mybir
from concourse.bass_utils import run_bass_kernel_spmd
from contextlib import ExitStack

F32 = mybir.dt.float32
BF16 = mybir.dt.bfloat16
AF = mybir.ActivationFunctionType
ALU = mybir.AluOpType

NCORES = 8
D = 1024
NT = 2176
NPR = 2048
SQ = 64
PAST = 1024
DFF = 2816
NJ = 22
OFF_Q, OFF_K, OFF_V, OFF_F = 0, 1024, 2048, 3072
OFF_B = OFF_F + 16
OFF_C = OFF_B + 1024
OFF_X = OFF_C + 1024
OFF_GA = OFF_X + 1024
OFF_GC = OFF_GA + 1024
IN_COLS = OFF_GC + 1024
EPS = 1e-6
TBS = [(0, 512), (512, 512), (1024, 512), (1536, 512), (2048, 128)]
NSLOT = 16
PROBE = ""
MASKVAL = -30000.0


def segs(tb):
    if tb < 4:
        return [(0, 512, 0)]
    return [(0, 64, 1), (64, 64, 2)]


ENGS = ["tensor", "vector", "scalar", "gpsimd", "sync"]


class Sem:
    def __init__(self, P, name, step):
        self.h = P.ctx.enter_context(P.nc.semaphore(name))
        self.count = 0
        self.step = step
        self.name = name


class Buf:
    __slots__ = ("name", "w", "r", "excl")

    def __init__(self, name="", excl=False):
        self.name = name
        self.w = None
        self.r = {}
        self.excl = excl


class Prog:
    def __init__(self, nc, ctx):
        self.nc = nc
        self.ctx = ctx
        self.ops = {e: [] for e in ENGS}
        self.esem = {}
        for e in ["tensor", "vector", "scalar", "gpsimd"]:
            self.esem[e] = Sem(self, "c_" + e, 1)
        self.dsems = {}
        self.rr = {}
        self.stopped = False

    def sem(self, name, step=16):
        return Sem(self, name, step)

    def sb(self, name, shape, dt):
        return self.ctx.enter_context(self.nc.sbuf_tensor(name, list(shape), dt))

    def ps(self, name, shape, dt=F32):
        return self.ctx.enter_context(self.nc.psum_tensor(name, list(shape), dt))

    def _collect(self, eng_sem, reads, writes, waits, is_pe):
        need = {}

        def add(tok, raw):
            if tok is None:
                return
            s, v = tok
            if s is eng_sem and is_pe:
                return
            if need.get(s, 0) < v:
                need[s] = v

        for b in reads:
            add(b.w, True)
        for b in writes:
            add(b.w, False)
            for s, v in b.r.items():
                add((s, v), False)
        for t in waits:
            add(t, True)
        return need

    def _update(self, tok, reads, writes):
        s, v = tok
        for b in reads:
            if b.r.get(s, 0) < v:
                b.r[s] = v
        for b in writes:
            b.w = tok
            b.r = {}

    def op(self, eng, fn, reads=(), writes=(), waits=()):
        if self.stopped:
            return None
        ex = [b for b in reads if b.excl]
        if ex:
            writes = list(writes) + [b for b in ex if b not in writes]
        s = self.esem[eng]
        need = self._collect(s, reads, writes, waits, eng == "tensor")
        s.count += 1
        tok = (s, s.count)
        self._update(tok, reads, writes)
        self.ops[eng].append((fn, need, s))
        return tok

    def dma(self, eng, fn, reads=(), writes=(), waits=(), sem=None):
        if self.stopped:
            return None
        if sem is None:
            lst = self.dsems.setdefault(eng, [])
            if len(lst) < 8:
                lst.append(Sem(self, "d_%s%d" % (eng, len(lst)), 16))
            i = self.rr.get(eng, 0)
            self.rr[eng] = i + 1
            sem = lst[i % len(lst)]
        need = self._collect(None, reads, writes, waits, False)
        if sem.count > 0:
            if need.get(sem, 0) < sem.count:
                need[sem] = sem.count
        sem.count += 16
        tok = (sem, sem.count)
        self._update(tok, reads, writes)
        self.ops[eng].append((fn, need, sem))
        return tok

    def barrier(self):
        return [(s, s.count) for s in self.esem.values() if s.count > 0]

    def wait_only(self, eng, toks):
        need = {}
        for s, v in [t for t in toks if t is not None]:
            if need.get(s, 0) < v:
                need[s] = v
        self.ops[eng].append((None, need, None))

    def emit(self, block):
        P = self

        def make(ename):
            def body(eng):
                waited = {}
                for fn, need, s in P.ops[ename]:
                    for sm, val in need.items():
                        if waited.get(sm.name, 0) >= val:
                            continue
                        eng.wait_ge(sm.h, val)
                        waited[sm.name] = val
                    if fn is None:
                        continue
                    ins = fn(eng)
                    if s is not None:
                        ins.then_inc(s.h, s.step)
            return body

        for e in ENGS:
            if P.ops[e]:
                getattr(block, e)(make(e))


class _Stop(Exception):
    pass


def build_program(stop=None):
    nc = bass.Bass("TRN2", target_bir_lowering=False)

    def din(name, shape):
        return nc.dram_tensor(name, list(shape), F32, kind="ExternalInput").ap()

    def dout(name, shape):
        return nc.dram_tensor(name, list(shape), F32, kind="ExternalOutput").ap()

    xT = din("xT", [D, NT])
    cT = din("cT", [128, 24])
    w_ada = din("w_ada", [D, 9 * D])
    b_ada3 = din("b_ada3", [128, 216])
    g3 = din("g3", [128, 96])
    w1i = din("w1i", [D, 2 * DFF])
    w1o = din("w1o", [DFF, D])
    w_in = din("w_in", [D, IN_COLS])
    w_out = din("w_out", [D, D])
    w2i = din("w2i", [D, 2 * DFF])
    w2o = din("w2o", [DFF, D])
    b_f = din("b_f", [16, 1])
    conv_wT = din("conv_wT", [128, 24])
    kcT = din("kcT", [2, 16, 64, PAST])
    vc = din("vc", [2, PAST, D])
    lfc = din("lfc", [2, 16, PAST])
    scT = din("scT", [128, 32])
    yT = dout("yT", [D, NT])
    kTo = dout("kT", [D, NT])
    vo = dout("vo", [NT, D])
    lfo = dout("lf", [16, NT])
    cvo = dout("cvT", [128, 48])
    xs = nc.dram_tensor("xs", [D, NT], F32, kind="Internal").ap()

    with ExitStack() as ctx:
        P = Prog(nc, ctx)
        CH = 2184
        RX = P.sb("RX", [128, 8 * CH], F32)
        xv = RX[:, :].rearrange("p (k t) -> p k t", k=8)[:, :, 0:NT]
        H = P.sb("H", [128, 8, NT], BF16)
        RG = P.sb("RG", [128, 11 * NT], BF16)
        gv = RG[:, :].rearrange("p (j t) -> p j t", j=11)
        WSALL = P.sb("wsall", [128, NSLOT * 1024], BF16)
        WS = [WSALL[:, i * 1024:(i + 1) * 1024] for i in range(NSLOT)]
        MOD = P.sb("MOD", [128, 216], F32)
        DER = P.sb("DER", [128, 120], F32)
        BADA = P.sb("BADA", [128, 216], F32)
        G3 = P.sb("G3", [128, 96], F32)
        ones_bf = P.sb("ones_bf", [128, 128], BF16)
        ident_f = P.sb("ident_f", [128, 128], F32)
        ident_bf = P.sb("ident_bf", [128, 128], BF16)
        maskneg = P.sb("maskneg", [128, 128], BF16)
        zer_bf = P.sb("zer_bf", [128, 128], BF16)
        mask2 = P.sb("mask2", [128, 128], BF16)
        SCR = P.sb("scr", [128, 2048], BF16)
        SQB = [SCR[:, i * 512:(i + 1) * 512] for i in range(2)]
        SAB = [SCR[:, (2 + i) * 512:(3 + i) * 512] for i in range(2)]
        VST = [SCR[:, s_ * 1024:(s_ + 1) * 1024].rearrange("p (i f) -> p i f", i=8) for s_ in range(2)]
        TMP = [P.sb("tmp%d" % i, [128, 512], F32) for i in range(2)]
        LNMS = P.sb("lnms", [128, 512], F32)
        STG = [P.sb("stg%d" % i, [128, 512], F32) for i in range(2)]
        CF = P.sb("CF", [128, 24], F32)
        CSB = P.sb("CSB", [128, 24], BF16)
        CW = P.sb("CW", [128, 24], F32)
        SCT = P.sb("SCT", [128, 32], F32)
        CVO = P.sb("CVO", [128, 48], F32)
        BFt = P.sb("BFt", [16, 1], F32)
        NF = P.sb("NF", [128, 256], F32)
        NFS = P.sb("NFS", [128, 2 * 8 * 16 + 16], F32)
        FNEW = P.sb("FNEW", [16, 128], F32)
        PB = [P.ps("pb%d" % i, [128, 512]) for i in range(8)]
        PBB = [Buf("pb%d" % i, excl=True) for i in range(8)]


        def rx_f32(k, off, n):
            return RX[:, k * CH + off:k * CH + off + n]

        def rx_bf(k, off, n):
            return RX[:, k * CH:(k + 1) * CH].bitcast(BF16)[:, off:off + n]

        q_aug = rx_bf(0, 0, 2 * NT).rearrange("p (h t) -> p h t", h=2)
        k_aug = rx_bf(1, 0, 2 * NT).rearrange("p (h t) -> p h t", h=2)
        LF = rx_f32(2, 0, NT)
        FP = rx_f32(3, 0, NPR)
        ONES16 = rx_f32(4, 0, NPR)
        v_ext = rx_bf(2, 0, 17 * 256).rearrange("p (i h d) -> p i h d", i=17, h=2)
        kc_aug = rx_bf(3, 0, 4096).rearrange("p (s h t) -> p s h t", s=2, h=2)
        vc_ext = rx_bf(4, 0, 4096).rearrange("p (s i h d) -> p s i h d", s=2, i=8, h=2)
        CS = rx_f32(5, 0, 2 * 1088).rearrange("p (s t) -> p s t", s=2)
        FS = rx_f32(6, 0, 2 * 1088).rearrange("p (s t) -> p s t", s=2)
        CT16 = rx_bf(7, 0, NT)
        T1 = rx_f32(7, 1088, 512)
        T2 = rx_f32(7, 1088 + 512, 512)
        mv = RG[:, 0:8 * NT].rearrange("p (k t) -> p k t", k=8)
        SGA = RG[:, 8 * NT:10 * NT].bitcast(F32)
        PT = [RG[:, 10 * NT + i * 512:10 * NT + (i + 1) * 512] for i in range(4)]
        LND = LNMS
        RD = TMP[0]
        TT = TMP[1]

        modv = MOD[:, :].rearrange("p (j s) -> p j s", s=3)
        derv = DER[:, :].rearrange("p (a k s) -> p a k s", a=5, s=3)
        g3v = G3[:, :].rearrange("p (a k s) -> p a k s", a=4, s=3)

        def mod_ap(kind, k, seq):
            return modv[:, kind * 8 + k, seq:seq + 1]

        def der_ap(a, k, seq):
            return derv[:, a, k, seq:seq + 1]

        XB = [[Buf("x%d_%d" % (k, t)) for t in range(5)] for k in range(8)]
        HB = [[Buf("h%d_%d" % (k, t)) for t in range(5)] for k in range(8)]
        GB = [[Buf("g%d_%d" % (j, t)) for t in range(5)] for j in range(11)]
        MB = [[Buf("m%d_%d" % (k, t)) for t in range(5)] for k in range(8)]
        WSB = [Buf("ws%d" % i) for i in range(NSLOT)]
        WSS = [P.sem("wsd%d" % i, 16) for i in range(NSLOT)]
        B_const = Buf("const")
        B_mod = [Buf("mod%d" % i) for i in range(9)]
        B_der = [Buf("der%d" % i) for i in range(5)]
        B_sq = [Buf() for _ in range(2)]
        B_sa = [Buf() for _ in range(2)]
        B_tmp = [Buf() for _ in range(2)]
        B_ln = Buf()
        B_stg = [Buf() for _ in range(2)]
        B_small = Buf("small")
        B_csb = Buf()
        B_cvo = Buf()

        slot_next = [0]
        slot_active = [list(range(NSLOT))]

        def wload(src_ap, view):
            act_ = slot_active[0]
            i = act_[slot_next[0] % len(act_)]
            slot_next[0] += 1
            t = WS[i]
            if view == "k128":
                dst = t.rearrange("p (k n) -> p k n", k=8)
            elif view == "k16":
                dst = t[:, 0:128].rearrange("p (k n) -> p k n", k=8)
            else:
                dst = t
            P.dma("gpsimd", lambda e, d=dst, s=src_ap: e.dma_start(out=d, in_=s),
                  writes=[WSB[i]], sem=WSS[i])
            return dst, WSB[i]

        def wcols(w, c0, n):
            return w.rearrange("(k p) n -> p k n", p=128)[:, :, c0:c0 + n]

        def mm(items, reads, writes):
            def fn(e, items=items):
                ins = None
                for (o, l, r, st, sp) in items:
                    ins = e.matmul(o, lhsT=l, rhs=r, start=st, stop=sp)
                return ins
            return P.op("tensor", fn, reads=reads, writes=writes)

        def act(out, in_, func, reads, writes, bias=None, scale=None):
            kw = {}
            if bias is not None:
                kw["bias"] = bias
            if scale is not None:
                kw["scale"] = scale
            return P.op("scalar", lambda e: e.activation(out=out, in_=in_, func=func, **kw),
                        reads=reads, writes=writes)

        out_toks = []
        sync_rr = [0]

        def stop_at(name):
            if stop == name:
                P.stopped = True

        P.op("gpsimd", lambda e: e.memset(ones_bf[:], 1.0), writes=[B_const])
        P.op("gpsimd", lambda e: e.memset(zer_bf[:], 0.0), writes=[B_const])
        P.op("gpsimd", lambda e: e.memset(ident_f[:], 0.0), writes=[B_const])
        P.op("gpsimd", lambda e: e.affine_select(out=ident_f[:], in_=ident_f[:], pattern=[[-1, 128]],
                                                  compare_op=ALU.not_equal, fill=1.0, base=0,
                                                  channel_multiplier=1), reads=[B_const], writes=[B_const])
        P.op("gpsimd", lambda e: e.tensor_copy(out=ident_bf[:], in_=ident_f[:]), reads=[B_const], writes=[B_const])
        P.op("gpsimd", lambda e: e.affine_select(out=maskneg[:], in_=zer_bf[:], pattern=[[1, 128]],
                                                  compare_op=ALU.is_ge, fill=MASKVAL, base=0,
                                                  channel_multiplier=-1), reads=[B_const], writes=[B_const])
        P.op("gpsimd", lambda e: e.tensor_copy(out=mask2[:, 0:64], in_=maskneg[:, 0:64]), reads=[B_const], writes=[B_const])
        P.op("gpsimd", lambda e: e.affine_select(out=mask2[:, 64:128], in_=zer_bf[:, 0:64], pattern=[[1, 64]],
                                                  compare_op=ALU.is_ge, fill=MASKVAL, base=64,
                                                  channel_multiplier=-1), reads=[B_const], writes=[B_const])
        P.op("gpsimd", lambda e: e.affine_select(out=mask2[:, 64:128], in_=mask2[:, 64:128], pattern=[[0, 64]],
                                                  compare_op=ALU.is_ge, fill=MASKVAL, base=-64,
                                                  channel_multiplier=1), reads=[B_const], writes=[B_const])
        P.dma("sync", lambda e: e.dma_start(out=CF[:], in_=cT), writes=[B_small])
        P.dma("sync", lambda e: e.dma_start(out=BADA[:], in_=b_ada3), writes=[B_small])
        P.dma("sync", lambda e: e.dma_start(out=G3[:], in_=g3), writes=[B_small])
        P.dma("sync", lambda e: e.dma_start(out=CW[:], in_=conv_wT), writes=[B_small])
        P.dma("sync", lambda e: e.dma_start(out=SCT[:], in_=scT), writes=[B_small])
        P.dma("sync", lambda e: e.dma_start(out=BFt[:], in_=b_f), writes=[B_small])
        xT3 = xT.rearrange("(k p) t -> p k t", p=128)
        for tb, (t0, n) in enumerate(TBS):
            P.dma("sync", lambda e, t0=t0, n=n: e.dma_start(out=xv[:, :, t0:t0 + n], in_=xT3[:, :, t0:t0 + n]),
                  writes=[XB[k][tb] for k in range(8)])
        act(CSB[:], CF[:], AF.Silu, [B_small], [B_csb])

        stop_at("setup")
        mod_state = {"kind": 0}

        def ada_kind(kind):
            bank = 6 + (kind % 2)
            for jj in range(8):
                j = kind * 8 + jj
                sl, sb_ = wload(wcols(w_ada, j * 128, 128), "k128")
                items = [(PB[bank][:, jj * 3:jj * 3 + 3], sl[:, k, :], CSB[:, k * 3:k * 3 + 3], k == 0, k == 7)
                         for k in range(8)]
                mm(items, [sb_, B_csb], [PBB[bank]])
            P.op("vector", lambda e, kind=kind, bank=bank: e.tensor_tensor(
                out=MOD[:, kind * 24:(kind + 1) * 24], in0=PB[bank][:, 0:24],
                in1=BADA[:, kind * 24:(kind + 1) * 24], op=ALU.add),
                reads=[PBB[bank], B_small], writes=[B_mod[kind]])
            if kind in (1, 4, 7):
                a = {1: 0, 4: 1, 7: 2}[kind]
                P.op("vector", lambda e, kind=kind, a=a: e.scalar_tensor_tensor(
                    out=DER[:, a * 24:(a + 1) * 24], in0=MOD[:, kind * 24:(kind + 1) * 24], scalar=1.0,
                    in1=G3[:, a * 24:(a + 1) * 24], op0=ALU.add, op1=ALU.mult),
                    reads=[B_mod[kind], B_small], writes=[B_der[a]])
            if kind in (2, 8):
                a = {2: 3, 8: 4}[kind]
                P.op("vector", lambda e, kind=kind, a=a: e.tensor_scalar(
                    out=DER[:, a * 24:(a + 1) * 24], in0=MOD[:, kind * 24:(kind + 1) * 24],
                    scalar1=0.5, scalar2=None, op0=ALU.mult),
                    reads=[B_mod[kind]], writes=[B_der[a]])

        ada_kind(0)
        ada_kind(1)

        cnt = {"sq": 0, "tmp": 0, "stat": 0, "stg": 0, "sa": 0, "stgm": 0, "fin": 0}

        def norm_tb(tb, a_idx, sh_kind, final=False):
            t0, n = TBS[tb]
            bank = 6 + (cnt["stat"] % 2)
            cnt["stat"] += 1
            for k in range(8):
                i = cnt["sq"] % 2
                cnt["sq"] += 1
                act(SQB[i][:, 0:n], xv[:, k, t0:t0 + n], AF.Square, [XB[k][tb]], [B_sq[i]])
                mm([(PB[bank][:, 0:n], ones_bf[:], SQB[i][:, 0:n], k == 0, k == 7)],
                   [B_sq[i], B_const], [PBB[bank]])
            act(LNMS[:, 0:n], PB[bank][:, 0:n], AF.Ln, [PBB[bank]], [B_ln], bias=EPS, scale=1.0 / D)
            act(PB[bank][:, 0:n], LNMS[:, 0:n], AF.Exp, [B_ln], [PBB[bank]], scale=-0.5)
            for k in range(8):
                if final:
                    i = cnt["fin"] % 4
                    cnt["fin"] += 1
                    stg_ = (STG + TMP)[i]
                    bst_ = (B_stg + B_tmp)[i]
                    P.op("vector", lambda e, k=k, stg_=stg_: e.scalar_tensor_tensor(
                        out=stg_[:, 0:n], in0=xv[:, k, t0:t0 + n], scalar=g3v[:, 3, k, 0:1],
                        in1=PB[bank][:, 0:n], op0=ALU.mult, op1=ALU.mult),
                        reads=[XB[k][tb], PBB[bank], B_small], writes=[bst_])
                    out_toks.append(P.dma("sync", lambda e, k=k, stg_=stg_: e.dma_start(
                        out=yT[k * 128:(k + 1) * 128, t0:t0 + n], in_=stg_[:, 0:n]), reads=[bst_]))
                    continue
                i = cnt["tmp"] % 2
                cnt["tmp"] += 1
                for (lo, ln_, seq) in segs(tb):
                    P.op("vector", lambda e, k=k, i=i, lo=lo, ln_=ln_, seq=seq: e.scalar_tensor_tensor(
                        out=TMP[i][:, lo:lo + ln_], in0=xv[:, k, t0 + lo:t0 + lo + ln_],
                        scalar=der_ap(a_idx, k, seq), in1=PB[bank][:, lo:lo + ln_],
                        op0=ALU.mult, op1=ALU.mult),
                        reads=[XB[k][tb], PBB[bank], B_der[a_idx]], writes=[B_tmp[i]])
                for (lo, ln_, seq) in segs(tb):
                    act(H[:, k, t0 + lo:t0 + lo + ln_], TMP[i][:, lo:lo + ln_], AF.Identity,
                        [B_tmp[i], B_mod[sh_kind]], [HB[k][tb]], bias=mod_ap(sh_kind, k, seq), scale=1.0)

        stop_at("ada01")
        for tb in range(5):
            norm_tb(tb, 0, 0)
        stop_at("norm1")

        def ffn(w_i, w_o, gh_idx, gate_buf, between_j=None, after_tb=None):
            for half in range(2):
                for jj in range(11):
                    j = half * 11 + jj
                    sA, bA = wload(wcols(w_i, j * 128, 128), "k128")
                    sBv, bB = wload(wcols(w_i, DFF + j * 128, 128), "k128")
                    for tb, (t0, n) in enumerate(TBS):
                        ia = cnt["sa"] % 2
                        cnt["sa"] += 1
                        ba, bb = ia, 2 + ia
                        hb = [HB[k][tb] for k in range(8)]
                        mm([(PB[ba][:, 0:n], sA[:, k, :], H[:, k, t0:t0 + n], k == 0, k == 7) for k in range(8)],
                           [bA] + hb, [PBB[ba]])
                        mm([(PB[bb][:, 0:n], sBv[:, k, :], H[:, k, t0:t0 + n], k == 0, k == 7) for k in range(8)],
                           [bB] + hb, [PBB[bb]])
                        act(SAB[ia][:, 0:n], PB[ba][:, 0:n], AF.Silu, [PBB[ba]], [B_sa[ia]])
                        P.op("vector", lambda e, ia=ia, bb=bb, jj=jj, t0=t0, n=n: e.tensor_tensor(
                            out=gv[:, jj, t0:t0 + n], in0=PB[bb][:, 0:n], in1=SAB[ia][:, 0:n], op=ALU.mult),
                            reads=[PBB[bb], B_sa[ia]], writes=[GB[jj][tb]])
                    if between_j is not None:
                        between_j(half, jj)
                so = []
                for jj in range(11):
                    j = half * 11 + jj
                    so.append(wload(w_o[j * 128:(j + 1) * 128, :], "flat"))
                for tb, (t0, n) in enumerate(TBS):
                    for mo in range(8):
                        bank = 4 + (cnt["stg"] % 2)
                        cnt["stg"] += 1
                        mm([(PB[bank][:, 0:n], so[jj][0][:, mo * 128:(mo + 1) * 128], gv[:, jj, t0:t0 + n],
                             jj == 0, jj == 10) for jj in range(11)],
                           [so[jj][1] for jj in range(11)] + [GB[jj][tb] for jj in range(11)], [PBB[bank]])
                        for (lo, ln_, seq) in segs(tb):
                            P.op("vector", lambda e, bank=bank, mo=mo, t0=t0, lo=lo, ln_=ln_, seq=seq: e.scalar_tensor_tensor(
                                out=xv[:, mo, t0 + lo:t0 + lo + ln_], in0=PB[bank][:, lo:lo + ln_],
                                scalar=der_ap(gh_idx, mo, seq), in1=xv[:, mo, t0 + lo:t0 + lo + ln_],
                                op0=ALU.mult, op1=ALU.add),
                                reads=[PBB[bank], gate_buf, XB[mo][tb]], writes=[XB[mo][tb]])
                        if half == 1 and after_tb is not None and tb > 0 and mo == 3:
                            after_tb(tb - 1)
                if half == 1 and after_tb is not None:
                    after_tb(4)

        def ada_between(half, jj):
            if half == 0 and jj < 7:
                ada_kind(2 + jj)

        xs3 = xs.rearrange("(k p) t -> p k t", p=128)
        spill = []

        def after_ffn1(tb):
            norm_tb(tb, 1, 3)
            t0, n = TBS[tb]
            spill.append(P.dma("sync", lambda e: e.dma_start(out=xs3[:, :, t0:t0 + n], in_=xv[:, :, t0:t0 + n]),
                               reads=[XB[k][tb] for k in range(8)]))

        ffn(w1i, w1o, 3, B_der[3], between_j=ada_between, after_tb=after_ffn1)

        stop_at("ffn1")
        class _AllSpill:
            pass
        def seeded(k, name=""):
            b = Buf(name)
            for tk in spill:
                if tk is not None:
                    b.r[tk[0]] = max(b.r.get(tk[0], 0), tk[1])
            return b

        B_q = [[seeded(0) for _ in range(5)] for _ in range(2)]
        B_qrow = [seeded(0) for _ in range(2)]
        B_k = [[seeded(1) for _ in range(5)] for _ in range(2)]
        B_krow = seeded(1)
        B_lf = seeded(2, "lf")
        B_fp = seeded(3, "fp")
        B_o16 = seeded(4, "o16")
        B_v = [seeded(2) for _ in range(5)]
        B_vones = seeded(2)
        B_kc = [[seeded(3) for _ in range(2)] for _ in range(2)]
        B_kcrow = seeded(3)
        B_vc = [[seeded(4) for _ in range(2)] for _ in range(2)]
        B_vcones = seeded(4)
        B_cs = seeded(5)
        B_fs = seeded(6)
        B_ct = seeded(7)
        B_t1 = seeded(7)
        B_t2 = seeded(7)
        B_sga = [Buf() for _ in range(5)]
        B_pt = [Buf() for _ in range(4)]
        B_nf = Buf()
        B_nfs = Buf()
        B_fnew = Buf()

        sF, bF = wload(wcols(w_in, OFF_F, 16), "k16")
        for tb, (t0, n) in enumerate(TBS):
            bank = tb % 2
            mm([(PB[bank][0:16, 0:n], sF[:, k, :], H[:, k, t0:t0 + n], k == 0, k == 7) for k in range(8)],
               [bF] + [HB[k][tb] for k in range(8)], [PBB[bank]])
            P.op("vector", lambda e, bank=bank, n=n: e.tensor_scalar(
                out=T1[0:16, 0:n], in0=PB[bank][0:16, 0:n], scalar1=BFt[0:16, 0:1], scalar2=None, op0=ALU.add),
                reads=[PBB[bank], B_small], writes=[B_t1])
            act(T2[0:16, 0:n], T1[0:16, 0:n], AF.Abs, [B_t1], [B_t2])
            act(T2[0:16, 0:n], T2[0:16, 0:n], AF.Exp, [B_t2], [B_t2], scale=-1.0)
            act(T2[0:16, 0:n], T2[0:16, 0:n], AF.Ln, [B_t2], [B_t2], bias=1.0, scale=1.0)
            P.op("vector", lambda e, t0=t0, n=n: e.scalar_tensor_tensor(
                out=LF[0:16, t0:t0 + n], in0=T1[0:16, 0:n], scalar=0.0, in1=T2[0:16, 0:n],
                op0=ALU.min, op1=ALU.subtract),
                reads=[B_t1, B_t2], writes=[B_lf])
        out_toks.append(P.dma("sync", lambda e: e.dma_start(out=lfo, in_=LF[0:16, :]), reads=[B_lf]))
        P.op("vector", lambda e: e.memset(ONES16[0:16, :], 1.0), writes=[B_o16])
        P.op("vector", lambda e: e.tensor_tensor_scan(
            out=FP[0:16, :], data0=ONES16[0:16, :], data1=LF[0:16, 0:NPR], initial=0.0,
            op0=ALU.mult, op1=ALU.add), reads=[B_o16, B_lf], writes=[B_fp])
        for s in range(2):
            P.dma("sync", lambda e, s=s: e.dma_start(out=CS[0:16, s, 0:PAST], in_=lfc[s]), writes=[B_cs])
        for s in range(2):
            P.op("vector", lambda e, s=s: e.tensor_copy(out=CS[0:16, s, PAST:PAST + SQ],
                                                        in_=LF[0:16, NPR + SQ * s:NPR + SQ * (s + 1)]),
                 reads=[B_lf], writes=[B_cs])
        for s in range(2):
            P.op("vector", lambda e, s=s: e.tensor_tensor_scan(
                out=FS[0:16, s, :], data0=ONES16[0:16, 0:PAST + SQ], data1=CS[0:16, s, :], initial=0.0,
                op0=ALU.mult, op1=ALU.add), reads=[B_o16, B_cs], writes=[B_fs])
        P.op("vector", lambda e: e.tensor_copy(out=CT16[0:16, 0:NPR], in_=FP[0:16, :]), reads=[B_fp], writes=[B_ct])
        for s in range(2):
            P.op("vector", lambda e, s=s: e.tensor_copy(out=CT16[0:16, NPR + SQ * s:NPR + SQ * (s + 1)],
                                                        in_=FS[0:16, s, PAST:PAST + SQ]),
                 reads=[B_fs], writes=[B_ct])
            P.op("vector", lambda e, s=s: e.tensor_copy(out=FNEW[0:16, SQ * s:SQ * (s + 1)],
                                                        in_=FS[0:16, s, PAST:PAST + SQ]),
                 reads=[B_fs], writes=[B_fnew])
        stop_at("fgate")
        def alias_seed(bufs, srcs):
            toks = {}
            for sbuf in srcs:
                if sbuf.w is not None:
                    s_, v_ = sbuf.w
                    toks[s_] = max(toks.get(s_, 0), v_)
                for s_, v_ in sbuf.r.items():
                    toks[s_] = max(toks.get(s_, 0), v_)
            for b_ in bufs:
                for s_, v_ in toks.items():
                    b_.r[s_] = max(b_.r.get(s_, 0), v_)

        U = rx_f32(5, 0, 2182)
        CY = rx_f32(6, 0, 2180)
        k_augs = [k_aug, WSALL[:, 6 * 1024:6 * 1024 + 2 * NT].rearrange("p (h t) -> p h t", h=2)]
        v_exts = [v_ext, WSALL[:, 11 * 1024:11 * 1024 + 17 * 256].rearrange("p (i h d) -> p i h d", i=17, h=2)]
        B_u = Buf("u")
        B_cy = Buf("cy")
        alias_seed([B_u], [B_cs])
        alias_seed([B_cy], [B_fs])
        B_ks = [B_k, [[Buf() for _ in range(5)] for _ in range(2)]]
        B_krows = [B_krow, Buf()]
        B_vs = [B_v, [Buf() for _ in range(5)]]
        B_vones_s = [B_vones, Buf()]
        alias_seed([x_ for l_ in B_ks[1] for x_ in l_] + [B_krows[1]], WSB[6:11])
        alias_seed(B_vs[1] + [B_vones_s[1]], WSB[11:16])
        alias_seed(B_vs[0] + [B_vones_s[0]], [B_lf])
        alias_seed([x_ for l_ in B_kc for x_ in l_] + [B_kcrow], [B_fp])
        alias_seed([x_ for l_ in B_vc for x_ in l_] + [B_vcones], [B_o16])
        slot_active[0] = list(range(5))
        slot_next[0] = 0
        STGM = [STG[0], STG[1], WSALL[:, 5 * 1024:6 * 1024].bitcast(F32)]
        B_stgm = [B_stg[0], B_stg[1], Buf()]
        alias_seed([B_stgm[2]], [WSB[5]])

        ONES2 = struct.unpack("<f", struct.pack("<I", 0x3F803F80))[0]

        def const_memsets():
            for st in range(2):
                P.op("vector", lambda e, st=st: e.memset(k_augs[st][64:65, :, :].bitcast(F32), ONES2),
                     writes=[B_krows[st]])
                P.op("vector", lambda e, st=st: e.memset(v_exts[st][:, :, :, 64:128].bitcast(F32), ONES2),
                     writes=[B_vones_s[st]])
            P.op("vector", lambda e: e.memset(kc_aug[64:65, :, :, :].bitcast(F32), ONES2), writes=[B_kcrow])
            P.op("vector", lambda e: e.memset(vc_ext[:, :, :, :, 64:128].bitcast(F32), ONES2), writes=[B_vcones])
            P.op("vector", lambda e: e.memset(U[:, 0:2], 0.0), writes=[B_u])

        cvv = CVO[:, :].rearrange("p (k s r) -> p k s r", k=8, s=3)
        sctv = SCT[:, :].rearrange("p (k s r) -> p k s r", k=8, s=2)
        UOFF = [2, 2052, 2118]
        CYOFF = [0, 2050, 2116]
        SEQ0 = [0, NPR, NPR + SQ]
        sbank = [0]
        obank = [0]
        pbank = [0]
        ptc = [0]

        def next_s():
            b = sbank[0] % 4
            sbank[0] += 1
            return b

        def next_o():
            b = 4 + (obank[0] % 2)
            obank[0] += 1
            return b

        def next_p():
            b = 6 + (pbank[0] % 2)
            pbank[0] += 1
            return b

        def proj(sw, bw, tb):
            t0, n = TBS[tb]
            b = next_p()
            mm([(PB[b][:, 0:n], sw[:, k, :], H[:, k, t0:t0 + n], k == 0, k == 7) for k in range(8)],
               [bw] + [HB[k][tb] for k in range(8)], [PBB[b]])
            return b

        def proj_g(sw, bw, tb):
            t0, n = TBS[tb]
            b = next_p()
            rd = [bw] + [HB[k][tb] for k in range(8)]
            mm([(PB[b][:, 0:n], sw[:, k, :], H[:, k, t0:t0 + n], k == 0, False) for k in range(4)], rd, [PBB[b]])
            yield
            mm([(PB[b][:, 0:n], sw[:, k, :], H[:, k, t0:t0 + n], False, k == 7) for k in range(4, 8)], rd, [PBB[b]])
            return b

        def sigmoid_inplace(dst, src_ps, rd, wr):
            act(dst, src_ps, AF.Exp, rd, wr, scale=-1.0)
            act(dst, dst, AF.Ln, wr, wr, bias=1.0, scale=1.0)
            act(dst, dst, AF.Exp, wr, wr, scale=-1.0)

        def sigmoid_split(dst, src_ps, rd, wr):
            return [lambda: act(dst, src_ps, AF.Exp, rd, wr, scale=-1.0),
                    lambda: act(dst, dst, AF.Ln, wr, wr, bias=1.0, scale=1.0),
                    lambda: act(dst, dst, AF.Exp, wr, wr, scale=-1.0)]

        mdone = set()

        def conv_items(c):
            items = []
            w = {}

            def ld1():
                w["C"] = wload(wcols(w_in, OFF_C + c * 128, 128), "k128")
                w["X"] = wload(wcols(w_in, OFF_X + c * 128, 128), "k128")
            items.append(ld1)

            def pads():
                for s in range(2):
                    P.op("vector", lambda e, s=s: e.tensor_copy(out=U[:, UOFF[1 + s] - 2:UOFF[1 + s]],
                                                                in_=sctv[:, c, s, :]),
                         reads=[B_small], writes=[B_u])
            items.append(pads)
            for tb, (t0, n) in enumerate(TBS):
                def cgrp(tb=tb, t0=t0, n=n):
                    bc = yield from proj_g(w["C"][0], w["C"][1], tb)
                    P.op("vector", lambda e: e.tensor_copy(out=T1[:, 0:n], in_=PB[bc][:, 0:n]),
                         reads=[PBB[bc]], writes=[B_t1])
                items.append(cgrp)

                def xgrp(tb=tb, t0=t0, n=n):
                    bx = yield from proj_g(w["X"][0], w["X"][1], tb)
                    for (lo, ln_, seq) in segs(tb):
                        u0 = UOFF[seq] + (t0 + lo - SEQ0[seq])
                        P.op("vector", lambda e, lo=lo, ln_=ln_, u0=u0: e.tensor_tensor(
                            out=U[:, u0:u0 + ln_], in0=PB[bx][:, lo:lo + ln_], in1=T1[:, lo:lo + ln_], op=ALU.mult),
                            reads=[PBB[bx], B_t1], writes=[B_u])
                items.append(xgrp)

            def ld2():
                w["B"] = wload(wcols(w_in, OFF_B + c * 128, 128), "k128")
                w["G"] = wload(wcols(w_in, OFF_GC + c * 128, 128), "k128")
            items.append(ld2)

            def taps():
                for s3 in range(3):
                    end = UOFF[s3] + [NPR, SQ, SQ][s3]
                    P.op("vector", lambda e, s3=s3, end=end: e.tensor_copy(out=cvv[:, c, s3, :], in_=U[:, end - 2:end]),
                         reads=[B_u], writes=[B_cvo])
                P.op("vector", lambda e: e.tensor_scalar(out=CY[:, 0:2180], in0=U[:, 0:2180],
                                                         scalar1=CW[:, c * 3:c * 3 + 1], scalar2=None, op0=ALU.mult),
                     reads=[B_u, B_small], writes=[B_cy])
                P.op("vector", lambda e: e.scalar_tensor_tensor(out=CY[:, 0:2180], in0=U[:, 1:2181],
                                                                scalar=CW[:, c * 3 + 1:c * 3 + 2], in1=CY[:, 0:2180],
                                                                op0=ALU.mult, op1=ALU.add),
                     reads=[B_u, B_cy, B_small], writes=[B_cy])
                P.op("vector", lambda e: e.scalar_tensor_tensor(out=CY[:, 0:2180], in0=U[:, 2:2182],
                                                                scalar=CW[:, c * 3 + 2:c * 3 + 3], in1=CY[:, 0:2180],
                                                                op0=ALU.mult, op1=ALU.add),
                     reads=[B_u, B_cy, B_small], writes=[B_cy])
            items.append(taps)
            for tb, (t0, n) in enumerate(TBS):
                def ggrp(tb=tb, t0=t0, n=n):
                    bg = yield from proj_g(w["G"][0], w["G"][1], tb)
                    if split_act[0]:
                        return ("front", sigmoid_split(T2[:, 0:n], PB[bg][:, 0:n], [PBB[bg]], [B_t2]))
                    sigmoid_inplace(T2[:, 0:n], PB[bg][:, 0:n], [PBB[bg]], [B_t2])
                items.append(ggrp)

                def bgrp(tb=tb, t0=t0, n=n):
                    bb = yield from proj_g(w["B"][0], w["B"][1], tb)
                    for (lo, ln_, seq) in segs(tb):
                        cy0 = CYOFF[seq] + (t0 + lo - SEQ0[seq])
                        P.op("vector", lambda e, lo=lo, ln_=ln_, cy0=cy0: e.tensor_tensor(
                            out=T1[:, lo:lo + ln_], in0=PB[bb][:, lo:lo + ln_], in1=CY[:, cy0:cy0 + ln_], op=ALU.mult),
                            reads=[PBB[bb], B_cy], writes=[B_t1])
                    for hh in range(2):
                        r0 = hh * 64
                        for si, (lo, ln_, seq) in enumerate(segs(tb)):
                            key = (c, tb, hh, si)
                            if key not in mdone:
                                P.op("vector", lambda e, r0=r0, lo=lo, ln_=ln_: e.tensor_tensor(
                                    out=mv[r0:r0 + 64, c, t0 + lo:t0 + lo + ln_], in0=T1[r0:r0 + 64, lo:lo + ln_],
                                    in1=T2[r0:r0 + 64, lo:lo + ln_], op=ALU.mult),
                                    reads=[B_t1, B_t2], writes=[MB[c][tb]])
                                mdone.add(key)
                            else:
                                P.op("vector", lambda e, r0=r0, lo=lo, ln_=ln_: e.tensor_tensor(
                                    out=T1[r0:r0 + 64, lo:lo + ln_], in0=T1[r0:r0 + 64, lo:lo + ln_],
                                    in1=T2[r0:r0 + 64, lo:lo + ln_], op=ALU.mult),
                                    reads=[B_t1, B_t2], writes=[B_t1])
                                P.op("vector", lambda e, r0=r0, lo=lo, ln_=ln_: e.tensor_tensor(
                                    out=mv[r0:r0 + 64, c, t0 + lo:t0 + lo + ln_], in0=T1[r0:r0 + 64, lo:lo + ln_],
                                    in1=mv[r0:r0 + 64, c, t0 + lo:t0 + lo + ln_], op=ALU.add),
                                    reads=[B_t1, MB[c][tb]], writes=[MB[c][tb]])
                items.append(bgrp)
            return items

        def kv_items(c, st):
            items = []
            w = {}
            ka, va = k_augs[st], v_exts[st]

            def ld():
                w["K"] = wload(wcols(w_in, OFF_K + c * 128, 128), "k128")
                w["V"] = wload(wcols(w_in, OFF_V + c * 128, 128), "k128")
            items.append(ld)
            for tb, (t0, n) in enumerate(TBS):
                def kgrp(tb=tb, t0=t0, n=n):
                    bk = yield from proj_g(w["K"][0], w["K"][1], tb)
                    for hh in range(2):
                        P.op("vector", lambda e, hh=hh: e.tensor_copy(out=ka[0:64, hh, t0:t0 + n],
                                                                     in_=PB[bk][hh * 64:(hh + 1) * 64, 0:n]),
                             reads=[PBB[bk]], writes=[B_ks[st][hh][tb]])
                    i = cnt["stgm"] % 3
                    cnt["stgm"] += 1
                    P.op("vector", lambda e, i=i: e.tensor_copy(out=STGM[i][:, 0:n], in_=PB[bk][:, 0:n]),
                         reads=[PBB[bk]], writes=[B_stgm[i]])
                    out_toks.append(P.dma("sync", lambda e, i=i: e.dma_start(
                        out=kTo[c * 128:(c + 1) * 128, t0:t0 + n], in_=STGM[i][:, 0:n]), reads=[B_stgm[i]]))
                items.append(kgrp)
            for tg in range(5):
                def vgrp(tg=tg):
                    tiles = list(range(tg * 4, min(17, tg * 4 + 4)))
                    bv = next_p()
                    for qi, ti in enumerate(tiles):
                        tb = min(ti // 4, 4)
                        if qi in (2,):
                            yield
                        mm([(PB[bv][:, qi * 128:(qi + 1) * 128], H[:, k, ti * 128:(ti + 1) * 128],
                             w["V"][0][:, k, :], k == 0, k == 7) for k in range(8)],
                           [w["V"][1]] + [HB[k][tb] for k in range(8)], [PBB[bv]])
                    nt_ = len(tiles)
                    pv4 = PB[bv][:, 0:nt_ * 128].rearrange("p (i h d) -> p i h d", i=nt_, h=2)
                    P.op("vector", lambda e: e.tensor_copy(out=va[:, tiles[0]:tiles[0] + nt_, :, 0:64], in_=pv4),
                         reads=[PBB[bv]], writes=[B_vs[st][tg]])
                    i = cnt["stgm"] % 3
                    cnt["stgm"] += 1
                    P.op("vector", lambda e, i=i: e.tensor_copy(out=STGM[i][:, 0:nt_ * 128], in_=PB[bv][:, 0:nt_ * 128]),
                         reads=[PBB[bv]], writes=[B_stgm[i]])
                    t00 = tiles[0] * 128
                    out_toks.append(P.dma("sync", lambda e, i=i: e.dma_start(
                        out=vo[t00:t00 + nt_ * 128, c * 128:(c + 1) * 128].rearrange("(i p) f -> p i f", p=128),
                        in_=STGM[i][:, 0:nt_ * 128].rearrange("p (i f) -> p i f", i=nt_)), reads=[B_stgm[i]]))
                items.append(vgrp)
            return items

        split_act = [False]
        side_q = []
        side_done = set()

        pushed = [0]
        lag_q = []
        gstep = [0]

        def run_lag(force=False):
            while lag_q and (force or lag_q[0][0] <= gstep[0]):
                _, tag, fn = lag_q.pop(0)
                fn()
                if tag is not None:
                    side_done.add(tag)
                if force:
                    break

        import inspect as _insp

        def finish_item(tag, more):
            if isinstance(more, tuple) and more[0] == "lag":
                base = max(gstep[0], lag_q[-1][0] if lag_q else 0)
                for q_, f in enumerate(more[1]):
                    lag_q.append((base + 1 + q_, tag if q_ == len(more[1]) - 1 else None, f))
            elif isinstance(more, tuple) and more[0] == "front":
                side_q[0:0] = [(None, f) for f in more[1][:-1]] + [(tag, more[1][-1])]
                pushed[0] += len(more[1])
            elif tag is not None:
                side_done.add(tag)

        def side(n=1):
            for _ in range(n):
                if not side_q:
                    return
                tag, obj = side_q.pop(0)
                res = obj() if callable(obj) else obj
                if _insp.isgenerator(res):
                    try:
                        next(res)
                        side_q.insert(0, (tag, res))
                        pushed[0] += 1
                        continue
                    except StopIteration as e_:
                        res = e_.value
                finish_item(tag, res)

        def run_full(it):
            res = it()
            if _insp.isgenerator(res):
                try:
                    while True:
                        next(res)
                except StopIteration as e_:
                    res = e_.value
            if isinstance(res, tuple):
                for f in res[1]:
                    f()

        def ensure(tag):
            while tag not in side_done:
                if lag_q:
                    run_lag(force=True)
                else:
                    assert side_q, tag
                    side(1)

        B_lnd = B_ln
        B_rd = B_tmp[0]
        B_tt = B_tmp[1]

        def normalize(ob, c, hh, tbm, col0, ncol):
            r0 = hh * 64
            act(LND[64:128, 0:ncol], PB[ob][64:128, 0:ncol], AF.Ln, [PBB[ob]], [B_lnd])
            act(RD[64:128, 0:ncol], LND[64:128, 0:ncol], AF.Exp, [B_lnd], [B_rd], scale=-1.0)
            P.op("vector", lambda e: e.tensor_tensor(out=TT[r0:r0 + 64, 0:ncol], in0=PB[ob][0:64, 0:ncol],
                                                     in1=RD[64:128, 0:ncol], op=ALU.mult),
                 reads=[PBB[ob], B_rd], writes=[B_tt])
            key = (c, tbm, hh, 0 if tbm < 4 else (col0 - NPR) // SQ)
            if key not in mdone:
                P.op("vector", lambda e: e.tensor_tensor(out=mv[r0:r0 + 64, c, col0:col0 + ncol],
                                                         in0=TT[r0:r0 + 64, 0:ncol],
                                                         in1=SGA[r0:r0 + 64, col0:col0 + ncol], op=ALU.mult),
                     reads=[B_tt, B_sga[tbm]], writes=[MB[c][tbm]])
                mdone.add(key)
            else:
                P.op("vector", lambda e: e.tensor_tensor(out=TT[r0:r0 + 64, 0:ncol], in0=TT[r0:r0 + 64, 0:ncol],
                                                         in1=SGA[r0:r0 + 64, col0:col0 + ncol], op=ALU.mult),
                     reads=[B_tt, B_sga[tbm]], writes=[B_tt])
                P.op("vector", lambda e: e.tensor_tensor(out=mv[r0:r0 + 64, c, col0:col0 + ncol],
                                                         in0=TT[r0:r0 + 64, 0:ncol],
                                                         in1=mv[r0:r0 + 64, c, col0:col0 + ncol], op=ALU.add),
                     reads=[B_tt, MB[c][tbm]], writes=[MB[c][tbm]])

        qga_w = {}
        pend_norm = []

        def qga_load(c):
            qga_w[c] = (wload(wcols(w_in, OFF_Q + c * 128, 128), "k128"),
                        wload(wcols(w_in, OFF_GA + c * 128, 128), "k128"))

        def cache_loads(c):
            for s in range(2):
                for hh in range(2):
                    P.dma("gpsimd", lambda e, s=s, hh=hh: e.dma_start(
                        out=kc_aug[0:64, s, hh, :], in_=kcT[s, 2 * c + hh]), writes=[B_kc[s][hh]])
                P.dma("gpsimd", lambda e, s=s: e.dma_start(
                    out=VST[s],
                    in_=vc[s].rearrange("(i p) f -> p i f", p=128)[:, :, c * 128:(c + 1) * 128]),
                    writes=[B_vst[s]])
                P.op("vector", lambda e, s=s: e.tensor_copy(
                    out=vc_ext[:, s, :, :, 0:64], in_=VST[s].rearrange("p i (h d) -> p i h d", h=2)),
                    reads=[B_vst[s]], writes=[B_vc[s][0], B_vc[s][1]])

        def qga_items(c):
            items = []
            (sQ, bQ), (sGa, bGa) = qga_w[c]
            for tb in (0, 3, 4, 1, 2):
                t0, n = TBS[tb]

                def qgrp(tb=tb, t0=t0, n=n):
                    bq = yield from proj_g(sQ, bQ, tb)
                    for hh in range(2):
                        P.op("vector", lambda e, hh=hh: e.tensor_scalar(
                            out=q_aug[0:64, hh, t0:t0 + n], in0=PB[bq][hh * 64:(hh + 1) * 64, 0:n],
                            scalar1=0.125, scalar2=None, op0=ALU.mult),
                            reads=[PBB[bq]], writes=[B_q[hh][tb]])
                items.append((("q", c, tb), qgrp))

                def ggrp(tb=tb, t0=t0, n=n):
                    bg = yield from proj_g(sGa, bGa, tb)
                    return ("lag", sigmoid_split(SGA[:, t0:t0 + n], PB[bg][:, 0:n], [PBB[bg]], [B_sga[tb]]))
                items.append((("ga", c, tb), ggrp))
            return items

        def pair_main(c, st, nside):
            ka, va = k_augs[st], v_exts[st]
            for hh in range(2):
                P.dma("sync", lambda e, hh=hh: e.dma_start(out=q_aug[64:65, hh, :],
                                                          in_=CT16[2 * c + hh:2 * c + hh + 1, :]),
                      reads=[B_ct], writes=[B_qrow[hh]])
            step = [0]
            nsteps = 88.0
            n0 = len(side_q)
            p0 = pushed[0]

            def flush_norm():
                if pend_norm:
                    a_ = pend_norm.pop(0)
                    ensure(("ga", a_[1], a_[3]))
                    normalize(*a_)

            def maybe_side():
                step[0] += 1
                gstep[0] += 1
                run_lag()
                tot = 2.4 * nside
                want = int(tot * step[0] / 88.0)
                while side_q and (n0 + (pushed[0] - p0) - len(side_q)) < want:
                    side(1)

            def sample_block(s, hh):
                ensure(("q", c, 4))
                ensure(("ga", c, 4))
                while pend_norm:
                    flush_norm()
                head = 2 * c + hh
                qc0 = NPR + SQ * s
                sb1 = next_s()
                mm([(PB[sb1][:, i * 64:(i + 1) * 64], kc_aug[0:65, s, hh, i * 128:(i + 1) * 128],
                     q_aug[0:65, hh, qc0:qc0 + SQ], True, True) for i in range(8)],
                   [B_kc[s][hh], B_kcrow, B_q[hh][4], B_qrow[hh]], [PBB[sb1]])
                sb2 = next_s()
                mm([(PB[sb2][:, 0:64], ka[0:65, hh, NPR:NPR + 2 * SQ], q_aug[0:65, hh, qc0:qc0 + SQ], True, False),
                    (PB[sb2][:, 0:64], ident_bf[:], mask2[:, s * 64:(s + 1) * 64], False, True)],
                   [B_ks[st][hh][4], B_krows[st], B_q[hh][4], B_qrow[hh], B_const], [PBB[sb2]])
                pi = ptc[0] % 4
                ptc[0] += 1
                for i in range(8):
                    act(PT[pi][:, i * 64:(i + 1) * 64], PB[sb1][:, i * 64:(i + 1) * 64], AF.Exp,
                        [PBB[sb1], B_nfs], [B_pt[pi]],
                        bias=NFS[:, (s * 8 + i) * 16 + head:(s * 8 + i) * 16 + head + 1], scale=1.0)
                pi2 = ptc[0] % 4
                ptc[0] += 1
                act(PT[pi2][:, 0:64], PB[sb2][:, 0:64], AF.Exp, [PBB[sb2], B_nfs], [B_pt[pi2]],
                    bias=NFS[:, 256 + head:256 + head + 1], scale=1.0)
                maybe_side()
                ob = next_o()
                items = [(PB[ob][:, 0:64], vc_ext[:, s, i, hh, :], PT[pi][:, i * 64:(i + 1) * 64], i == 0, False)
                         for i in range(8)]
                items.append((PB[ob][:, 0:64], va[:, 16, hh, :], PT[pi2][:, 0:64], False, True))
                mm(items, [B_vc[s][hh], B_vcones, B_vs[st][4], B_vones_s[st], B_pt[pi], B_pt[pi2]], [PBB[ob]])
                maybe_side()
                pend_norm.append((ob, c, hh, 4, qc0, 64))


            cache_loads(c)
            for hh in range(2):
                head = 2 * c + hh
                for qb in (0, 3, 1, 2):
                    ntile = 4 * qb + 4
                    ob = next_o()
                    q0 = qb * 512
                    ensure(("q", c, qb))

                    def S(i, hh=hh, qb=qb, q0=q0):
                        sb_ = next_s()
                        r = i - 4 * qb
                        lo = max(r, 0) * 128
                        items = [(PB[sb_][:, lo:512], ka[0:65, hh, i * 128:(i + 1) * 128],
                                  q_aug[0:65, hh, q0 + lo:q0 + 512], True, r < 0)]
                        if r >= 0:
                            items.append((PB[sb_][:, lo:lo + 128], ident_bf[:], maskneg[:], False, True))
                        mm(items, [B_ks[st][hh][i // 4], B_krows[st], B_q[hh][qb], B_qrow[hh], B_const], [PBB[sb_]])
                        return sb_, lo

                    pend = [S(0), S(1), S(2)]
                    for i in range(ntile):
                        sb_, lo = pend.pop(0)
                        pi = ptc[0] % 4
                        ptc[0] += 1
                        act(PT[pi][:, lo:512], PB[sb_][:, lo:512], AF.Exp, [PBB[sb_], B_nf], [B_pt[pi]],
                            bias=(None if PROBE == "nobias" else NF[:, i * 16 + head:i * 16 + head + 1]), scale=1.0)
                        if i + 3 < ntile:
                            pend.append(S(i + 3))
                        if i == 3:
                            while pend_norm:
                                flush_norm()
                        maybe_side()
                        mm([(PB[ob][:, lo:512], va[:, i, hh, :], PT[pi][:, lo:512], i == 0, i == ntile - 1)],
                           [B_vs[st][i // 4], B_vones_s[st], B_pt[pi]], [PBB[ob]])
                    pend_norm.append((ob, c, hh, qb, q0, 512))
                    if qb == 3:
                        sample_block(0, hh)
                    elif qb == 1:
                        sample_block(1, hh)

        B_vst = [Buf(), Buf()]
        for it in kv_items(0, 0):
            run_full(it)
        const_memsets()
        def tr(items, reads, writes):
            def fn(e, items=items):
                ins = None
                for (o, i_, idn) in items:
                    ins = e.transpose(o, i_, idn)
                return ins
            return P.op("tensor", fn, reads=reads, writes=writes)

        idn16 = ident_f[0:16, 0:16]
        tr([(PB[2][:, i * 16:(i + 1) * 16], FP[0:16, i * 128:(i + 1) * 128], idn16) for i in range(16)],
           [B_fp, B_const], [PBB[2]])
        act(NF[:, :], PB[2][:, 0:256], AF.Copy, [PBB[2]], [B_nf], scale=-1.0)
        tr([(PB[3][:, (s * 8 + i) * 16:(s * 8 + i + 1) * 16], FS[0:16, s, i * 128:(i + 1) * 128], idn16)
            for s in range(2) for i in range(8)] +
           [(PB[3][:, 256:272], FNEW[0:16, :], idn16)],
           [B_fs, B_fnew, B_const], [PBB[3]])
        act(NFS[:, :], PB[3][:, 0:272], AF.Copy, [PBB[3]], [B_nfs], scale=-1.0)

        alias_seed([x_ for l_ in B_kc for x_ in l_] + [B_kcrow], [B_fp])
        alias_seed([B_cy], [B_fs])
        split_act[0] = True
        qga_load(0)
        for c in range(8):
            st = c % 2
            side_q.extend(qga_items(c))
            cv = conv_items(c)
            if c + 1 < 8:
                side_q.extend((None, f) for f in kv_items(c + 1, 1 - st))
            side_q.append((None, cv[0]))
            if c + 1 < 8:
                side_q.append((None, lambda c=c: qga_load(c + 1)))
            side_q.extend((None, f) for f in cv[1:])
            if PROBE == "serial":
                while side_q or lag_q:
                    gstep[0] += 1
                    run_lag()
                    side(1)
            pair_main(c, st, len(side_q))
            while side_q or lag_q:
                gstep[0] += 1
                run_lag()
                side(1)
        while pend_norm:
            a_ = pend_norm.pop(0)
            normalize(*a_)
        out_toks.append(P.dma("sync", lambda e: e.dma_start(out=cvo, in_=CVO[:, :]), reads=[B_cvo]))

        alias_seed([WSB[5]], [B_stgm[2]])
        alias_seed(WSB[6:11], [x_ for l_ in B_ks[1] for x_ in l_] + [B_krows[1]])
        alias_seed(WSB[11:16], B_vs[1] + [B_vones_s[1]])
        slot_active[0] = list(range(NSLOT))
        slot_next[0] = 0

        stop_at("attn")
        bar = P.barrier()
        for tb, (t0, n) in enumerate(TBS):
            P.dma("sync", lambda e, t0=t0, n=n: e.dma_start(out=xv[:, :, t0:t0 + n], in_=xs3[:, :, t0:t0 + n]),
                  writes=[XB[k][tb] for k in range(8)], waits=[t for t in bar + spill if t is not None])
        so = [wload(w_out[k * 128:(k + 1) * 128, :], "flat") for k in range(8)]
        for tb, (t0, n) in enumerate(TBS):
            for mo in range(8):
                bank = 4 + (cnt["stg"] % 2)
                cnt["stg"] += 1
                mm([(PB[bank][:, 0:n], so[k][0][:, mo * 128:(mo + 1) * 128], mv[:, k, t0:t0 + n], k == 0, k == 7)
                    for k in range(8)],
                   [so[k][1] for k in range(8)] + [MB[k][tb] for k in range(8)], [PBB[bank]])
                for (lo, ln_, seq) in segs(tb):
                    P.op("vector", lambda e, bank=bank, mo=mo, t0=t0, lo=lo, ln_=ln_, seq=seq: e.scalar_tensor_tensor(
                        out=xv[:, mo, t0 + lo:t0 + lo + ln_], in0=PB[bank][:, lo:lo + ln_],
                        scalar=mod_ap(5, mo, seq), in1=xv[:, mo, t0 + lo:t0 + lo + ln_],
                        op0=ALU.mult, op1=ALU.add),
                        reads=[PBB[bank], B_mod[5], XB[mo][tb]], writes=[XB[mo][tb]])
                if tb > 0 and mo == 3:
                    norm_tb(tb - 1, 2, 6)
        norm_tb(4, 2, 6)

        stop_at("wout")
        ffn(w2i, w2o, 4, B_der[4], after_tb=lambda tb: norm_tb(tb, None, None, final=True))

        P.stopped = False
        P.wait_only("sync", out_toks + P.barrier())
        block = ctx.enter_context(nc.Block())
        P.emit(block)
    return nc


_NC_CACHE = {}


def _prep(x_prompt, x_sample, cache_k, cache_v, cache_logf, state_conv, c_prompt, c_sample,
          w_ada, b_ada, g_ffn1, w_ffn1_in, w_ffn1_out, g_mix, w_in, b_f, conv_w, w_out,
          g_ffn2, w_ffn2_in, w_ffn2_out, g_final):
    f = lambda a: np.ascontiguousarray(np.asarray(a, dtype=np.float32))
    x_prompt, x_sample = f(x_prompt), f(x_sample)
    cache_k, cache_v, cache_logf, state_conv = f(cache_k), f(cache_v), f(cache_logf), f(state_conv)
    c_prompt, c_sample = f(c_prompt), f(c_sample)
    def pk(vec):
        return vec.reshape(8, 128).T
    b_ada3 = np.repeat(f(b_ada)[0].reshape(72, 128).T[:, :, None], 3, axis=2).reshape(128, 216)
    gains = np.stack([pk(f(g_ffn1)[0]), pk(f(g_mix)[0]), pk(f(g_ffn2)[0]), pk(f(g_final))], axis=1)
    g3 = np.repeat(gains[:, :, :, None], 3, axis=3).reshape(128, 96)
    conv_wT = np.stack([pk(f(conv_w)[0, j]) for j in range(3)], axis=2).reshape(128, 24)
    shared = {
        "w_ada": f(w_ada)[0], "b_ada3": f(b_ada3), "g3": f(g3),
        "w1i": f(w_ffn1_in)[0], "w1o": f(w_ffn1_out)[0], "w_in": f(w_in)[0], "w_out": f(w_out)[0],
        "w2i": f(w_ffn2_in)[0], "w2o": f(w_ffn2_out)[0],
        "b_f": f(b_f)[0].reshape(16, 1), "conv_wT": f(conv_wT),
    }
    in_maps = []
    for c in range(NCORES):
        s0, s1 = 2 * c, 2 * c + 1
        xt = np.concatenate([x_prompt[c], x_sample[s0], x_sample[s1]], axis=0)
        cs = np.stack([c_prompt[c], c_sample[s0], c_sample[s1]], axis=0)
        cT = cs.reshape(3, 8, 128).transpose(2, 1, 0).reshape(128, 24)
        kcT = cache_k[0, s0:s1 + 1].transpose(0, 1, 3, 2)
        vcc = cache_v[0, s0:s1 + 1].transpose(0, 2, 1, 3).reshape(2, PAST, D)
        sc = state_conv[0, s0:s1 + 1]
        scT = sc.reshape(2, 2, 8, 128).transpose(3, 2, 0, 1).reshape(128, 32)
        m = dict(shared)
        m.update({"xT": f(xt.T), "cT": f(cT), "kcT": f(kcT), "vc": f(vcc),
                  "lfc": f(cache_logf[0, s0:s1 + 1]), "scT": f(scT)})
        in_maps.append(m)
    return in_maps


def _assemble(R):
    y_prompt = np.empty((8, NPR, D), np.float32)
    y_sample = np.empty((16, SQ, D), np.float32)
    k_prompt = np.empty((1, 8, 16, NPR, 64), np.float32)
    v_prompt = np.empty((1, 8, 16, NPR, 64), np.float32)
    logf_prompt = np.empty((1, 8, 16, NPR), np.float32)
    conv_prompt = np.empty((1, 8, 2, D), np.float32)
    k_sample = np.empty((1, 16, 16, SQ, 64), np.float32)
    v_sample = np.empty((1, 16, 16, SQ, 64), np.float32)
    logf_sample = np.empty((1, 16, 16, SQ), np.float32)
    conv_sample = np.empty((1, 16, 2, D), np.float32)
    for c in range(NCORES):
        r = R[c]
        y = np.asarray(r["yT"]).T
        kk = np.asarray(r["kT"]).reshape(16, 64, NT).transpose(0, 2, 1)
        vv = np.asarray(r["vo"]).reshape(NT, 16, 64).transpose(1, 0, 2)
        lf = np.asarray(r["lf"])
        cv = np.asarray(r["cvT"]).reshape(128, 8, 3, 2).transpose(2, 3, 1, 0).reshape(3, 2, D)
        y_prompt[c] = y[:NPR]
        k_prompt[0, c] = kk[:, :NPR]
        v_prompt[0, c] = vv[:, :NPR]
        logf_prompt[0, c] = lf[:, :NPR]
        conv_prompt[0, c] = cv[0]
        for s in range(2):
            sl = slice(NPR + SQ * s, NPR + SQ * (s + 1))
            y_sample[2 * c + s] = y[sl]
            k_sample[0, 2 * c + s] = kk[:, sl]
            v_sample[0, 2 * c + s] = vv[:, sl]
            logf_sample[0, 2 * c + s] = lf[:, sl]
            conv_sample[0, 2 * c + s] = cv[1 + s]
    return (y_prompt, y_sample, k_prompt, v_prompt, logf_prompt, conv_prompt,
            k_sample, v_sample, logf_sample, conv_sample)


def kernel(**inputs):
    in_maps = _prep(**inputs)
    if "nc" not in _NC_CACHE:
        _NC_CACHE["nc"] = build_program()
    nc = _NC_CACHE["nc"]
    res = run_bass_kernel_spmd(nc, in_maps, core_ids=list(range(NCORES)))
    return _assemble(res.results)
```

```python
import struct
import numpy as np
import concourse.bass as bass
import concourse.mybir as mybir
from concourse.bass_utils import run_bass_kernel_spmd
from contextlib import ExitStack

F32 = mybir.dt.float32
BF16 = mybir.dt.bfloat16
AF = mybir.ActivationFunctionType
ALU = mybir.AluOpType

NCORES = 8
D = 1024
NT = 2176
NPR = 2048
SQ = 64
PAST = 1024
DFF = 2816
NJ = 22
OFF_Q, OFF_K, OFF_V, OFF_F = 0, 1024, 2048, 3072
OFF_B = OFF_F + 16
OFF_C = OFF_B + 1024
OFF_X = OFF_C + 1024
OFF_GA = OFF_X + 1024
OFF_GC = OFF_GA + 1024
IN_COLS = OFF_GC + 1024
EPS = 1e-6
TBS = [(0, 512), (512, 512), (1024, 512), (1536, 512), (2048, 128)]
NSLOT = 16
PROBE = ""
MASKVAL = -30000.0


def segs(tb):
    if tb < 4:
        return [(0, 512, 0)]
    return [(0, 64, 1), (64, 64, 2)]


ENGS = ["tensor", "vector", "scalar", "gpsimd", "sync"]


class Sem:
    def __init__(self, P, name, step):
        self.h = P.ctx.enter_context(P.nc.semaphore(name))
        self.count = 0
        self.step = step
        self.name = name


class Buf:
    __slots__ = ("name", "w", "r", "excl")

    def __init__(self, name="", excl=False):
        self.name = name
        self.w = None
        self.r = {}
        self.excl = excl


class Prog:
    def __init__(self, nc, ctx):
        self.nc = nc
        self.ctx = ctx
        self.ops = {e: [] for e in ENGS}
        self.esem = {}
        for e in ["tensor", "vector", "scalar", "gpsimd"]:
            self.esem[e] = Sem(self, "c_" + e, 1)
        self.dsems = {}
        self.rr = {}
        self.stopped = False

    def sem(self, name, step=16):
        return Sem(self, name, step)

    def sb(self, name, shape, dt):
        return self.ctx.enter_context(self.nc.sbuf_tensor(name, list(shape), dt))

    def ps(self, name, shape, dt=F32):
        return self.ctx.enter_context(self.nc.psum_tensor(name, list(shape), dt))

    def _collect(self, eng_sem, reads, writes, waits, is_pe):
        need = {}

        def add(tok, raw):
            if tok is None:
                return
            s, v = tok
            if s is eng_sem and is_pe:
                return
            if need.get(s, 0) < v:
                need[s] = v

        for b in reads:
            add(b.w, True)
        for b in writes:
            add(b.w, False)
            for s, v in b.r.items():
                add((s, v), False)
        for t in waits:
            add(t, True)
        return need

    def _update(self, tok, reads, writes):
        s, v = tok
        for b in reads:
            if b.r.get(s, 0) < v:
                b.r[s] = v
        for b in writes:
            b.w = tok
            b.r = {}

    def op(self, eng, fn, reads=(), writes=(), waits=()):
        if self.stopped:
            return None
        ex = [b for b in reads if b.excl]
        if ex:
            writes = list(writes) + [b for b in ex if b not in writes]
        s = self.esem[eng]
        need = self._collect(s, reads, writes, waits, eng == "tensor")
        s.count += 1
        tok = (s, s.count)
        self._update(tok, reads, writes)
        self.ops[eng].append((fn, need, s))
        return tok

    def dma(self, eng, fn, reads=(), writes=(), waits=(), sem=None):
        if self.stopped:
            return None
        if sem is None:
            lst = self.dsems.setdefault(eng, [])
            if len(lst) < 8:
                lst.append(Sem(self, "d_%s%d" % (eng, len(lst)), 16))
            i = self.rr.get(eng, 0)
            self.rr[eng] = i + 1
            sem = lst[i % len(lst)]
        need = self._collect(None, reads, writes, waits, False)
        if sem.count > 0:
            if need.get(sem, 0) < sem.count:
                need[sem] = sem.count
        sem.count += 16
        tok = (sem, sem.count)
        self._update(tok, reads, writes)
        self.ops[eng].append((fn, need, sem))
        return tok

    def barrier(self):
        return [(s, s.count) for s in self.esem.values() if s.count > 0]

    def wait_only(self, eng, toks):
        need = {}
        for s, v in [t for t in toks if t is not None]:
            if need.get(s, 0) < v:
                need[s] = v
        self.ops[eng].append((None, need, None))

    def emit(self, block):
        P = self

        def make(ename):
            def body(eng):
                waited = {}
                for fn, need, s in P.ops[ename]:
                    for sm, val in need.items():
                        if waited.get(sm.name, 0) >= val:
                            continue
                        eng.wait_ge(sm.h, val)
                        waited[sm.name] = val
                    if fn is None:
                        continue
                    ins = fn(eng)
                    if s is not None:
                        ins.then_inc(s.h, s.step)
            return body

        for e in ENGS:
            if P.ops[e]:
                getattr(block, e)(make(e))


class _Stop(Exception):
    pass


def build_program(stop=None):
    nc = bass.Bass("TRN2", target_bir_lowering=False)

    def din(name, shape):
        return nc.dram_tensor(name, list(shape), F32, kind="ExternalInput").ap()

    def dout(name, shape):
        return nc.dram_tensor(name, list(shape), F32, kind="ExternalOutput").ap()

    xT = din("xT", [D, NT])
    cT = din("cT", [128, 24])
    w_ada = din("w_ada", [D, 9 * D])
    b_ada3 = din("b_ada3", [128, 216])
    g3 = din("g3", [128, 96])
    w1i = din("w1i", [D, 2 * DFF])
    w1o = din("w1o", [DFF, D])
    w_in = din("w_in", [D, IN_COLS])
    w_out = din("w_out", [D, D])
    w2i = din("w2i", [D, 2 * DFF])
    w2o = din("w2o", [DFF, D])
    b_f = din("b_f", [16, 1])
    conv_wT = din("conv_wT", [128, 24])
    kcT = din("kcT", [2, 16, 64, PAST])
    vc = din("vc", [2, PAST, D])
    lfc = din("lfc", [2, 16, PAST])
    scT = din("scT", [128, 32])
    yT = dout("yT", [D, NT])
    kTo = dout("kT", [D, NT])
    vo = dout("vo", [NT, D])
    lfo = dout("lf", [16, NT])
    cvo = dout("cvT", [128, 48])
    xs = nc.dram_tensor("xs", [D, NT], F32, kind="Internal").ap()

    with ExitStack() as ctx:
        P = Prog(nc, ctx)
        CH = 2184
        RX = P.sb("RX", [128, 8 * CH], F32)
        xv = RX[:, :].rearrange("p (k t) -> p k t", k=8)[:, :, 0:NT]
        H = P.sb("H", [128, 8, NT], BF16)
        RG = P.sb("RG", [128, 11 * NT], BF16)
        gv = RG[:, :].rearrange("p (j t) -> p j t", j=11)
        WSALL = P.sb("wsall", [128, NSLOT * 1024], BF16)
        WS = [WSALL[:, i * 1024:(i + 1) * 1024] for i in range(NSLOT)]
        MOD = P.sb("MOD", [128, 216], F32)
        DER = P.sb("DER", [128, 120], F32)
        BADA = P.sb("BADA", [128, 216], F32)
        G3 = P.sb("G3", [128, 96], F32)
        ones_bf = P.sb("ones_bf", [128, 128], BF16)
        ident_f = P.sb("ident_f", [128, 128], F32)
        ident_bf = P.sb("ident_bf", [128, 128], BF16)
        maskneg = P.sb("maskneg", [128, 128], BF16)
        zer_bf = P.sb("zer_bf", [128, 128], BF16)
        mask2 = P.sb("mask2", [128, 128], BF16)
        SCR = P.sb("scr", [128, 2048], BF16)
        SQB = [SCR[:, i * 512:(i + 1) * 512] for i in range(2)]
        SAB = [SCR[:, (2 + i) * 512:(3 + i) * 512] for i in range(2)]
        VST = [SCR[:, s_ * 1024:(s_ + 1) * 1024].rearrange("p (i f) -> p i f", i=8) for s_ in range(2)]
        TMP = [P.sb("tmp%d" % i, [128, 512], F32) for i in range(2)]
        LNMS = P.sb("lnms", [128, 512], F32)
        STG = [P.sb("stg%d" % i, [128, 512], F32) for i in range(2)]
        CF = P.sb("CF", [128, 24], F32)
        CSB = P.sb("CSB", [128, 24], BF16)
        CW = P.sb("CW", [128, 24], F32)
        SCT = P.sb("SCT", [128, 32], F32)
        CVO = P.sb("CVO", [128, 48], F32)
        BFt = P.sb("BFt", [16, 1], F32)
        NF = P.sb("NF", [128, 256], F32)
        NFS = P.sb("NFS", [128, 2 * 8 * 16 + 16], F32)
        FNEW = P.sb("FNEW", [16, 128], F32)
        PB = [P.ps("pb%d" % i, [128, 512]) for i in range(8)]
        PBB = [Buf("pb%d" % i, excl=True) for i in range(8)]


        def rx_f32(k, off, n):
            return RX[:, k * CH + off:k * CH + off + n]

        def rx_bf(k, off, n):
            return RX[:, k * CH:(k + 1) * CH].bitcast(BF16)[:, off:off + n]

        q_aug = rx_bf(0, 0, 2 * NT).rearrange("p (h t) -> p h t", h=2)
        k_aug = rx_bf(1, 0, 2 * NT).rearrange("p (h t) -> p h t", h=2)
        LF = rx_f32(2, 0, NT)
        FP = rx_f32(3, 0, NPR)
        ONES16 = rx_f32(4, 0, NPR)
        v_ext = rx_bf(2, 0, 17 * 256).rearrange("p (i h d) -> p i h d", i=17, h=2)
        kc_aug = rx_bf(3, 0, 4096).rearrange("p (s h t) -> p s h t", s=2, h=2)
        vc_ext = rx_bf(4, 0, 4096).rearrange("p (s i h d) -> p s i h d", s=2, i=8, h=2)
        CS = rx_f32(5, 0, 2 * 1088).rearrange("p (s t) -> p s t", s=2)
        FS = rx_f32(6, 0, 2 * 1088).rearrange("p (s t) -> p s t", s=2)
        CT16 = rx_bf(7, 0, NT)
        T1 = rx_f32(7, 1088, 512)
        T2 = rx_f32(7, 1088 + 512, 512)
        mv = RG[:, 0:8 * NT].rearrange("p (k t) -> p k t", k=8)
        SGA = RG[:, 8 * NT:10 * NT].bitcast(F32)
        PT = [RG[:, 10 * NT + i * 512:10 * NT + (i + 1) * 512] for i in range(4)]
        LND = LNMS
        RD = TMP[0]
        TT = TMP[1]

        modv = MOD[:, :].rearrange("p (j s) -> p j s", s=3)
        derv = DER[:, :].rearrange("p (a k s) -> p a k s", a=5, s=3)
        g3v = G3[:, :].rearrange("p (a k s) -> p a k s", a=4, s=3)

        def mod_ap(kind, k, seq):
            return modv[:, kind * 8 + k, seq:seq + 1]

        def der_ap(a, k, seq):
            return derv[:, a, k, seq:seq + 1]

        XB = [[Buf("x%d_%d" % (k, t)) for t in range(5)] for k in range(8)]
        HB = [[Buf("h%d_%d" % (k, t)) for t in range(5)] for k in range(8)]
        GB = [[Buf("g%d_%d" % (j, t)) for t in range(5)] for j in range(11)]
        MB = [[Buf("m%d_%d" % (k, t)) for t in range(5)] for k in range(8)]
        WSB = [Buf("ws%d" % i) for i in range(NSLOT)]
        WSS = [P.sem("wsd%d" % i, 16) for i in range(NSLOT)]
        B_const = Buf("const")
        B_mod = [Buf("mod%d" % i) for i in range(9)]
        B_der = [Buf("der%d" % i) for i in range(5)]
        B_sq = [Buf() for _ in range(2)]
        B_sa = [Buf() for _ in range(2)]
        B_tmp = [Buf() for _ in range(2)]
        B_ln = Buf()
        B_stg = [Buf() for _ in range(2)]
        B_small = Buf("small")
        B_csb = Buf()
        B_cvo = Buf()

        slot_next = [0]
        slot_active = [list(range(NSLOT))]

        def wload(src_ap, view):
            act_ = slot_active[0]
            i = act_[slot_next[0] % len(act_)]
            slot_next[0] += 1
            t = WS[i]
            if view == "k128":
                dst = t.rearrange("p (k n) -> p k n", k=8)
            elif view == "k16":
                dst = t[:, 0:128].rearrange("p (k n) -> p k n", k=8)
            else:
                dst = t
            P.dma("gpsimd", lambda e, d=dst, s=src_ap: e.dma_start(out=d, in_=s),
                  writes=[WSB[i]], sem=WSS[i])
            return dst, WSB[i]

        def wcols(w, c0, n):
            return w.rearrange("(k p) n -> p k n", p=128)[:, :, c0:c0 + n]

        def mm(items, reads, writes):
            def fn(e, items=items):
                ins = None
                for (o, l, r, st, sp) in items:
                    ins = e.matmul(o, lhsT=l, rhs=r, start=st, stop=sp)
                return ins
            return P.op("tensor", fn, reads=reads, writes=writes)

        def act(out, in_, func, reads, writes, bias=None, scale=None):
            kw = {}
            if bias is not None:
                kw["bias"] = bias
            if scale is not None:
                kw["scale"] = scale
            return P.op("scalar", lambda e: e.activation(out=out, in_=in_, func=func, **kw),
                        reads=reads, writes=writes)

        out_toks = []
        sync_rr = [0]

        def stop_at(name):
            if stop == name:
                P.stopped = True

        P.op("gpsimd", lambda e: e.memset(ones_bf[:], 1.0), writes=[B_const])
        P.op("gpsimd", lambda e: e.memset(zer_bf[:], 0.0), writes=[B_const])
        P.op("gpsimd", lambda e: e.memset(ident_f[:], 0.0), writes=[B_const])
        P.op("gpsimd", lambda e: e.affine_select(out=ident_f[:], in_=ident_f[:], pattern=[[-1, 128]],
                                                  compare_op=ALU.not_equal, fill=1.0, base=0,
                                                  channel_multiplier=1), reads=[B_const], writes=[B_const])
        P.op("gpsimd", lambda e: e.tensor_copy(out=ident_bf[:], in_=ident_f[:]), reads=[B_const], writes=[B_const])
        P.op("gpsimd", lambda e: e.affine_select(out=maskneg[:], in_=zer_bf[:], pattern=[[1, 128]],
                                                  compare_op=ALU.is_ge, fill=MASKVAL, base=0,
                                                  channel_multiplier=-1), reads=[B_const], writes=[B_const])
        P.op("gpsimd", lambda e: e.tensor_copy(out=mask2[:, 0:64], in_=maskneg[:, 0:64]), reads=[B_const], writes=[B_const])
        P.op("gpsimd", lambda e: e.affine_select(out=mask2[:, 64:128], in_=zer_bf[:, 0:64], pattern=[[1, 64]],
                                                  compare_op=ALU.is_ge, fill=MASKVAL, base=64,
                                                  channel_multiplier=-1), reads=[B_const], writes=[B_const])
        P.op("gpsimd", lambda e: e.affine_select(out=mask2[:, 64:128], in_=mask2[:, 64:128], pattern=[[0, 64]],
                                                  compare_op=ALU.is_ge, fill=MASKVAL, base=-64,
                                                  channel_multiplier=1), reads=[B_const], writes=[B_const])
        P.dma("sync", lambda e: e.dma_start(out=CF[:], in_=cT), writes=[B_small])
        P.dma("sync", lambda e: e.dma_start(out=BADA[:], in_=b_ada3), writes=[B_small])
        P.dma("sync", lambda e: e.dma_start(out=G3[:], in_=g3), writes=[B_small])
        P.dma("sync", lambda e: e.dma_start(out=CW[:], in_=conv_wT), writes=[B_small])
        P.dma("sync", lambda e: e.dma_start(out=SCT[:], in_=scT), writes=[B_small])
        P.dma("sync", lambda e: e.dma_start(out=BFt[:], in_=b_f), writes=[B_small])
        xT3 = xT.rearrange("(k p) t -> p k t", p=128)
        for tb, (t0, n) in enumerate(TBS):
            P.dma("sync", lambda e, t0=t0, n=n: e.dma_start(out=xv[:, :, t0:t0 + n], in_=xT3[:, :, t0:t0 + n]),
                  writes=[XB[k][tb] for k in range(8)])
        act(CSB[:], CF[:], AF.Silu, [B_small], [B_csb])

        stop_at("setup")
        mod_state = {"kind": 0}

        def ada_kind(kind):
            bank = 6 + (kind % 2)
            for jj in range(8):
                j = kind * 8 + jj
                sl, sb_ = wload(wcols(w_ada, j * 128, 128), "k128")
                items = [(PB[bank][:, jj * 3:jj * 3 + 3], sl[:, k, :], CSB[:, k * 3:k * 3 + 3], k == 0, k == 7)
                         for k in range(8)]
                mm(items, [sb_, B_csb], [PBB[bank]])
            P.op("vector", lambda e, kind=kind, bank=bank: e.tensor_tensor(
                out=MOD[:, kind * 24:(kind + 1) * 24], in0=PB[bank][:, 0:24],
                in1=BADA[:, kind * 24:(kind + 1) * 24], op=ALU.add),
                reads=[PBB[bank], B_small], writes=[B_mod[kind]])
            if kind in (1, 4, 7):
                a = {1: 0, 4: 1, 7: 2}[kind]
                P.op("vector", lambda e, kind=kind, a=a: e.scalar_tensor_tensor(
                    out=DER[:, a * 24:(a + 1) * 24], in0=MOD[:, kind * 24:(kind + 1) * 24], scalar=1.0,
                    in1=G3[:, a * 24:(a + 1) * 24], op0=ALU.add, op1=ALU.mult),
                    reads=[B_mod[kind], B_small], writes=[B_der[a]])
            if kind in (2, 8):
                a = {2: 3, 8: 4}[kind]
                P.op("vector", lambda e, kind=kind, a=a: e.tensor_scalar(
                    out=DER[:, a * 24:(a + 1) * 24], in0=MOD[:, kind * 24:(kind + 1) * 24],
                    scalar1=0.5, scalar2=None, op0=ALU.mult),
                    reads=[B_mod[kind]], writes=[B_der[a]])

        ada_kind(0)
        ada_kind(1)

        cnt = {"sq": 0, "tmp": 0, "stat": 0, "stg": 0, "sa": 0, "stgm": 0, "fin": 0}

        def norm_tb(tb, a_idx, sh_kind, final=False):
            t0, n = TBS[tb]
            bank = 6 + (cnt["stat"] % 2)
            cnt["stat"] += 1
            for k in range(8):
                i = cnt["sq"] % 2
                cnt["sq"] += 1
                act(SQB[i][:, 0:n], xv[:, k, t0:t0 + n], AF.Square, [XB[k][tb]], [B_sq[i]])
                mm([(PB[bank][:, 0:n], ones_bf[:], SQB[i][:, 0:n], k == 0, k == 7)],
                   [B_sq[i], B_const], [PBB[bank]])
            act(LNMS[:, 0:n], PB[bank][:, 0:n], AF.Ln, [PBB[bank]], [B_ln], bias=EPS, scale=1.0 / D)
            act(PB[bank][:, 0:n], LNMS[:, 0:n], AF.Exp, [B_ln], [PBB[bank]], scale=-0.5)
            for k in range(8):
                if final:
                    i = cnt["fin"] % 4
                    cnt["fin"] += 1
                    stg_ = (STG + TMP)[i]
                    bst_ = (B_stg + B_tmp)[i]
                    P.op("vector", lambda e, k=k, stg_=stg_: e.scalar_tensor_tensor(
                        out=stg_[:, 0:n], in0=xv[:, k, t0:t0 + n], scalar=g3v[:, 3, k, 0:1],
                        in1=PB[bank][:, 0:n], op0=ALU.mult, op1=ALU.mult),
                        reads=[XB[k][tb], PBB[bank], B_small], writes=[bst_])
                    out_toks.append(P.dma("sync", lambda e, k=k, stg_=stg_: e.dma_start(
                        out=yT[k * 128:(k + 1) * 128, t0:t0 + n], in_=stg_[:, 0:n]), reads=[bst_]))
                    continue
                i = cnt["tmp"] % 2
                cnt["tmp"] += 1
                for (lo, ln_, seq) in segs(tb):
                    P.op("vector", lambda e, k=k, i=i, lo=lo, ln_=ln_, seq=seq: e.scalar_tensor_tensor(
                        out=TMP[i][:, lo:lo + ln_], in0=xv[:, k, t0 + lo:t0 + lo + ln_],
                        scalar=der_ap(a_idx, k, seq), in1=PB[bank][:, lo:lo + ln_],
                        op0=ALU.mult, op1=ALU.mult),
                        reads=[XB[k][tb], PBB[bank], B_der[a_idx]], writes=[B_tmp[i]])
                for (lo, ln_, seq) in segs(tb):
                    act(H[:, k, t0 + lo:t0 + lo + ln_], TMP[i][:, lo:lo + ln_], AF.Identity,
                        [B_tmp[i], B_mod[sh_kind]], [HB[k][tb]], bias=mod_ap(sh_kind, k, seq), scale=1.0)

        stop_at("ada01")
        for tb in range(5):
            norm_tb(tb, 0, 0)
        stop_at("norm1")

        def ffn(w_i, w_o, gh_idx, gate_buf, between_j=None, after_tb=None):
            for half in range(2):
                for jj in range(11):
                    j = half * 11 + jj
                    sA, bA = wload(wcols(w_i, j * 128, 128), "k128")
                    sBv, bB = wload(wcols(w_i, DFF + j * 128, 128), "k128")
                    for tb, (t0, n) in enumerate(TBS):
                        ia = cnt["sa"] % 2
                        cnt["sa"] += 1
                        ba, bb = ia, 2 + ia
                        hb = [HB[k][tb] for k in range(8)]
                        mm([(PB[ba][:, 0:n], sA[:, k, :], H[:, k, t0:t0 + n], k == 0, k == 7) for k in range(8)],
                           [bA] + hb, [PBB[ba]])
                        mm([(PB[bb][:, 0:n], sBv[:, k, :], H[:, k, t0:t0 + n], k == 0, k == 7) for k in range(8)],
                           [bB] + hb, [PBB[bb]])
                        act(SAB[ia][:, 0:n], PB[ba][:, 0:n], AF.Silu, [PBB[ba]], [B_sa[ia]])
                        P.op("vector", lambda e, ia=ia, bb=bb, jj=jj, t0=t0, n=n: e.tensor_tensor(
                            out=gv[:, jj, t0:t0 + n], in0=PB[bb][:, 0:n], in1=SAB[ia][:, 0:n], op=ALU.mult),
                            reads=[PBB[bb], B_sa[ia]], writes=[GB[jj][tb]])
                    if between_j is not None:
                        between_j(half, jj)
                so = []
                for jj in range(11):
                    j = half * 11 + jj
                    so.append(wload(w_o[j * 128:(j + 1) * 128, :], "flat"))
                for tb, (t0, n) in enumerate(TBS):
                    for mo in range(8):
                        bank = 4 + (cnt["stg"] % 2)
                        cnt["stg"] += 1
                        mm([(PB[bank][:, 0:n], so[jj][0][:, mo * 128:(mo + 1) * 128], gv[:, jj, t0:t0 + n],
                             jj == 0, jj == 10) for jj in range(11)],
                           [so[jj][1] for jj in range(11)] + [GB[jj][tb] for jj in range(11)], [PBB[bank]])
                        for (lo, ln_, seq) in segs(tb):
                            P.op("vector", lambda e, bank=bank, mo=mo, t0=t0, lo=lo, ln_=ln_, seq=seq: e.scalar_tensor_tensor(
                                out=xv[:, mo, t0 + lo:t0 + lo + ln_], in0=PB[bank][:, lo:lo + ln_],
                                scalar=der_ap(gh_idx, mo, seq), in1=xv[:, mo, t0 + lo:t0 + lo + ln_],
                                op0=ALU.mult, op1=ALU.add),
                                reads=[PBB[bank], gate_buf, XB[mo][tb]], writes=[XB[mo][tb]])
                        if half == 1 and after_tb is not None and tb > 0 and mo == 3:
                            after_tb(tb - 1)
                if half == 1 and after_tb is not None:
                    after_tb(4)

        def ada_between(half, jj):
            if half == 0 and jj < 7:
                ada_kind(2 + jj)

        xs3 = xs.rearrange("(k p) t -> p k t", p=128)
        spill = []

        def after_ffn1(tb):
            norm_tb(tb, 1, 3)
            t0, n = TBS[tb]
            spill.append(P.dma("sync", lambda e: e.dma_start(out=xs3[:, :, t0:t0 + n], in_=xv[:, :, t0:t0 + n]),
                               reads=[XB[k][tb] for k in range(8)]))

        ffn(w1i, w1o, 3, B_der[3], between_j=ada_between, after_tb=after_ffn1)

        stop_at("ffn1")
        class _AllSpill:
            pass
        def seeded(k, name=""):
            b = Buf(name)
            for tk in spill:
                if tk is not None:
                    b.r[tk[0]] = max(b.r.get(tk[0], 0), tk[1])
            return b

        B_q = [[seeded(0) for _ in range(5)] for _ in range(2)]
        B_qrow = [seeded(0) for _ in range(2)]
        B_k = [[seeded(1) for _ in range(5)] for _ in range(2)]
        B_krow = seeded(1)
        B_lf = seeded(2, "lf")
        B_fp = seeded(3, "fp")
        B_o16 = seeded(4, "o16")
        B_v = [seeded(2) for _ in range(5)]
        B_vones = seeded(2)
        B_kc = [[seeded(3) for _ in range(2)] for _ in range(2)]
        B_kcrow = seeded(3)
        B_vc = [[seeded(4) for _ in range(2)] for _ in range(2)]
        B_vcones = seeded(4)
        B_cs = seeded(5)
        B_fs = seeded(6)
        B_ct = seeded(7)
        B_t1 = seeded(7)
        B_t2 = seeded(7)
        B_sga = [Buf() for _ in range(5)]
        B_pt = [Buf() for _ in range(4)]
        B_nf = Buf()
        B_nfs = Buf()
        B_fnew = Buf()

        sF, bF = wload(wcols(w_in, OFF_F, 16), "k16")
        for tb, (t0, n) in enumerate(TBS):
            bank = tb % 2
            mm([(PB[bank][0:16, 0:n], sF[:, k, :], H[:, k, t0:t0 + n], k == 0, k == 7) for k in range(8)],
               [bF] + [HB[k][tb] for k in range(8)], [PBB[bank]])
            P.op("vector", lambda e, bank=bank, n=n: e.tensor_scalar(
                out=T1[0:16, 0:n], in0=PB[bank][0:16, 0:n], scalar1=BFt[0:16, 0:1], scalar2=None, op0=ALU.add),
                reads=[PBB[bank], B_small], writes=[B_t1])
            act(T2[0:16, 0:n], T1[0:16, 0:n], AF.Abs, [B_t1], [B_t2])
            act(T2[0:16, 0:n], T2[0:16, 0:n], AF.Exp, [B_t2], [B_t2], scale=-1.0)
            act(T2[0:16, 0:n], T2[0:16, 0:n], AF.Ln, [B_t2], [B_t2], bias=1.0, scale=1.0)
            P.op("vector", lambda e, t0=t0, n=n: e.scalar_tensor_tensor(
                out=LF[0:16, t0:t0 + n], in0=T1[0:16, 0:n], scalar=0.0, in1=T2[0:16, 0:n],
                op0=ALU.min, op1=ALU.subtract),
                reads=[B_t1, B_t2], writes=[B_lf])
        out_toks.append(P.dma("sync", lambda e: e.dma_start(out=lfo, in_=LF[0:16, :]), reads=[B_lf]))
        P.op("vector", lambda e: e.memset(ONES16[0:16, :], 1.0), writes=[B_o16])
        P.op("vector", lambda e: e.tensor_tensor_scan(
            out=FP[0:16, :], data0=ONES16[0:16, :], data1=LF[0:16, 0:NPR], initial=0.0,
            op0=ALU.mult, op1=ALU.add), reads=[B_o16, B_lf], writes=[B_fp])
        for s in range(2):
            P.dma("sync", lambda e, s=s: e.dma_start(out=CS[0:16, s, 0:PAST], in_=lfc[s]), writes=[B_cs])
        for s in range(2):
            P.op("vector", lambda e, s=s: e.tensor_copy(out=CS[0:16, s, PAST:PAST + SQ],
                                                        in_=LF[0:16, NPR + SQ * s:NPR + SQ * (s + 1)]),
                 reads=[B_lf], writes=[B_cs])
        for s in range(2):
            P.op("vector", lambda e, s=s: e.tensor_tensor_scan(
                out=FS[0:16, s, :], data0=ONES16[0:16, 0:PAST + SQ], data1=CS[0:16, s, :], initial=0.0,
                op0=ALU.mult, op1=ALU.add), reads=[B_o16, B_cs], writes=[B_fs])
        P.op("vector", lambda e: e.tensor_copy(out=CT16[0:16, 0:NPR], in_=FP[0:16, :]), reads=[B_fp], writes=[B_ct])
        for s in range(2):
            P.op("vector", lambda e, s=s: e.tensor_copy(out=CT16[0:16, NPR + SQ * s:NPR + SQ * (s + 1)],
                                                        in_=FS[0:16, s, PAST:PAST + SQ]),
                 reads=[B_fs], writes=[B_ct])
            P.op("vector", lambda e, s=s: e.tensor_copy(out=FNEW[0:16, SQ * s:SQ * (s + 1)],
                                                        in_=FS[0:16, s, PAST:PAST + SQ]),
                 reads=[B_fs], writes=[B_fnew])
        stop_at("fgate")
        def alias_seed(bufs, srcs):
            toks = {}
            for sbuf in srcs:
                if sbuf.w is not None:
                    s_, v_ = sbuf.w
                    toks[s_] = max(toks.get(s_, 0), v_)
                for s_, v_ in sbuf.r.items():
                    toks[s_] = max(toks.get(s_, 0), v_)
            for b_ in bufs:
                for s_, v_ in toks.items():
                    b_.r[s_] = max(b_.r.get(s_, 0), v_)

        U = rx_f32(5, 0, 2182)
        CY = rx_f32(6, 0, 2180)
        k_augs = [k_aug, WSALL[:, 6 * 1024:6 * 1024 + 2 * NT].rearrange("p (h t) -> p h t", h=2)]
        v_exts = [v_ext, WSALL[:, 11 * 1024:11 * 1024 + 17 * 256].rearrange("p (i h d) -> p i h d", i=17, h=2)]
        B_u = Buf("u")
        B_cy = Buf("cy")
        alias_seed([B_u], [B_cs])
        alias_seed([B_cy], [B_fs])
        B_ks = [B_k, [[Buf() for _ in range(5)] for _ in range(2)]]
        B_krows = [B_krow, Buf()]
        B_vs = [B_v, [Buf() for _ in range(5)]]
        B_vones_s = [B_vones, Buf()]
        alias_seed([x_ for l_ in B_ks[1] for x_ in l_] + [B_krows[1]], WSB[6:11])
        alias_seed(B_vs[1] + [B_vones_s[1]], WSB[11:16])
        alias_seed(B_vs[0] + [B_vones_s[0]], [B_lf])
        alias_seed([x_ for l_ in B_kc for x_ in l_] + [B_kcrow], [B_fp])
        alias_seed([x_ for l_ in B_vc for x_ in l_] + [B_vcones], [B_o16])
        slot_active[0] = list(range(5))
        slot_next[0] = 0
        STGM = [STG[0], STG[1], WSALL[:, 5 * 1024:6 * 1024].bitcast(F32)]
        B_stgm = [B_stg[0], B_stg[1], Buf()]
        alias_seed([B_stgm[2]], [WSB[5]])

        ONES2 = struct.unpack("<f", struct.pack("<I", 0x3F803F80))[0]

        def const_memsets():
            for st in range(2):
                P.op("vector", lambda e, st=st: e.memset(k_augs[st][64:65, :, :].bitcast(F32), ONES2),
                     writes=[B_krows[st]])
                P.op("vector", lambda e, st=st: e.memset(v_exts[st][:, :, :, 64:128].bitcast(F32), ONES2),
                     writes=[B_vones_s[st]])
            P.op("vector", lambda e: e.memset(kc_aug[64:65, :, :, :].bitcast(F32), ONES2), writes=[B_kcrow])
            P.op("vector", lambda e: e.memset(vc_ext[:, :, :, :, 64:128].bitcast(F32), ONES2), writes=[B_vcones])
            P.op("vector", lambda e: e.memset(U[:, 0:2], 0.0), writes=[B_u])

        cvv = CVO[:, :].rearrange("p (k s r) -> p k s r", k=8, s=3)
        sctv = SCT[:, :].rearrange("p (k s r) -> p k s r", k=8, s=2)
        UOFF = [2, 2052, 2118]
        CYOFF = [0, 2050, 2116]
        SEQ0 = [0, NPR, NPR + SQ]
        sbank = [0]
        obank = [0]
        pbank = [0]
        ptc = [0]

        def next_s():
            b = sbank[0] % 4
            sbank[0] += 1
            return b

        def next_o():
            b = 4 + (obank[0] % 2)
            obank[0] += 1
            return b

        def next_p():
            b = 6 + (pbank[0] % 2)
            pbank[0] += 1
            return b

        def proj(sw, bw, tb):
            t0, n = TBS[tb]
            b = next_p()
            mm([(PB[b][:, 0:n], sw[:, k, :], H[:, k, t0:t0 + n], k == 0, k == 7) for k in range(8)],
               [bw] + [HB[k][tb] for k in range(8)], [PBB[b]])
            return b

        def proj_g(sw, bw, tb):
            t0, n = TBS[tb]
            b = next_p()
            rd = [bw] + [HB[k][tb] for k in range(8)]
            mm([(PB[b][:, 0:n], sw[:, k, :], H[:, k, t0:t0 + n], k == 0, False) for k in range(4)], rd, [PBB[b]])
            yield
            mm([(PB[b][:, 0:n], sw[:, k, :], H[:, k, t0:t0 + n], False, k == 7) for k in range(4, 8)], rd, [PBB[b]])
            return b

        def sigmoid_inplace(dst, src_ps, rd, wr):
            act(dst, src_ps, AF.Exp, rd, wr, scale=-1.0)
            act(dst, dst, AF.Ln, wr, wr, bias=1.0, scale=1.0)
            act(dst, dst, AF.Exp, wr, wr, scale=-1.0)

        def sigmoid_split(dst, src_ps, rd, wr):
            return [lambda: act(dst, src_ps, AF.Exp, rd, wr, scale=-1.0),
                    lambda: act(dst, dst, AF.Ln, wr, wr, bias=1.0, scale=1.0),
                    lambda: act(dst, dst, AF.Exp, wr, wr, scale=-1.0)]

        mdone = set()

        def conv_items(c):
            items = []
            w = {}

            def ld1():
                w["C"] = wload(wcols(w_in, OFF_C + c * 128, 128), "k128")
                w["X"] = wload(wcols(w_in, OFF_X + c * 128, 128), "k128")
            items.append(ld1)

            def pads():
                for s in range(2):
                    P.op("vector", lambda e, s=s: e.tensor_copy(out=U[:, UOFF[1 + s] - 2:UOFF[1 + s]],
                                                                in_=sctv[:, c, s, :]),
                         reads=[B_small], writes=[B_u])
            items.append(pads)
            for tb, (t0, n) in enumerate(TBS):
                def cgrp(tb=tb, t0=t0, n=n):
                    bc = yield from proj_g(w["C"][0], w["C"][1], tb)
                    P.op("vector", lambda e: e.tensor_copy(out=T1[:, 0:n], in_=PB[bc][:, 0:n]),
                         reads=[PBB[bc]], writes=[B_t1])
                items.append(cgrp)

                def xgrp(tb=tb, t0=t0, n=n):
                    bx = yield from proj_g(w["X"][0], w["X"][1], tb)
                    for (lo, ln_, seq) in segs(tb):
                        u0 = UOFF[seq] + (t0 + lo - SEQ0[seq])
                        P.op("vector", lambda e, lo=lo, ln_=ln_, u0=u0: e.tensor_tensor(
                            out=U[:, u0:u0 + ln_], in0=PB[bx][:, lo:lo + ln_], in1=T1[:, lo:lo + ln_], op=ALU.mult),
                            reads=[PBB[bx], B_t1], writes=[B_u])
                items.append(xgrp)

            def ld2():
                w["B"] = wload(wcols(w_in, OFF_B + c * 128, 128), "k128")
                w["G"] = wload(wcols(w_in, OFF_GC + c * 128, 128), "k128")
            items.append(ld2)

            def taps():
                for s3 in range(3):
                    end = UOFF[s3] + [NPR, SQ, SQ][s3]
                    P.op("vector", lambda e, s3=s3, end=end: e.tensor_copy(out=cvv[:, c, s3, :], in_=U[:, end - 2:end]),
                         reads=[B_u], writes=[B_cvo])
                P.op("vector", lambda e: e.tensor_scalar(out=CY[:, 0:2180], in0=U[:, 0:2180],
                                                         scalar1=CW[:, c * 3:c * 3 + 1], scalar2=None, op0=ALU.mult),
                     reads=[B_u, B_small], writes=[B_cy])
                P.op("vector", lambda e: e.scalar_tensor_tensor(out=CY[:, 0:2180], in0=U[:, 1:2181],
                                                                scalar=CW[:, c * 3 + 1:c * 3 + 2], in1=CY[:, 0:2180],
                                                                op0=ALU.mult, op1=ALU.add),
                     reads=[B_u, B_cy, B_small], writes=[B_cy])
                P.op("vector", lambda e: e.scalar_tensor_tensor(out=CY[:, 0:2180], in0=U[:, 2:2182],
                                                                scalar=CW[:, c * 3 + 2:c * 3 + 3], in1=CY[:, 0:2180],
                                                                op0=ALU.mult, op1=ALU.add),
                     reads=[B_u, B_cy, B_small], writes=[B_cy])
            items.append(taps)
            for tb, (t0, n) in enumerate(TBS):
                def ggrp(tb=tb, t0=t0, n=n):
                    bg = yield from proj_g(w["G"][0], w["G"][1], tb)
                    if split_act[0]:
                        return ("front", sigmoid_split(T2[:, 0:n], PB[bg][:, 0:n], [PBB[bg]], [B_t2]))
                    sigmoid_inplace(T2[:, 0:n], PB[bg][:, 0:n], [PBB[bg]], [B_t2])
                items.append(ggrp)

                def bgrp(tb=tb, t0=t0, n=n):
                    bb = yield from proj_g(w["B"][0], w["B"][1], tb)
                    for (lo, ln_, seq) in segs(tb):
                        cy0 = CYOFF[seq] + (t0 + lo - SEQ0[seq])
                        P.op("vector", lambda e, lo=lo, ln_=ln_, cy0=cy0: e.tensor_tensor(
                            out=T1[:, lo:lo + ln_], in0=PB[bb][:, lo:lo + ln_], in1=CY[:, cy0:cy0 + ln_], op=ALU.mult),
                            reads=[PBB[bb], B_cy], writes=[B_t1])
                    for hh in range(2):
                        r0 = hh * 64
                        for si, (lo, ln_, seq) in enumerate(segs(tb)):
                            key = (c, tb, hh, si)
                            if key not in mdone:
                                P.op("vector", lambda e, r0=r0, lo=lo, ln_=ln_: e.tensor_tensor(
                                    out=mv[r0:r0 + 64, c, t0 + lo:t0 + lo + ln_], in0=T1[r0:r0 + 64, lo:lo + ln_],
                                    in1=T2[r0:r0 + 64, lo:lo + ln_], op=ALU.mult),
                                    reads=[B_t1, B_t2], writes=[MB[c][tb]])
                                mdone.add(key)
                            else:
                                P.op("vector", lambda e, r0=r0, lo=lo, ln_=ln_: e.tensor_tensor(
                                    out=T1[r0:r0 + 64, lo:lo + ln_], in0=T1[r0:r0 + 64, lo:lo + ln_],
                                    in1=T2[r0:r0 + 64, lo:lo + ln_], op=ALU.mult),
                                    reads=[B_t1, B_t2], writes=[B_t1])
                                P.op("vector", lambda e, r0=r0, lo=lo, ln_=ln_: e.tensor_tensor(
                                    out=mv[r0:r0 + 64, c, t0 + lo:t0 + lo + ln_], in0=T1[r0:r0 + 64, lo:lo + ln_],
                                    in1=mv[r0:r0 + 64, c, t0 + lo:t0 + lo + ln_], op=ALU.add),
                                    reads=[B_t1, MB[c][tb]], writes=[MB[c][tb]])
                items.append(bgrp)
            return items

        def kv_items(c, st):
            items = []
            w = {}
            ka, va = k_augs[st], v_exts[st]

            def ld():
                w["K"] = wload(wcols(w_in, OFF_K + c * 128, 128), "k128")
                w["V"] = wload(wcols(w_in, OFF_V + c * 128, 128), "k128")
            items.append(ld)
            for tb, (t0, n) in enumerate(TBS):
                def kgrp(tb=tb, t0=t0, n=n):
                    bk = yield from proj_g(w["K"][0], w["K"][1], tb)
                    for hh in range(2):
                        P.op("vector", lambda e, hh=hh: e.tensor_copy(out=ka[0:64, hh, t0:t0 + n],
                                                                     in_=PB[bk][hh * 64:(hh + 1) * 64, 0:n]),
                             reads=[PBB[bk]], writes=[B_ks[st][hh][tb]])
                    i = cnt["stgm"] % 3
                    cnt["stgm"] += 1
                    P.op("vector", lambda e, i=i: e.tensor_copy(out=STGM[i][:, 0:n], in_=PB[bk][:, 0:n]),
                         reads=[PBB[bk]], writes=[B_stgm[i]])
                    out_toks.append(P.dma("sync", lambda e, i=i: e.dma_start(
                        out=kTo[c * 128:(c + 1) * 128, t0:t0 + n], in_=STGM[i][:, 0:n]), reads=[B_stgm[i]]))
                items.append(kgrp)
            for tg in range(5):
                def vgrp(tg=tg):
                    tiles = list(range(tg * 4, min(17, tg * 4 + 4)))
                    bv = next_p()
                    for qi, ti in enumerate(tiles):
                        tb = min(ti // 4, 4)
                        if qi in (2,):
                            yield
                        mm([(PB[bv][:, qi * 128:(qi + 1) * 128], H[:, k, ti * 128:(ti + 1) * 128],
                             w["V"][0][:, k, :], k == 0, k == 7) for k in range(8)],
                           [w["V"][1]] + [HB[k][tb] for k in range(8)], [PBB[bv]])
                    nt_ = len(tiles)
                    pv4 = PB[bv][:, 0:nt_ * 128].rearrange("p (i h d) -> p i h d", i=nt_, h=2)
                    P.op("vector", lambda e: e.tensor_copy(out=va[:, tiles[0]:tiles[0] + nt_, :, 0:64], in_=pv4),
                         reads=[PBB[bv]], writes=[B_vs[st][tg]])
                    i = cnt["stgm"] % 3
                    cnt["stgm"] += 1
                    P.op("vector", lambda e, i=i: e.tensor_copy(out=STGM[i][:, 0:nt_ * 128], in_=PB[bv][:, 0:nt_ * 128]),
                         reads=[PBB[bv]], writes=[B_stgm[i]])
                    t00 = tiles[0] * 128
                    out_toks.append(P.dma("sync", lambda e, i=i: e.dma_start(
                        out=vo[t00:t00 + nt_ * 128, c * 128:(c + 1) * 128].rearrange("(i p) f -> p i f", p=128),
                        in_=STGM[i][:, 0:nt_ * 128].rearrange("p (i f) -> p i f", i=nt_)), reads=[B_stgm[i]]))
                items.append(vgrp)
            return items

        split_act = [False]
        side_q = []
        side_done = set()

        pushed = [0]
        lag_q = []
        gstep = [0]

        def run_lag(force=False):
            while lag_q and (force or lag_q[0][0] <= gstep[0]):
                _, tag, fn = lag_q.pop(0)
                fn()
                if tag is not None:
                    side_done.add(tag)
                if force:
                    break

        import inspect as _insp

        def finish_item(tag, more):
            if isinstance(more, tuple) and more[0] == "lag":
                base = max(gstep[0], lag_q[-1][0] if lag_q else 0)
                for q_, f in enumerate(more[1]):
                    lag_q.append((base + 1 + q_, tag if q_ == len(more[1]) - 1 else None, f))
            elif isinstance(more, tuple) and more[0] == "front":
                side_q[0:0] = [(None, f) for f in more[1][:-1]] + [(tag, more[1][-1])]
                pushed[0] += len(more[1])
            elif tag is not None:
                side_done.add(tag)

        def side(n=1):
            for _ in range(n):
                if not side_q:
                    return
                tag, obj = side_q.pop(0)
                res = obj() if callable(obj) else obj
                if _insp.isgenerator(res):
                    try:
                        next(res)
                        side_q.insert(0, (tag, res))
                        pushed[0] += 1
                        continue
                    except StopIteration as e_:
                        res = e_.value
                finish_item(tag, res)

        def run_full(it):
            res = it()
            if _insp.isgenerator(res):
                try:
                    while True:
                        next(res)
                except StopIteration as e_:
                    res = e_.value
            if isinstance(res, tuple):
                for f in res[1]:
                    f()

        def ensure(tag):
            while tag not in side_done:
                if lag_q:
                    run_lag(force=True)
                else:
                    assert side_q, tag
                    side(1)

        B_lnd = B_ln
        B_rd = B_tmp[0]
        B_tt = B_tmp[1]

        def normalize(ob, c, hh, tbm, col0, ncol):
            r0 = hh * 64
            act(LND[64:128, 0:ncol], PB[ob][64:128, 0:ncol], AF.Ln, [PBB[ob]], [B_lnd])
            act(RD[64:128, 0:ncol], LND[64:128, 0:ncol], AF.Exp, [B_lnd], [B_rd], scale=-1.0)
            P.op("vector", lambda e: e.tensor_tensor(out=TT[r0:r0 + 64, 0:ncol], in0=PB[ob][0:64, 0:ncol],
                                                     in1=RD[64:128, 0:ncol], op=ALU.mult),
                 reads=[PBB[ob], B_rd], writes=[B_tt])
            key = (c, tbm, hh, 0 if tbm < 4 else (col0 - NPR) // SQ)
            if key not in mdone:
                P.op("vector", lambda e: e.tensor_tensor(out=mv[r0:r0 + 64, c, col0:col0 + ncol],
                                                         in0=TT[r0:r0 + 64, 0:ncol],
                                                         in1=SGA[r0:r0 + 64, col0:col0 + ncol], op=ALU.mult),
                     reads=[B_tt, B_sga[tbm]], writes=[MB[c][tbm]])
                mdone.add(key)
            else:
                P.op("vector", lambda e: e.tensor_tensor(out=TT[r0:r0 + 64, 0:ncol], in0=TT[r0:r0 + 64, 0:ncol],
                                                         in1=SGA[r0:r0 + 64, col0:col0 + ncol], op=ALU.mult),
                     reads=[B_tt, B_sga[tbm]], writes=[B_tt])
                P.op("vector", lambda e: e.tensor_tensor(out=mv[r0:r0 + 64, c, col0:col0 + ncol],
                                                         in0=TT[r0:r0 + 64, 0:ncol],
                                                         in1=mv[r0:r0 + 64, c, col0:col0 + ncol], op=ALU.add),
                     reads=[B_tt, MB[c][tbm]], writes=[MB[c][tbm]])

        qga_w = {}
        pend_norm = []

        def qga_load(c):
            qga_w[c] = (wload(wcols(w_in, OFF_Q + c * 128, 128), "k128"),
                        wload(wcols(w_in, OFF_GA + c * 128, 128), "k128"))

        def cache_loads(c):
            for s in range(2):
                for hh in range(2):
                    P.dma("gpsimd", lambda e, s=s, hh=hh: e.dma_start(
                        out=kc_aug[0:64, s, hh, :], in_=kcT[s, 2 * c + hh]), writes=[B_kc[s][hh]])
                P.dma("gpsimd", lambda e, s=s: e.dma_start(
                    out=VST[s],
                    in_=vc[s].rearrange("(i p) f -> p i f", p=128)[:, :, c * 128:(c + 1) * 128]),
                    writes=[B_vst[s]])

        def cache_scatter(c):
            for s in range(2):
                P.op("vector", lambda e, s=s: e.tensor_copy(
                    out=vc_ext[:, s, :, :, 0:64], in_=VST[s].rearrange("p i (h d) -> p i h d", h=2)),
                    reads=[B_vst[s]], writes=[B_vc[s][0], B_vc[s][1]])

        def qga_items(c):
            items = []
            (sQ, bQ), (sGa, bGa) = qga_w[c]
            for tb in (0, 3, 4, 1, 2):
                t0, n = TBS[tb]

                def qgrp(tb=tb, t0=t0, n=n):
                    bq = yield from proj_g(sQ, bQ, tb)
                    for hh in range(2):
                        P.op("vector", lambda e, hh=hh: e.tensor_scalar(
                            out=q_aug[0:64, hh, t0:t0 + n], in0=PB[bq][hh * 64:(hh + 1) * 64, 0:n],
                            scalar1=0.125, scalar2=None, op0=ALU.mult),
                            reads=[PBB[bq]], writes=[B_q[hh][tb]])
                items.append((("q", c, tb), qgrp))

                def ggrp(tb=tb, t0=t0, n=n):
                    bg = yield from proj_g(sGa, bGa, tb)
                    return ("lag", sigmoid_split(SGA[:, t0:t0 + n], PB[bg][:, 0:n], [PBB[bg]], [B_sga[tb]]))
                items.append((("ga", c, tb), ggrp))
            return items

        def pair_main(c, st, nside):
            ka, va = k_augs[st], v_exts[st]
            for hh in range(2):
                P.dma("sync", lambda e, hh=hh: e.dma_start(out=q_aug[64:65, hh, :],
                                                          in_=CT16[2 * c + hh:2 * c + hh + 1, :]),
                      reads=[B_ct], writes=[B_qrow[hh]])
            step = [0]
            nsteps = 88.0
            n0 = len(side_q)
            p0 = pushed[0]

            def flush_norm():
                if pend_norm:
                    a_ = pend_norm.pop(0)
                    ensure(("ga", a_[1], a_[3]))
                    normalize(*a_)

            def maybe_side():
                step[0] += 1
                gstep[0] += 1
                run_lag()
                tot = 2.4 * nside
                want = int(tot * step[0] / 88.0)
                while side_q and (n0 + (pushed[0] - p0) - len(side_q)) < want:
                    side(1)

            def sample_block(s, hh):
                ensure(("q", c, 4))
                ensure(("ga", c, 4))
                if s == 0 and hh == 0:
                    cache_scatter(c)
                while pend_norm:
                    flush_norm()
                head = 2 * c + hh
                qc0 = NPR + SQ * s
                sb1 = next_s()
                mm([(PB[sb1][:, i * 64:(i + 1) * 64], kc_aug[0:65, s, hh, i * 128:(i + 1) * 128],
                     q_aug[0:65, hh, qc0:qc0 + SQ], True, True) for i in range(8)],
                   [B_kc[s][hh], B_kcrow, B_q[hh][4], B_qrow[hh]], [PBB[sb1]])
                sb2 = next_s()
                mm([(PB[sb2][:, 0:64], ka[0:65, hh, NPR:NPR + 2 * SQ], q_aug[0:65, hh, qc0:qc0 + SQ], True, False),
                    (PB[sb2][:, 0:64], ident_bf[:], mask2[:, s * 64:(s + 1) * 64], False, True)],
                   [B_ks[st][hh][4], B_krows[st], B_q[hh][4], B_qrow[hh], B_const], [PBB[sb2]])
                pi = ptc[0] % 4
                ptc[0] += 1
                for i in range(8):
                    act(PT[pi][:, i * 64:(i + 1) * 64], PB[sb1][:, i * 64:(i + 1) * 64], AF.Exp,
                        [PBB[sb1], B_nfs], [B_pt[pi]],
                        bias=NFS[:, (s * 8 + i) * 16 + head:(s * 8 + i) * 16 + head + 1], scale=1.0)
                pi2 = ptc[0] % 4
                ptc[0] += 1
                act(PT[pi2][:, 0:64], PB[sb2][:, 0:64], AF.Exp, [PBB[sb2], B_nfs], [B_pt[pi2]],
                    bias=NFS[:, 256 + head:256 + head + 1], scale=1.0)
                maybe_side()
                ob = next_o()
                items = [(PB[ob][:, 0:64], vc_ext[:, s, i, hh, :], PT[pi][:, i * 64:(i + 1) * 64], i == 0, False)
                         for i in range(8)]
                items.append((PB[ob][:, 0:64], va[:, 16, hh, :], PT[pi2][:, 0:64], False, True))
                mm(items, [B_vc[s][hh], B_vcones, B_vs[st][4], B_vones_s[st], B_pt[pi], B_pt[pi2]], [PBB[ob]])
                maybe_side()
                pend_norm.append((ob, c, hh, 4, qc0, 64))


            cache_loads(c)
            for hh in range(2):
                head = 2 * c + hh
                for qb in (0, 3, 1, 2):
                    ntile = 4 * qb + 4
                    ob = next_o()
                    q0 = qb * 512
                    ensure(("q", c, qb))

                    def S(i, hh=hh, qb=qb, q0=q0):
                        sb_ = next_s()
                        r = i - 4 * qb
                        lo = max(r, 0) * 128
                        items = [(PB[sb_][:, lo:512], ka[0:65, hh, i * 128:(i + 1) * 128],
                                  q_aug[0:65, hh, q0 + lo:q0 + 512], True, r < 0)]
                        if r >= 0:
                            items.append((PB[sb_][:, lo:lo + 128], ident_bf[:], maskneg[:], False, True))
                        mm(items, [B_ks[st][hh][i // 4], B_krows[st], B_q[hh][qb], B_qrow[hh], B_const], [PBB[sb_]])
                        return sb_, lo

                    pend = [S(0), S(1), S(2)]
                    for i in range(ntile):
                        sb_, lo = pend.pop(0)
                        pi = ptc[0] % 4
                        ptc[0] += 1
                        act(PT[pi][:, lo:512], PB[sb_][:, lo:512], AF.Exp, [PBB[sb_], B_nf], [B_pt[pi]],
                            bias=(None if PROBE == "nobias" else NF[:, i * 16 + head:i * 16 + head + 1]), scale=1.0)
                        if i + 3 < ntile:
                            pend.append(S(i + 3))
                        if i == 3:
                            while pend_norm:
                                flush_norm()
                        maybe_side()
                        mm([(PB[ob][:, lo:512], va[:, i, hh, :], PT[pi][:, lo:512], i == 0, i == ntile - 1)],
                           [B_vs[st][i // 4], B_vones_s[st], B_pt[pi]], [PBB[ob]])
                    pend_norm.append((ob, c, hh, qb, q0, 512))
                    if qb == 3:
                        sample_block(0, hh)
                    elif qb == 1:
                        sample_block(1, hh)

        B_vst = [Buf(), Buf()]
        for it in kv_items(0, 0):
            run_full(it)
        const_memsets()
        def tr(items, reads, writes):
            def fn(e, items=items):
                ins = None
                for (o, i_, idn) in items:
                    ins = e.transpose(o, i_, idn)
                return ins
            return P.op("tensor", fn, reads=reads, writes=writes)

        idn16 = ident_f[0:16, 0:16]
        tr([(PB[2][:, i * 16:(i + 1) * 16], FP[0:16, i * 128:(i + 1) * 128], idn16) for i in range(16)],
           [B_fp, B_const], [PBB[2]])
        act(NF[:, :], PB[2][:, 0:256], AF.Copy, [PBB[2]], [B_nf], scale=-1.0)
        tr([(PB[3][:, (s * 8 + i) * 16:(s * 8 + i + 1) * 16], FS[0:16, s, i * 128:(i + 1) * 128], idn16)
            for s in range(2) for i in range(8)] +
           [(PB[3][:, 256:272], FNEW[0:16, :], idn16)],
           [B_fs, B_fnew, B_const], [PBB[3]])
        act(NFS[:, :], PB[3][:, 0:272], AF.Copy, [PBB[3]], [B_nfs], scale=-1.0)

        alias_seed([x_ for l_ in B_kc for x_ in l_] + [B_kcrow], [B_fp])
        alias_seed([B_cy], [B_fs])
        split_act[0] = True
        qga_load(0)
        for c in range(8):
            st = c % 2
            side_q.extend(qga_items(c))
            cv = conv_items(c)
            if c + 1 < 8:
                side_q.extend((None, f) for f in kv_items(c + 1, 1 - st))
            side_q.append((None, cv[0]))
            if c + 1 < 8:
                side_q.append((None, lambda c=c: qga_load(c + 1)))
            side_q.extend((None, f) for f in cv[1:])
            if PROBE == "serial":
                while side_q or lag_q:
                    gstep[0] += 1
                    run_lag()
                    side(1)
            pair_main(c, st, len(side_q))
            while side_q or lag_q:
                gstep[0] += 1
                run_lag()
                side(1)
        while pend_norm:
            a_ = pend_norm.pop(0)
            normalize(*a_)
        out_toks.append(P.dma("sync", lambda e: e.dma_start(out=cvo, in_=CVO[:, :]), reads=[B_cvo]))

        alias_seed([WSB[5]], [B_stgm[2]])
        alias_seed(WSB[6:11], [x_ for l_ in B_ks[1] for x_ in l_] + [B_krows[1]])
        alias_seed(WSB[11:16], B_vs[1] + [B_vones_s[1]])
        slot_active[0] = list(range(NSLOT))
        slot_next[0] = 0

        stop_at("attn")
        bar = P.barrier()
        for tb, (t0, n) in enumerate(TBS):
            P.dma("sync", lambda e, t0=t0, n=n: e.dma_start(out=xv[:, :, t0:t0 + n], in_=xs3[:, :, t0:t0 + n]),
                  writes=[XB[k][tb] for k in range(8)], waits=[t for t in bar + spill if t is not None])
        so = [wload(w_out[k * 128:(k + 1) * 128, :], "flat") for k in range(8)]
        for tb, (t0, n) in enumerate(TBS):
            for mo in range(8):
                bank = 4 + (cnt["stg"] % 2)
                cnt["stg"] += 1
                mm([(PB[bank][:, 0:n], so[k][0][:, mo * 128:(mo + 1) * 128], mv[:, k, t0:t0 + n], k == 0, k == 7)
                    for k in range(8)],
                   [so[k][1] for k in range(8)] + [MB[k][tb] for k in range(8)], [PBB[bank]])
                for (lo, ln_, seq) in segs(tb):
                    P.op("vector", lambda e, bank=bank, mo=mo, t0=t0, lo=lo, ln_=ln_, seq=seq: e.scalar_tensor_tensor(
                        out=xv[:, mo, t0 + lo:t0 + lo + ln_], in0=PB[bank][:, lo:lo + ln_],
                        scalar=mod_ap(5, mo, seq), in1=xv[:, mo, t0 + lo:t0 + lo + ln_],
                        op0=ALU.mult, op1=ALU.add),
                        reads=[PBB[bank], B_mod[5], XB[mo][tb]], writes=[XB[mo][tb]])
                if tb > 0 and mo == 3:
                    norm_tb(tb - 1, 2, 6)
        norm_tb(4, 2, 6)

        stop_at("wout")
        ffn(w2i, w2o, 4, B_der[4], after_tb=lambda tb: norm_tb(tb, None, None, final=True))

        P.stopped = False
        P.wait_only("sync", out_toks + P.barrier())
        block = ctx.enter_context(nc.Block())
        P.emit(block)
    return nc


_NC_CACHE = {}


def _prep(x_prompt, x_sample, cache_k, cache_v, cache_logf, state_conv, c_prompt, c_sample,
          w_ada, b_ada, g_ffn1, w_ffn1_in, w_ffn1_out, g_mix, w_in, b_f, conv_w, w_out,
          g_ffn2, w_ffn2_in, w_ffn2_out, g_final):
    f = lambda a: np.ascontiguousarray(np.asarray(a, dtype=np.float32))
    x_prompt, x_sample = f(x_prompt), f(x_sample)
    cache_k, cache_v, cache_logf, state_conv = f(cache_k), f(cache_v), f(cache_logf), f(state_conv)
    c_prompt, c_sample = f(c_prompt), f(c_sample)
    def pk(vec):
        return vec.reshape(8, 128).T
    b_ada3 = np.repeat(f(b_ada)[0].reshape(72, 128).T[:, :, None], 3, axis=2).reshape(128, 216)
    gains = np.stack([pk(f(g_ffn1)[0]), pk(f(g_mix)[0]), pk(f(g_ffn2)[0]), pk(f(g_final))], axis=1)
    g3 = np.repeat(gains[:, :, :, None], 3, axis=3).reshape(128, 96)
    conv_wT = np.stack([pk(f(conv_w)[0, j]) for j in range(3)], axis=2).reshape(128, 24)
    shared = {
        "w_ada": f(w_ada)[0], "b_ada3": f(b_ada3), "g3": f(g3),
        "w1i": f(w_ffn1_in)[0], "w1o": f(w_ffn1_out)[0], "w_in": f(w_in)[0], "w_out": f(w_out)[0],
        "w2i": f(w_ffn2_in)[0], "w2o": f(w_ffn2_out)[0],
        "b_f": f(b_f)[0].reshape(16, 1), "conv_wT": f(conv_wT),
    }
    in_maps = []
    for c in range(NCORES):
        s0, s1 = 2 * c, 2 * c + 1
        xt = np.concatenate([x_prompt[c], x_sample[s0], x_sample[s1]], axis=0)
        cs = np.stack([c_prompt[c], c_sample[s0], c_sample[s1]], axis=0)
        cT = cs.reshape(3, 8, 128).transpose(2, 1, 0).reshape(128, 24)
        kcT = cache_k[0, s0:s1 + 1].transpose(0, 1, 3, 2)
        vcc = cache_v[0, s0:s1 + 1].transpose(0, 2, 1, 3).reshape(2, PAST, D)
        sc = state_conv[0, s0:s1 + 1]
        scT = sc.reshape(2, 2, 8, 128).transpose(3, 2, 0, 1).reshape(128, 32)
        m = dict(shared)
        m.update({"xT": f(xt.T), "cT": f(cT), "kcT": f(kcT), "vc": f(vcc),
                  "lfc": f(cache_logf[0, s0:s1 + 1]), "scT": f(scT)})
        in_maps.append(m)
    return in_maps


def _assemble(R):
    y_prompt = np.empty((8, NPR, D), np.float32)
    y_sample = np.empty((16, SQ, D), np.float32)
    k_prompt = np.empty((1, 8, 16, NPR, 64), np.float32)
    v_prompt = np.empty((1, 8, 16, NPR, 64), np.float32)
    logf_prompt = np.empty((1, 8, 16, NPR), np.float32)
    conv_prompt = np.empty((1, 8, 2, D), np.float32)
    k_sample = np.empty((1, 16, 16, SQ, 64), np.float32)
    v_sample = np.empty((1, 16, 16, SQ, 64), np.float32)
    logf_sample = np.empty((1, 16, 16, SQ), np.float32)
    conv_sample = np.empty((1, 16, 2, D), np.float32)
    for c in range(NCORES):
        r = R[c]
        y = np.asarray(r["yT"]).T
        kk = np.asarray(r["kT"]).reshape(16, 64, NT).transpose(0, 2, 1)
        vv = np.asarray(r["vo"]).reshape(NT, 16, 64).transpose(1, 0, 2)
        lf = np.asarray(r["lf"])
        cv = np.asarray(r["cvT"]).reshape(128, 8, 3, 2).transpose(2, 3, 1, 0).reshape(3, 2, D)
        y_prompt[c] = y[:NPR]
        k_prompt[0, c] = kk[:, :NPR]
        v_prompt[0, c] = vv[:, :NPR]
        logf_prompt[0, c] = lf[:, :NPR]
        conv_prompt[0, c] = cv[0]
        for s in range(2):
            sl = slice(NPR + SQ * s, NPR + SQ * (s + 1))
            y_sample[2 * c + s] = y[sl]
            k_sample[0, 2 * c + s] = kk[:, sl]
            v_sample[0, 2 * c + s] = vv[:, sl]
            logf_sample[0, 2 * c + s] = lf[:, sl]
            conv_sample[0, 2 * c + s] = cv[1 + s]
    return (y_prompt, y_sample, k_prompt, v_prompt, logf_prompt, conv_prompt,
            k_sample, v_sample, logf_sample, conv_sample)


def kernel(**inputs):
    in_maps = _prep(**inputs)
    if "nc" not in _NC_CACHE:
        _NC_CACHE["nc"] = build_program()
    nc = _NC_CACHE["nc"]
    res = run_bass_kernel_spmd(nc, in_maps, core_ids=list(range(NCORES)))
    return _assemble(res.results)
```

```python
import struct
import numpy as np
import concourse.bass as bass
import concourse.mybir as mybir
from concourse.bass_utils import run_bass_kernel_spmd
from contextlib import ExitStack

F32 = mybir.dt.float32
BF16 = mybir.dt.bfloat16
AF = mybir.ActivationFunctionType
ALU = mybir.AluOpType

NCORES = 8
D = 1024
NT = 2176
NPR = 2048
SQ = 64
PAST = 1024
DFF = 2816
NJ = 22
OFF_Q, OFF_K, OFF_V, OFF_F = 0, 1024, 2048, 3072
OFF_B = OFF_F + 16
OFF_C = OFF_B + 1024
OFF_X = OFF_C + 1024
OFF_GA = OFF_X + 1024
OFF_GC = OFF_GA + 1024
IN_COLS = OFF_GC + 1024
EPS = 1e-6
TBS = [(0, 512), (512, 512), (1024, 512), (1536, 512), (2048, 128)]
NSLOT = 16
PROBE = ""
MASKVAL = -30000.0


def segs(tb):
    if tb < 4:
        return [(0, 512, 0)]
    return [(0, 64, 1), (64, 64, 2)]


ENGS = ["tensor", "vector", "scalar", "gpsimd", "sync"]


class Sem:
    def __init__(self, P, name, step):
        self.h = P.ctx.enter_context(P.nc.semaphore(name))
        self.count = 0
        self.step = step
        self.name = name


class Buf:
    __slots__ = ("name", "w", "r", "excl")

    def __init__(self, name="", excl=False):
        self.name = name
        self.w = None
        self.r = {}
        self.excl = excl


class Prog:
    def __init__(self, nc, ctx):
        self.nc = nc
        self.ctx = ctx
        self.ops = {e: [] for e in ENGS}
        self.esem = {}
        for e in ["tensor", "vector", "scalar", "gpsimd"]:
            self.esem[e] = Sem(self, "c_" + e, 1)
        self.dsems = {}
        self.rr = {}
        self.stopped = False

    def sem(self, name, step=16):
        return Sem(self, name, step)

    def sb(self, name, shape, dt):
        return self.ctx.enter_context(self.nc.sbuf_tensor(name, list(shape), dt))

    def ps(self, name, shape, dt=F32):
        return self.ctx.enter_context(self.nc.psum_tensor(name, list(shape), dt))

    def _collect(self, eng_sem, reads, writes, waits, is_pe):
        need = {}

        def add(tok, raw):
            if tok is None:
                return
            s, v = tok
            if s is eng_sem and is_pe:
                return
            if need.get(s, 0) < v:
                need[s] = v

        for b in reads:
            add(b.w, True)
        for b in writes:
            add(b.w, False)
            for s, v in b.r.items():
                add((s, v), False)
        for t in waits:
            add(t, True)
        return need

    def _update(self, tok, reads, writes):
        s, v = tok
        for b in reads:
            if b.r.get(s, 0) < v:
                b.r[s] = v
        for b in writes:
            b.w = tok
            b.r = {}

    def op(self, eng, fn, reads=(), writes=(), waits=()):
        if self.stopped:
            return None
        ex = [b for b in reads if b.excl]
        if ex:
            writes = list(writes) + [b for b in ex if b not in writes]
        s = self.esem[eng]
        need = self._collect(s, reads, writes, waits, eng == "tensor")
        s.count += 1
        tok = (s, s.count)
        self._update(tok, reads, writes)
        self.ops[eng].append((fn, need, s))
        return tok

    def dma(self, eng, fn, reads=(), writes=(), waits=(), sem=None):
        if self.stopped:
            return None
        if sem is None:
            lst = self.dsems.setdefault(eng, [])
            if len(lst) < 8:
                lst.append(Sem(self, "d_%s%d" % (eng, len(lst)), 16))
            i = self.rr.get(eng, 0)
            self.rr[eng] = i + 1
            sem = lst[i % len(lst)]
        need = self._collect(None, reads, writes, waits, False)
        if sem.count > 0:
            if need.get(sem, 0) < sem.count:
                need[sem] = sem.count
        sem.count += 16
        tok = (sem, sem.count)
        self._update(tok, reads, writes)
        self.ops[eng].append((fn, need, sem))
        return tok

    def barrier(self):
        return [(s, s.count) for s in self.esem.values() if s.count > 0]

    def wait_only(self, eng, toks):
        need = {}
        for s, v in [t for t in toks if t is not None]:
            if need.get(s, 0) < v:
                need[s] = v
        self.ops[eng].append((None, need, None))

    def emit(self, block):
        P = self

        def make(ename):
            def body(eng):
                waited = {}
                for fn, need, s in P.ops[ename]:
                    for sm, val in need.items():
                        if waited.get(sm.name, 0) >= val:
                            continue
                        eng.wait_ge(sm.h, val)
                        waited[sm.name] = val
                    if fn is None:
                        continue
                    ins = fn(eng)
                    if s is not None:
                        ins.then_inc(s.h, s.step)
            return body

        for e in ENGS:
            if P.ops[e]:
                getattr(block, e)(make(e))


class _Stop(Exception):
    pass


def build_program(stop=None):
    nc = bass.Bass("TRN2", target_bir_lowering=False)

    def din(name, shape):
        return nc.dram_tensor(name, list(shape), F32, kind="ExternalInput").ap()

    def dout(name, shape):
        return nc.dram_tensor(name, list(shape), F32, kind="ExternalOutput").ap()

    xT = din("xT", [D, NT])
    cT = din("cT", [128, 24])
    w_ada = din("w_ada", [D, 9 * D])
    b_ada3 = din("b_ada3", [128, 216])
    g3 = din("g3", [128, 96])
    w1i = din("w1i", [D, 2 * DFF])
    w1o = din("w1o", [DFF, D])
    w_in = din("w_in", [D, IN_COLS])
    w_out = din("w_out", [D, D])
    w2i = din("w2i", [D, 2 * DFF])
    w2o = din("w2o", [DFF, D])
    b_f = din("b_f", [16, 1])
    conv_wT = din("conv_wT", [128, 24])
    kcT = din("kcT", [2, 16, 64, PAST])
    vc = din("vc", [2, PAST, D])
    lfc = din("lfc", [2, 16, PAST])
    scT = din("scT", [128, 32])
    yT = dout("yT", [D, NT])
    kTo = dout("kT", [D, NT])
    vo = dout("vo", [NT, D])
    lfo = dout("lf", [16, NT])
    cvo = dout("cvT", [128, 48])
    xs = nc.dram_tensor("xs", [D, NT], F32, kind="Internal").ap()

    with ExitStack() as ctx:
        P = Prog(nc, ctx)
        CH = 2184
        RX = P.sb("RX", [128, 8 * CH], F32)
        xv = RX[:, :].rearrange("p (k t) -> p k t", k=8)[:, :, 0:NT]
        H = P.sb("H", [128, 8, NT], BF16)
        RG = P.sb("RG", [128, 11 * NT], BF16)
        gv = RG[:, :].rearrange("p (j t) -> p j t", j=11)
        WSALL = P.sb("wsall", [128, NSLOT * 1024], BF16)
        WS = [WSALL[:, i * 1024:(i + 1) * 1024] for i in range(NSLOT)]
        MOD = P.sb("MOD", [128, 216], F32)
        DER = P.sb("DER", [128, 120], F32)
        BADA = P.sb("BADA", [128, 216], F32)
        G3 = P.sb("G3", [128, 96], F32)
        ones_bf = P.sb("ones_bf", [128, 128], BF16)
        ident_f = P.sb("ident_f", [128, 128], F32)
        ident_bf = P.sb("ident_bf", [128, 128], BF16)
        maskneg = P.sb("maskneg", [128, 128], BF16)
        zer_bf = P.sb("zer_bf", [128, 128], BF16)
        mask2 = P.sb("mask2", [128, 128], BF16)
        SCR = P.sb("scr", [128, 2048], BF16)
        SQB = [SCR[:, i * 512:(i + 1) * 512] for i in range(2)]
        SAB = [SCR[:, (2 + i) * 512:(3 + i) * 512] for i in range(2)]
        VST = [SCR[:, s_ * 1024:(s_ + 1) * 1024].rearrange("p (i f) -> p i f", i=8) for s_ in range(2)]
        TMP = [P.sb("tmp%d" % i, [128, 512], F32) for i in range(2)]
        LNMS = P.sb("lnms", [128, 512], F32)
        STG = [P.sb("stg%d" % i, [128, 512], F32) for i in range(2)]
        CF = P.sb("CF", [128, 24], F32)
        CSB = P.sb("CSB", [128, 24], BF16)
        CW = P.sb("CW", [128, 24], F32)
        SCT = P.sb("SCT", [128, 32], F32)
        CVO = P.sb("CVO", [128, 48], F32)
        BFt = P.sb("BFt", [16, 1], F32)
        NF = P.sb("NF", [128, 256], F32)
        NFS = P.sb("NFS", [128, 2 * 8 * 16 + 16], F32)
        FNEW = P.sb("FNEW", [16, 128], F32)
        PB = [P.ps("pb%d" % i, [128, 512]) for i in range(8)]
        PBB = [Buf("pb%d" % i, excl=True) for i in range(8)]


        def rx_f32(k, off, n):
            return RX[:, k * CH + off:k * CH + off + n]

        def rx_bf(k, off, n):
            return RX[:, k * CH:(k + 1) * CH].bitcast(BF16)[:, off:off + n]

        q_aug = rx_bf(0, 0, 2 * NT).rearrange("p (h t) -> p h t", h=2)
        k_aug = rx_bf(1, 0, 2 * NT).rearrange("p (h t) -> p h t", h=2)
        LF = rx_f32(2, 0, NT)
        FP = rx_f32(3, 0, NPR)
        ONES16 = rx_f32(4, 0, NPR)
        v_ext = rx_bf(2, 0, 17 * 256).rearrange("p (i h d) -> p i h d", i=17, h=2)
        kc_aug = rx_bf(3, 0, 4096).rearrange("p (s h t) -> p s h t", s=2, h=2)
        vc_ext = rx_bf(4, 0, 4096).rearrange("p (s i h d) -> p s i h d", s=2, i=8, h=2)
        CS = rx_f32(5, 0, 2 * 1088).rearrange("p (s t) -> p s t", s=2)
        FS = rx_f32(6, 0, 2 * 1088).rearrange("p (s t) -> p s t", s=2)
        CT16 = rx_bf(7, 0, NT)
        T1 = rx_f32(7, 1088, 512)
        T2 = rx_f32(7, 1088 + 512, 512)
        mv = RG[:, 0:8 * NT].rearrange("p (k t) -> p k t", k=8)
        SGA = RG[:, 8 * NT:10 * NT].bitcast(F32)
        PT = [RG[:, 10 * NT + i * 512:10 * NT + (i + 1) * 512] for i in range(4)]
        LND = LNMS
        RD = TMP[0]
        TT = TMP[1]

        modv = MOD[:, :].rearrange("p (j s) -> p j s", s=3)
        derv = DER[:, :].rearrange("p (a k s) -> p a k s", a=5, s=3)
        g3v = G3[:, :].rearrange("p (a k s) -> p a k s", a=4, s=3)

        def mod_ap(kind, k, seq):
            return modv[:, kind * 8 + k, seq:seq + 1]

        def der_ap(a, k, seq):
            return derv[:, a, k, seq:seq + 1]

        XB = [[Buf("x%d_%d" % (k, t)) for t in range(5)] for k in range(8)]
        HB = [[Buf("h%d_%d" % (k, t)) for t in range(5)] for k in range(8)]
        GB = [[Buf("g%d_%d" % (j, t)) for t in range(5)] for j in range(11)]
        MB = [[Buf("m%d_%d" % (k, t)) for t in range(5)] for k in range(8)]
        WSB = [Buf("ws%d" % i) for i in range(NSLOT)]
        WSS = [P.sem("wsd%d" % i, 16) for i in range(NSLOT)]
        B_const = Buf("const")
        B_mod = [Buf("mod%d" % i) for i in range(9)]
        B_der = [Buf("der%d" % i) for i in range(5)]
        B_sq = [Buf() for _ in range(2)]
        B_sa = [Buf() for _ in range(2)]
        B_tmp = [Buf() for _ in range(2)]
        B_ln = Buf()
        B_stg = [Buf() for _ in range(2)]
        B_small = Buf("small")
        B_csb = Buf()
        B_cvo = Buf()

        slot_next = [0]
        slot_active = [list(range(NSLOT))]

        def wload(src_ap, view):
            act_ = slot_active[0]
            i = act_[slot_next[0] % len(act_)]
            slot_next[0] += 1
            t = WS[i]
            if view == "k128":
                dst = t.rearrange("p (k n) -> p k n", k=8)
            elif view == "k16":
                dst = t[:, 0:128].rearrange("p (k n) -> p k n", k=8)
            else:
                dst = t
            P.dma("gpsimd", lambda e, d=dst, s=src_ap: e.dma_start(out=d, in_=s),
                  writes=[WSB[i]], sem=WSS[i])
            return dst, WSB[i]

        def wcols(w, c0, n):
            return w.rearrange("(k p) n -> p k n", p=128)[:, :, c0:c0 + n]

        def mm(items, reads, writes):
            def fn(e, items=items):
                ins = None
                for (o, l, r, st, sp) in items:
                    ins = e.matmul(o, lhsT=l, rhs=r, start=st, stop=sp)
                return ins
            return P.op("tensor", fn, reads=reads, writes=writes)

        def act(out, in_, func, reads, writes, bias=None, scale=None):
            kw = {}
            if bias is not None:
                kw["bias"] = bias
            if scale is not None:
                kw["scale"] = scale
            return P.op("scalar", lambda e: e.activation(out=out, in_=in_, func=func, **kw),
                        reads=reads, writes=writes)

        out_toks = []
        sync_rr = [0]

        def stop_at(name):
            if stop == name:
                P.stopped = True

        P.op("gpsimd", lambda e: e.memset(ones_bf[:], 1.0), writes=[B_const])
        P.op("gpsimd", lambda e: e.memset(zer_bf[:], 0.0), writes=[B_const])
        P.op("gpsimd", lambda e: e.memset(ident_f[:], 0.0), writes=[B_const])
        P.op("gpsimd", lambda e: e.affine_select(out=ident_f[:], in_=ident_f[:], pattern=[[-1, 128]],
                                                  compare_op=ALU.not_equal, fill=1.0, base=0,
                                                  channel_multiplier=1), reads=[B_const], writes=[B_const])
        P.op("gpsimd", lambda e: e.tensor_copy(out=ident_bf[:], in_=ident_f[:]), reads=[B_const], writes=[B_const])
        P.op("gpsimd", lambda e: e.affine_select(out=maskneg[:], in_=zer_bf[:], pattern=[[1, 128]],
                                                  compare_op=ALU.is_ge, fill=MASKVAL, base=0,
                                                  channel_multiplier=-1), reads=[B_const], writes=[B_const])
        P.op("gpsimd", lambda e: e.tensor_copy(out=mask2[:, 0:64], in_=maskneg[:, 0:64]), reads=[B_const], writes=[B_const])
        P.op("gpsimd", lambda e: e.affine_select(out=mask2[:, 64:128], in_=zer_bf[:, 0:64], pattern=[[1, 64]],
                                                  compare_op=ALU.is_ge, fill=MASKVAL, base=64,
                                                  channel_multiplier=-1), reads=[B_const], writes=[B_const])
        P.op("gpsimd", lambda e: e.affine_select(out=mask2[:, 64:128], in_=mask2[:, 64:128], pattern=[[0, 64]],
                                                  compare_op=ALU.is_ge, fill=MASKVAL, base=-64,
                                                  channel_multiplier=1), reads=[B_const], writes=[B_const])
        P.dma("sync", lambda e: e.dma_start(out=CF[:], in_=cT), writes=[B_small])
        P.dma("sync", lambda e: e.dma_start(out=BADA[:], in_=b_ada3), writes=[B_small])
        P.dma("sync", lambda e: e.dma_start(out=G3[:], in_=g3), writes=[B_small])
        P.dma("sync", lambda e: e.dma_start(out=CW[:], in_=conv_wT), writes=[B_small])
        P.dma("sync", lambda e: e.dma_start(out=SCT[:], in_=scT), writes=[B_small])
        P.dma("sync", lambda e: e.dma_start(out=BFt[:], in_=b_f), writes=[B_small])
        xT3 = xT.rearrange("(k p) t -> p k t", p=128)
        for tb, (t0, n) in enumerate(TBS):
            P.dma("sync", lambda e, t0=t0, n=n: e.dma_start(out=xv[:, :, t0:t0 + n], in_=xT3[:, :, t0:t0 + n]),
                  writes=[XB[k][tb] for k in range(8)])
        act(CSB[:], CF[:], AF.Silu, [B_small], [B_csb])

        stop_at("setup")
        mod_state = {"kind": 0}

        def ada_kind(kind):
            bank = 6 + (kind % 2)
            for jj in range(8):
                j = kind * 8 + jj
                sl, sb_ = wload(wcols(w_ada, j * 128, 128), "k128")
                items = [(PB[bank][:, jj * 3:jj * 3 + 3], sl[:, k, :], CSB[:, k * 3:k * 3 + 3], k == 0, k == 7)
                         for k in range(8)]
                mm(items, [sb_, B_csb], [PBB[bank]])
            P.op("vector", lambda e, kind=kind, bank=bank: e.tensor_tensor(
                out=MOD[:, kind * 24:(kind + 1) * 24], in0=PB[bank][:, 0:24],
                in1=BADA[:, kind * 24:(kind + 1) * 24], op=ALU.add),
                reads=[PBB[bank], B_small], writes=[B_mod[kind]])
            if kind in (1, 4, 7):
                a = {1: 0, 4: 1, 7: 2}[kind]
                P.op("vector", lambda e, kind=kind, a=a: e.scalar_tensor_tensor(
                    out=DER[:, a * 24:(a + 1) * 24], in0=MOD[:, kind * 24:(kind + 1) * 24], scalar=1.0,
                    in1=G3[:, a * 24:(a + 1) * 24], op0=ALU.add, op1=ALU.mult),
                    reads=[B_mod[kind], B_small], writes=[B_der[a]])
            if kind in (2, 8):
                a = {2: 3, 8: 4}[kind]
                P.op("vector", lambda e, kind=kind, a=a: e.tensor_scalar(
                    out=DER[:, a * 24:(a + 1) * 24], in0=MOD[:, kind * 24:(kind + 1) * 24],
                    scalar1=0.5, scalar2=None, op0=ALU.mult),
                    reads=[B_mod[kind]], writes=[B_der[a]])


        cnt = {"sq": 0, "tmp": 0, "stat": 0, "stg": 0, "sa": 0, "stgm": 0, "fin": 0}

        def norm_tb(tb, a_idx, sh_kind, final=False, phase="both", bank=None):
            t0, n = TBS[tb]
            if bank is None:
                bank = 6 + (cnt["stat"] % 2)
                cnt["stat"] += 1
            if phase in ("both", "stats"):
                for k in range(8):
                    i = cnt["sq"] % 2
                    cnt["sq"] += 1
                    act(SQB[i][:, 0:n], xv[:, k, t0:t0 + n], AF.Square, [XB[k][tb]], [B_sq[i]])
                    mm([(PB[bank][:, 0:n], ones_bf[:], SQB[i][:, 0:n], k == 0, k == 7)],
                       [B_sq[i], B_const], [PBB[bank]])
                act(LNMS[:, 0:n], PB[bank][:, 0:n], AF.Ln, [PBB[bank]], [B_ln], bias=EPS, scale=1.0 / D)
                act(PB[bank][:, 0:n], LNMS[:, 0:n], AF.Exp, [B_ln], [PBB[bank]], scale=-0.5)
            if phase == "stats":
                return
            for k in range(8):
                if final:
                    i = cnt["fin"] % 4
                    cnt["fin"] += 1
                    stg_ = (STG + TMP)[i]
                    bst_ = (B_stg + B_tmp)[i]
                    P.op("vector", lambda e, k=k, stg_=stg_: e.scalar_tensor_tensor(
                        out=stg_[:, 0:n], in0=xv[:, k, t0:t0 + n], scalar=g3v[:, 3, k, 0:1],
                        in1=PB[bank][:, 0:n], op0=ALU.mult, op1=ALU.mult),
                        reads=[XB[k][tb], PBB[bank], B_small], writes=[bst_])
                    out_toks.append(P.dma("sync", lambda e, k=k, stg_=stg_: e.dma_start(
                        out=yT[k * 128:(k + 1) * 128, t0:t0 + n], in_=stg_[:, 0:n]), reads=[bst_]))
                    continue
                i = cnt["tmp"] % 2
                cnt["tmp"] += 1
                for (lo, ln_, seq) in segs(tb):
                    P.op("vector", lambda e, k=k, i=i, lo=lo, ln_=ln_, seq=seq: e.scalar_tensor_tensor(
                        out=TMP[i][:, lo:lo + ln_], in0=xv[:, k, t0 + lo:t0 + lo + ln_],
                        scalar=der_ap(a_idx, k, seq), in1=PB[bank][:, lo:lo + ln_],
                        op0=ALU.mult, op1=ALU.mult),
                        reads=[XB[k][tb], PBB[bank], B_der[a_idx]], writes=[B_tmp[i]])
                for (lo, ln_, seq) in segs(tb):
                    act(H[:, k, t0 + lo:t0 + lo + ln_], TMP[i][:, lo:lo + ln_], AF.Identity,
                        [B_tmp[i], B_mod[sh_kind]], [HB[k][tb]], bias=mod_ap(sh_kind, k, seq), scale=1.0)

        stop_at("ada01")
        for tb in range(5):
            norm_tb(tb, 0, 0, phase="stats", bank=tb)
        ada_kind(0)
        ada_kind(1)
        for tb in range(5):
            norm_tb(tb, 0, 0, phase="mod", bank=tb)
        stop_at("norm1")

        def ffn(w_i, w_o, gh_idx, gate_buf, between_j=None, after_tb=None):
            for half in range(2):
                for jj in range(11):
                    j = half * 11 + jj
                    sA, bA = wload(wcols(w_i, j * 128, 128), "k128")
                    sBv, bB = wload(wcols(w_i, DFF + j * 128, 128), "k128")
                    for tb, (t0, n) in enumerate(TBS):
                        ia = cnt["sa"] % 2
                        cnt["sa"] += 1
                        ba, bb = ia, 2 + ia
                        hb = [HB[k][tb] for k in range(8)]
                        mm([(PB[ba][:, 0:n], sA[:, k, :], H[:, k, t0:t0 + n], k == 0, k == 7) for k in range(8)],
                           [bA] + hb, [PBB[ba]])
                        mm([(PB[bb][:, 0:n], sBv[:, k, :], H[:, k, t0:t0 + n], k == 0, k == 7) for k in range(8)],
                           [bB] + hb, [PBB[bb]])
                        act(SAB[ia][:, 0:n], PB[ba][:, 0:n], AF.Silu, [PBB[ba]], [B_sa[ia]])
                        P.op("vector", lambda e, ia=ia, bb=bb, jj=jj, t0=t0, n=n: e.tensor_tensor(
                            out=gv[:, jj, t0:t0 + n], in0=PB[bb][:, 0:n], in1=SAB[ia][:, 0:n], op=ALU.mult),
                            reads=[PBB[bb], B_sa[ia]], writes=[GB[jj][tb]])
                    if between_j is not None:
                        between_j(half, jj)
                so = []
                for jj in range(11):
                    j = half * 11 + jj
                    so.append(wload(w_o[j * 128:(j + 1) * 128, :], "flat"))
                for tb, (t0, n) in enumerate(TBS):
                    for mo in range(8):
                        bank = 4 + (cnt["stg"] % 2)
                        cnt["stg"] += 1
                        mm([(PB[bank][:, 0:n], so[jj][0][:, mo * 128:(mo + 1) * 128], gv[:, jj, t0:t0 + n],
                             jj == 0, jj == 10) for jj in range(11)],
                           [so[jj][1] for jj in range(11)] + [GB[jj][tb] for jj in range(11)], [PBB[bank]])
                        for (lo, ln_, seq) in segs(tb):
                            P.op("vector", lambda e, bank=bank, mo=mo, t0=t0, lo=lo, ln_=ln_, seq=seq: e.scalar_tensor_tensor(
                                out=xv[:, mo, t0 + lo:t0 + lo + ln_], in0=PB[bank][:, lo:lo + ln_],
                                scalar=der_ap(gh_idx, mo, seq), in1=xv[:, mo, t0 + lo:t0 + lo + ln_],
                                op0=ALU.mult, op1=ALU.add),
                                reads=[PBB[bank], gate_buf, XB[mo][tb]], writes=[XB[mo][tb]])
                        if half == 1 and after_tb is not None and tb > 0 and mo == 3:
                            after_tb(tb - 1)
                if half == 1 and after_tb is not None:
                    after_tb(4)

        def ada_between(half, jj):
            if half == 0 and jj < 7:
                ada_kind(2 + jj)

        xs3 = xs.rearrange("(k p) t -> p k t", p=128)
        spill = []

        def after_ffn1(tb):
            norm_tb(tb, 1, 3)
            t0, n = TBS[tb]
            spill.append(P.dma("sync", lambda e: e.dma_start(out=xs3[:, :, t0:t0 + n], in_=xv[:, :, t0:t0 + n]),
                               reads=[XB[k][tb] for k in range(8)]))

        ffn(w1i, w1o, 3, B_der[3], between_j=ada_between, after_tb=after_ffn1)

        stop_at("ffn1")
        class _AllSpill:
            pass
        def seeded(k, name=""):
            b = Buf(name)
            for tk in spill:
                if tk is not None:
                    b.r[tk[0]] = max(b.r.get(tk[0], 0), tk[1])
            return b

        B_q = [[seeded(0) for _ in range(5)] for _ in range(2)]
        B_qrow = [seeded(0) for _ in range(2)]
        B_k = [[seeded(1) for _ in range(5)] for _ in range(2)]
        B_krow = seeded(1)
        B_lf = seeded(2, "lf")
        B_fp = seeded(3, "fp")
        B_o16 = seeded(4, "o16")
        B_v = [seeded(2) for _ in range(5)]
        B_vones = seeded(2)
        B_kc = [[seeded(3) for _ in range(2)] for _ in range(2)]
        B_kcrow = seeded(3)
        B_vc = [[seeded(4) for _ in range(2)] for _ in range(2)]
        B_vcones = seeded(4)
        B_cs = seeded(5)
        B_fs = seeded(6)
        B_ct = seeded(7)
        B_t1 = seeded(7)
        B_t2 = seeded(7)
        B_sga = [Buf() for _ in range(5)]
        B_pt = [Buf() for _ in range(4)]
        B_nf = Buf()
        B_nfs = Buf()
        B_fnew = Buf()

        sF, bF = wload(wcols(w_in, OFF_F, 16), "k16")
        for tb, (t0, n) in enumerate(TBS):
            bank = tb % 2
            mm([(PB[bank][0:16, 0:n], sF[:, k, :], H[:, k, t0:t0 + n], k == 0, k == 7) for k in range(8)],
               [bF] + [HB[k][tb] for k in range(8)], [PBB[bank]])
            P.op("vector", lambda e, bank=bank, n=n: e.tensor_scalar(
                out=T1[0:16, 0:n], in0=PB[bank][0:16, 0:n], scalar1=BFt[0:16, 0:1], scalar2=None, op0=ALU.add),
                reads=[PBB[bank], B_small], writes=[B_t1])
            act(T2[0:16, 0:n], T1[0:16, 0:n], AF.Abs, [B_t1], [B_t2])
            act(T2[0:16, 0:n], T2[0:16, 0:n], AF.Exp, [B_t2], [B_t2], scale=-1.0)
            act(T2[0:16, 0:n], T2[0:16, 0:n], AF.Ln, [B_t2], [B_t2], bias=1.0, scale=1.0)
            P.op("vector", lambda e, t0=t0, n=n: e.scalar_tensor_tensor(
                out=LF[0:16, t0:t0 + n], in0=T1[0:16, 0:n], scalar=0.0, in1=T2[0:16, 0:n],
                op0=ALU.min, op1=ALU.subtract),
                reads=[B_t1, B_t2], writes=[B_lf])
        out_toks.append(P.dma("sync", lambda e: e.dma_start(out=lfo, in_=LF[0:16, :]), reads=[B_lf]))
        P.op("vector", lambda e: e.memset(ONES16[0:16, :], 1.0), writes=[B_o16])
        P.op("vector", lambda e: e.tensor_tensor_scan(
            out=FP[0:16, :], data0=ONES16[0:16, :], data1=LF[0:16, 0:NPR], initial=0.0,
            op0=ALU.mult, op1=ALU.add), reads=[B_o16, B_lf], writes=[B_fp])
        for s in range(2):
            P.dma("sync", lambda e, s=s: e.dma_start(out=CS[0:16, s, 0:PAST], in_=lfc[s]), writes=[B_cs])
        for s in range(2):
            P.op("vector", lambda e, s=s: e.tensor_copy(out=CS[0:16, s, PAST:PAST + SQ],
                                                        in_=LF[0:16, NPR + SQ * s:NPR + SQ * (s + 1)]),
                 reads=[B_lf], writes=[B_cs])
        for s in range(2):
            P.op("vector", lambda e, s=s: e.tensor_tensor_scan(
                out=FS[0:16, s, :], data0=ONES16[0:16, 0:PAST + SQ], data1=CS[0:16, s, :], initial=0.0,
                op0=ALU.mult, op1=ALU.add), reads=[B_o16, B_cs], writes=[B_fs])
        P.op("vector", lambda e: e.tensor_copy(out=CT16[0:16, 0:NPR], in_=FP[0:16, :]), reads=[B_fp], writes=[B_ct])
        for s in range(2):
            P.op("vector", lambda e, s=s: e.tensor_copy(out=CT16[0:16, NPR + SQ * s:NPR + SQ * (s + 1)],
                                                        in_=FS[0:16, s, PAST:PAST + SQ]),
                 reads=[B_fs], writes=[B_ct])
            P.op("vector", lambda e, s=s: e.tensor_copy(out=FNEW[0:16, SQ * s:SQ * (s + 1)],
                                                        in_=FS[0:16, s, PAST:PAST + SQ]),
                 reads=[B_fs], writes=[B_fnew])
        stop_at("fgate")
        def alias_seed(bufs, srcs):
            toks = {}
            for sbuf in srcs:
                if sbuf.w is not None:
                    s_, v_ = sbuf.w
                    toks[s_] = max(toks.get(s_, 0), v_)
                for s_, v_ in sbuf.r.items():
                    toks[s_] = max(toks.get(s_, 0), v_)
            for b_ in bufs:
                for s_, v_ in toks.items():
                    b_.r[s_] = max(b_.r.get(s_, 0), v_)

        U = rx_f32(5, 0, 2182)
        CY = rx_f32(6, 0, 2180)
        k_augs = [k_aug, WSALL[:, 6 * 1024:6 * 1024 + 2 * NT].rearrange("p (h t) -> p h t", h=2)]
        v_exts = [v_ext, WSALL[:, 11 * 1024:11 * 1024 + 17 * 256].rearrange("p (i h d) -> p i h d", i=17, h=2)]
        B_u = Buf("u")
        B_cy = Buf("cy")
        alias_seed([B_u], [B_cs])
        alias_seed([B_cy], [B_fs])
        B_ks = [B_k, [[Buf() for _ in range(5)] for _ in range(2)]]
        B_krows = [B_krow, Buf()]
        B_vs = [B_v, [Buf() for _ in range(5)]]
        B_vones_s = [B_vones, Buf()]
        alias_seed([x_ for l_ in B_ks[1] for x_ in l_] + [B_krows[1]], WSB[6:11])
        alias_seed(B_vs[1] + [B_vones_s[1]], WSB[11:16])
        alias_seed(B_vs[0] + [B_vones_s[0]], [B_lf])
        alias_seed([x_ for l_ in B_kc for x_ in l_] + [B_kcrow], [B_fp])
        alias_seed([x_ for l_ in B_vc for x_ in l_] + [B_vcones], [B_o16])
        slot_active[0] = list(range(5))
        slot_next[0] = 0
        STGM = [STG[0], STG[1], WSALL[:, 5 * 1024:6 * 1024].bitcast(F32)]
        B_stgm = [B_stg[0], B_stg[1], Buf()]
        alias_seed([B_stgm[2]], [WSB[5]])

        ONES2 = struct.unpack("<f", struct.pack("<I", 0x3F803F80))[0]

        def const_memsets():
            for st in range(2):
                P.op("vector", lambda e, st=st: e.memset(k_augs[st][64:65, :, :].bitcast(F32), ONES2),
                     writes=[B_krows[st]])
                P.op("vector", lambda e, st=st: e.memset(v_exts[st][:, :, :, 64:128].bitcast(F32), ONES2),
                     writes=[B_vones_s[st]])
            P.op("vector", lambda e: e.memset(kc_aug[64:65, :, :, :].bitcast(F32), ONES2), writes=[B_kcrow])
            P.op("vector", lambda e: e.memset(vc_ext[:, :, :, :, 64:128].bitcast(F32), ONES2), writes=[B_vcones])
            P.op("vector", lambda e: e.memset(U[:, 0:2], 0.0), writes=[B_u])

        cvv = CVO[:, :].rearrange("p (k s r) -> p k s r", k=8, s=3)
        sctv = SCT[:, :].rearrange("p (k s r) -> p k s r", k=8, s=2)
        UOFF = [2, 2052, 2118]
        CYOFF = [0, 2050, 2116]
        SEQ0 = [0, NPR, NPR + SQ]
        sbank = [0]
        obank = [0]
        pbank = [0]
        ptc = [0]

        def next_s():
            b = sbank[0] % 4
            sbank[0] += 1
            return b

        def next_o():
            b = 4 + (obank[0] % 2)
            obank[0] += 1
            return b

        def next_p():
            b = 6 + (pbank[0] % 2)
            pbank[0] += 1
            return b

        def proj(sw, bw, tb):
            t0, n = TBS[tb]
            b = next_p()
            mm([(PB[b][:, 0:n], sw[:, k, :], H[:, k, t0:t0 + n], k == 0, k == 7) for k in range(8)],
               [bw] + [HB[k][tb] for k in range(8)], [PBB[b]])
            return b

        def proj_g(sw, bw, tb):
            t0, n = TBS[tb]
            b = next_p()
            rd = [bw] + [HB[k][tb] for k in range(8)]
            mm([(PB[b][:, 0:n], sw[:, k, :], H[:, k, t0:t0 + n], k == 0, False) for k in range(4)], rd, [PBB[b]])
            yield
            mm([(PB[b][:, 0:n], sw[:, k, :], H[:, k, t0:t0 + n], False, k == 7) for k in range(4, 8)], rd, [PBB[b]])
            return b

        def sigmoid_inplace(dst, src_ps, rd, wr):
            act(dst, src_ps, AF.Exp, rd, wr, scale=-1.0)
            act(dst, dst, AF.Ln, wr, wr, bias=1.0, scale=1.0)
            act(dst, dst, AF.Exp, wr, wr, scale=-1.0)

        def sigmoid_split(dst, src_ps, rd, wr):
            return [lambda: act(dst, src_ps, AF.Exp, rd, wr, scale=-1.0),
                    lambda: act(dst, dst, AF.Ln, wr, wr, bias=1.0, scale=1.0),
                    lambda: act(dst, dst, AF.Exp, wr, wr, scale=-1.0)]

        mdone = set()

        def conv_items(c):
            items = []
            w = {}

            def ld1():
                w["C"] = wload(wcols(w_in, OFF_C + c * 128, 128), "k128")
                w["X"] = wload(wcols(w_in, OFF_X + c * 128, 128), "k128")
            items.append(ld1)

            def pads():
                for s in range(2):
                    P.op("vector", lambda e, s=s: e.tensor_copy(out=U[:, UOFF[1 + s] - 2:UOFF[1 + s]],
                                                                in_=sctv[:, c, s, :]),
                         reads=[B_small], writes=[B_u])
            items.append(pads)
            for tb, (t0, n) in enumerate(TBS):
                def cgrp(tb=tb, t0=t0, n=n):
                    bc = yield from proj_g(w["C"][0], w["C"][1], tb)
                    P.op("vector", lambda e: e.tensor_copy(out=T1[:, 0:n], in_=PB[bc][:, 0:n]),
                         reads=[PBB[bc]], writes=[B_t1])
                items.append(cgrp)

                def xgrp(tb=tb, t0=t0, n=n):
                    bx = yield from proj_g(w["X"][0], w["X"][1], tb)
                    for (lo, ln_, seq) in segs(tb):
                        u0 = UOFF[seq] + (t0 + lo - SEQ0[seq])
                        P.op("vector", lambda e, lo=lo, ln_=ln_, u0=u0: e.tensor_tensor(
                            out=U[:, u0:u0 + ln_], in0=PB[bx][:, lo:lo + ln_], in1=T1[:, lo:lo + ln_], op=ALU.mult),
                            reads=[PBB[bx], B_t1], writes=[B_u])
                items.append(xgrp)

            def ld2():
                w["B"] = wload(wcols(w_in, OFF_B + c * 128, 128), "k128")
                w["G"] = wload(wcols(w_in, OFF_GC + c * 128, 128), "k128")
            items.append(ld2)

            def taps():
                for s3 in range(3):
                    end = UOFF[s3] + [NPR, SQ, SQ][s3]
                    P.op("vector", lambda e, s3=s3, end=end: e.tensor_copy(out=cvv[:, c, s3, :], in_=U[:, end - 2:end]),
                         reads=[B_u], writes=[B_cvo])
                P.op("vector", lambda e: e.tensor_scalar(out=CY[:, 0:2180], in0=U[:, 0:2180],
                                                         scalar1=CW[:, c * 3:c * 3 + 1], scalar2=None, op0=ALU.mult),
                     reads=[B_u, B_small], writes=[B_cy])
                P.op("vector", lambda e: e.scalar_tensor_tensor(out=CY[:, 0:2180], in0=U[:, 1:2181],
                                                                scalar=CW[:, c * 3 + 1:c * 3 + 2], in1=CY[:, 0:2180],
                                                                op0=ALU.mult, op1=ALU.add),
                     reads=[B_u, B_cy, B_small], writes=[B_cy])
                P.op("vector", lambda e: e.scalar_tensor_tensor(out=CY[:, 0:2180], in0=U[:, 2:2182],
                                                                scalar=CW[:, c * 3 + 2:c * 3 + 3], in1=CY[:, 0:2180],
                                                                op0=ALU.mult, op1=ALU.add),
                     reads=[B_u, B_cy, B_small], writes=[B_cy])
            items.append(taps)
            for tb, (t0, n) in enumerate(TBS):
                def ggrp(tb=tb, t0=t0, n=n):
                    bg = yield from proj_g(w["G"][0], w["G"][1], tb)
                    if split_act[0]:
                        return ("front", sigmoid_split(T2[:, 0:n], PB[bg][:, 0:n], [PBB[bg]], [B_t2]))
                    sigmoid_inplace(T2[:, 0:n], PB[bg][:, 0:n], [PBB[bg]], [B_t2])
                items.append(ggrp)

                def bgrp(tb=tb, t0=t0, n=n):
                    bb = yield from proj_g(w["B"][0], w["B"][1], tb)
                    for (lo, ln_, seq) in segs(tb):
                        cy0 = CYOFF[seq] + (t0 + lo - SEQ0[seq])
                        P.op("vector", lambda e, lo=lo, ln_=ln_, cy0=cy0: e.tensor_tensor(
                            out=T1[:, lo:lo + ln_], in0=PB[bb][:, lo:lo + ln_], in1=CY[:, cy0:cy0 + ln_], op=ALU.mult),
                            reads=[PBB[bb], B_cy], writes=[B_t1])
                    for hh in range(2):
                        r0 = hh * 64
                        for si, (lo, ln_, seq) in enumerate(segs(tb)):
                            key = (c, tb, hh, si)
                            if key not in mdone:
                                P.op("vector", lambda e, r0=r0, lo=lo, ln_=ln_: e.tensor_tensor(
                                    out=mv[r0:r0 + 64, c, t0 + lo:t0 + lo + ln_], in0=T1[r0:r0 + 64, lo:lo + ln_],
                                    in1=T2[r0:r0 + 64, lo:lo + ln_], op=ALU.mult),
                                    reads=[B_t1, B_t2], writes=[MB[c][tb]])
                                mdone.add(key)
                            else:
                                P.op("vector", lambda e, r0=r0, lo=lo, ln_=ln_: e.tensor_tensor(
                                    out=T1[r0:r0 + 64, lo:lo + ln_], in0=T1[r0:r0 + 64, lo:lo + ln_],
                                    in1=T2[r0:r0 + 64, lo:lo + ln_], op=ALU.mult),
                                    reads=[B_t1, B_t2], writes=[B_t1])
                                P.op("vector", lambda e, r0=r0, lo=lo, ln_=ln_: e.tensor_tensor(
                                    out=mv[r0:r0 + 64, c, t0 + lo:t0 + lo + ln_], in0=T1[r0:r0 + 64, lo:lo + ln_],
                                    in1=mv[r0:r0 + 64, c, t0 + lo:t0 + lo + ln_], op=ALU.add),
                                    reads=[B_t1, MB[c][tb]], writes=[MB[c][tb]])
                items.append(bgrp)
            return items

        def kv_items(c, st):
            items = []
            w = {}
            ka, va = k_augs[st], v_exts[st]

            def ld():
                w["K"] = wload(wcols(w_in, OFF_K + c * 128, 128), "k128")
                w["V"] = wload(wcols(w_in, OFF_V + c * 128, 128), "k128")
            items.append(ld)
            for tb, (t0, n) in enumerate(TBS):
                def kgrp(tb=tb, t0=t0, n=n):
                    bk = yield from proj_g(w["K"][0], w["K"][1], tb)
                    for hh in range(2):
                        P.op("vector", lambda e, hh=hh: e.tensor_copy(out=ka[0:64, hh, t0:t0 + n],
                                                                     in_=PB[bk][hh * 64:(hh + 1) * 64, 0:n]),
                             reads=[PBB[bk]], writes=[B_ks[st][hh][tb]])
                    i = cnt["stgm"] % 3
                    cnt["stgm"] += 1
                    P.op("vector", lambda e, i=i: e.tensor_copy(out=STGM[i][:, 0:n], in_=PB[bk][:, 0:n]),
                         reads=[PBB[bk]], writes=[B_stgm[i]])
                    out_toks.append(P.dma("sync", lambda e, i=i: e.dma_start(
                        out=kTo[c * 128:(c + 1) * 128, t0:t0 + n], in_=STGM[i][:, 0:n]), reads=[B_stgm[i]]))
                items.append(kgrp)
            for tg in range(5):
                def vgrp(tg=tg):
                    tiles = list(range(tg * 4, min(17, tg * 4 + 4)))
                    bv = next_p()
                    for qi, ti in enumerate(tiles):
                        tb = min(ti // 4, 4)
                        if qi in (2,):
                            yield
                        mm([(PB[bv][:, qi * 128:(qi + 1) * 128], H[:, k, ti * 128:(ti + 1) * 128],
                             w["V"][0][:, k, :], k == 0, k == 7) for k in range(8)],
                           [w["V"][1]] + [HB[k][tb] for k in range(8)], [PBB[bv]])
                    nt_ = len(tiles)
                    pv4 = PB[bv][:, 0:nt_ * 128].rearrange("p (i h d) -> p i h d", i=nt_, h=2)
                    P.op("vector", lambda e: e.tensor_copy(out=va[:, tiles[0]:tiles[0] + nt_, :, 0:64], in_=pv4),
                         reads=[PBB[bv]], writes=[B_vs[st][tg]])
                    i = cnt["stgm"] % 3
                    cnt["stgm"] += 1
                    P.op("vector", lambda e, i=i: e.tensor_copy(out=STGM[i][:, 0:nt_ * 128], in_=PB[bv][:, 0:nt_ * 128]),
                         reads=[PBB[bv]], writes=[B_stgm[i]])
                    t00 = tiles[0] * 128
                    out_toks.append(P.dma("sync", lambda e, i=i: e.dma_start(
                        out=vo[t00:t00 + nt_ * 128, c * 128:(c + 1) * 128].rearrange("(i p) f -> p i f", p=128),
                        in_=STGM[i][:, 0:nt_ * 128].rearrange("p (i f) -> p i f", i=nt_)), reads=[B_stgm[i]]))
                items.append(vgrp)
            return items

        split_act = [False]
        side_q = []
        side_done = set()

        pushed = [0]
        lag_q = []
        gstep = [0]

        def run_lag(force=False):
            while lag_q and (force or lag_q[0][0] <= gstep[0]):
                _, tag, fn = lag_q.pop(0)
                fn()
                if tag is not None:
                    side_done.add(tag)
                if force:
                    break

        import inspect as _insp

        def finish_item(tag, more):
            if isinstance(more, tuple) and more[0] == "lag":
                base = max(gstep[0], lag_q[-1][0] if lag_q else 0)
                for q_, f in enumerate(more[1]):
                    lag_q.append((base + 1 + q_, tag if q_ == len(more[1]) - 1 else None, f))
            elif isinstance(more, tuple) and more[0] == "front":
                side_q[0:0] = [(None, f) for f in more[1][:-1]] + [(tag, more[1][-1])]
                pushed[0] += len(more[1])
            elif tag is not None:
                side_done.add(tag)

        def side(n=1):
            for _ in range(n):
                if not side_q:
                    return
                tag, obj = side_q.pop(0)
                res = obj() if callable(obj) else obj
                if _insp.isgenerator(res):
                    try:
                        next(res)
                        side_q.insert(0, (tag, res))
                        pushed[0] += 1
                        continue
                    except StopIteration as e_:
                        res = e_.value
                finish_item(tag, res)

        def run_full(it):
            res = it()
            if _insp.isgenerator(res):
                try:
                    while True:
                        next(res)
                except StopIteration as e_:
                    res = e_.value
            if isinstance(res, tuple):
                for f in res[1]:
                    f()

        def ensure(tag):
            while tag not in side_done:
                if lag_q:
                    run_lag(force=True)
                else:
                    assert side_q, tag
                    side(1)

        B_lnd = B_ln
        B_rd = B_tmp[0]
        B_tt = B_tmp[1]

        def normalize(ob, c, hh, tbm, col0, ncol):
            r0 = hh * 64
            act(LND[64:128, 0:ncol], PB[ob][64:128, 0:ncol], AF.Ln, [PBB[ob]], [B_lnd])
            act(RD[64:128, 0:ncol], LND[64:128, 0:ncol], AF.Exp, [B_lnd], [B_rd], scale=-1.0)
            P.op("vector", lambda e: e.tensor_tensor(out=TT[r0:r0 + 64, 0:ncol], in0=PB[ob][0:64, 0:ncol],
                                                     in1=RD[64:128, 0:ncol], op=ALU.mult),
                 reads=[PBB[ob], B_rd], writes=[B_tt])
            key = (c, tbm, hh, 0 if tbm < 4 else (col0 - NPR) // SQ)
            if key not in mdone:
                P.op("vector", lambda e: e.tensor_tensor(out=mv[r0:r0 + 64, c, col0:col0 + ncol],
                                                         in0=TT[r0:r0 + 64, 0:ncol],
                                                         in1=SGA[r0:r0 + 64, col0:col0 + ncol], op=ALU.mult),
                     reads=[B_tt, B_sga[tbm]], writes=[MB[c][tbm]])
                mdone.add(key)
            else:
                P.op("vector", lambda e: e.tensor_tensor(out=TT[r0:r0 + 64, 0:ncol], in0=TT[r0:r0 + 64, 0:ncol],
                                                         in1=SGA[r0:r0 + 64, col0:col0 + ncol], op=ALU.mult),
                     reads=[B_tt, B_sga[tbm]], writes=[B_tt])
                P.op("vector", lambda e: e.tensor_tensor(out=mv[r0:r0 + 64, c, col0:col0 + ncol],
                                                         in0=TT[r0:r0 + 64, 0:ncol],
                                                         in1=mv[r0:r0 + 64, c, col0:col0 + ncol], op=ALU.add),
                     reads=[B_tt, MB[c][tbm]], writes=[MB[c][tbm]])

        qga_w = {}
        pend_norm = []

        def qga_load(c):
            qga_w[c] = (wload(wcols(w_in, OFF_Q + c * 128, 128), "k128"),
                        wload(wcols(w_in, OFF_GA + c * 128, 128), "k128"))

        def cache_loads(c):
            for s in range(2):
                for hh in range(2):
                    P.dma("gpsimd", lambda e, s=s, hh=hh: e.dma_start(
                        out=kc_aug[0:64, s, hh, :], in_=kcT[s, 2 * c + hh]), writes=[B_kc[s][hh]])
                P.dma("gpsimd", lambda e, s=s: e.dma_start(
                    out=VST[s],
                    in_=vc[s].rearrange("(i p) f -> p i f", p=128)[:, :, c * 128:(c + 1) * 128]),
                    writes=[B_vst[s]])

        def cache_scatter(c):
            for s in range(2):
                P.op("vector", lambda e, s=s: e.tensor_copy(
                    out=vc_ext[:, s, :, :, 0:64], in_=VST[s].rearrange("p i (h d) -> p i h d", h=2)),
                    reads=[B_vst[s]], writes=[B_vc[s][0], B_vc[s][1]])

        def qga_items(c):
            items = []
            (sQ, bQ), (sGa, bGa) = qga_w[c]
            for tb in (0, 3, 4, 1, 2):
                t0, n = TBS[tb]

                def qgrp(tb=tb, t0=t0, n=n):
                    bq = yield from proj_g(sQ, bQ, tb)
                    for hh in range(2):
                        P.op("vector", lambda e, hh=hh: e.tensor_scalar(
                            out=q_aug[0:64, hh, t0:t0 + n], in0=PB[bq][hh * 64:(hh + 1) * 64, 0:n],
                            scalar1=0.125, scalar2=None, op0=ALU.mult),
                            reads=[PBB[bq]], writes=[B_q[hh][tb]])
                items.append((("q", c, tb), qgrp))

                def ggrp(tb=tb, t0=t0, n=n):
                    bg = yield from proj_g(sGa, bGa, tb)
                    return ("lag", sigmoid_split(SGA[:, t0:t0 + n], PB[bg][:, 0:n], [PBB[bg]], [B_sga[tb]]))
                items.append((("ga", c, tb), ggrp))
            return items

        def pair_main(c, st, nside):
            ka, va = k_augs[st], v_exts[st]
            for hh in range(2):
                P.dma("sync", lambda e, hh=hh: e.dma_start(out=q_aug[64:65, hh, :],
                                                          in_=CT16[2 * c + hh:2 * c + hh + 1, :]),
                      reads=[B_ct], writes=[B_qrow[hh]])
            step = [0]
            nsteps = 88.0
            n0 = len(side_q)
            p0 = pushed[0]

            def flush_norm():
                if pend_norm:
                    a_ = pend_norm.pop(0)
                    ensure(("ga", a_[1], a_[3]))
                    normalize(*a_)

            def maybe_side():
                step[0] += 1
                gstep[0] += 1
                run_lag()
                tot = 2.4 * nside
                want = int(tot * step[0] / 88.0)
                while side_q and (n0 + (pushed[0] - p0) - len(side_q)) < want:
                    side(1)

            def sample_block(s, hh):
                ensure(("q", c, 4))
                ensure(("ga", c, 4))
                if s == 0 and hh == 0:
                    cache_scatter(c)
                while pend_norm:
                    flush_norm()
                head = 2 * c + hh
                qc0 = NPR + SQ * s
                sb1 = next_s()
                mm([(PB[sb1][:, i * 64:(i + 1) * 64], kc_aug[0:65, s, hh, i * 128:(i + 1) * 128],
                     q_aug[0:65, hh, qc0:qc0 + SQ], True, True) for i in range(8)],
                   [B_kc[s][hh], B_kcrow, B_q[hh][4], B_qrow[hh]], [PBB[sb1]])
                sb2 = next_s()
                mm([(PB[sb2][:, 0:64], ka[0:65, hh, NPR:NPR + 2 * SQ], q_aug[0:65, hh, qc0:qc0 + SQ], True, False),
                    (PB[sb2][:, 0:64], ident_bf[:], mask2[:, s * 64:(s + 1) * 64], False, True)],
                   [B_ks[st][hh][4], B_krows[st], B_q[hh][4], B_qrow[hh], B_const], [PBB[sb2]])
                pi = ptc[0] % 4
                ptc[0] += 1
                for i in range(8):
                    act(PT[pi][:, i * 64:(i + 1) * 64], PB[sb1][:, i * 64:(i + 1) * 64], AF.Exp,
                        [PBB[sb1], B_nfs], [B_pt[pi]],
                        bias=NFS[:, (s * 8 + i) * 16 + head:(s * 8 + i) * 16 + head + 1], scale=1.0)
                pi2 = ptc[0] % 4
                ptc[0] += 1
                act(PT[pi2][:, 0:64], PB[sb2][:, 0:64], AF.Exp, [PBB[sb2], B_nfs], [B_pt[pi2]],
                    bias=NFS[:, 256 + head:256 + head + 1], scale=1.0)
                maybe_side()
                ob = next_o()
                items = [(PB[ob][:, 0:64], vc_ext[:, s, i, hh, :], PT[pi][:, i * 64:(i + 1) * 64], i == 0, False)
                         for i in range(8)]
                items.append((PB[ob][:, 0:64], va[:, 16, hh, :], PT[pi2][:, 0:64], False, True))
                mm(items, [B_vc[s][hh], B_vcones, B_vs[st][4], B_vones_s[st], B_pt[pi], B_pt[pi2]], [PBB[ob]])
                maybe_side()
                pend_norm.append((ob, c, hh, 4, qc0, 64))


            cache_loads(c)
            for hh in range(2):
                head = 2 * c + hh
                for qb in (0, 3, 1, 2):
                    ntile = 4 * qb + 4
                    ob = next_o()
                    q0 = qb * 512
                    ensure(("q", c, qb))

                    def S(i, hh=hh, qb=qb, q0=q0):
                        sb_ = next_s()
                        r = i - 4 * qb
                        lo = max(r, 0) * 128
                        items = [(PB[sb_][:, lo:512], ka[0:65, hh, i * 128:(i + 1) * 128],
                                  q_aug[0:65, hh, q0 + lo:q0 + 512], True, r < 0)]
                        if r >= 0:
                            items.append((PB[sb_][:, lo:lo + 128], ident_bf[:], maskneg[:], False, True))
                        mm(items, [B_ks[st][hh][i // 4], B_krows[st], B_q[hh][qb], B_qrow[hh], B_const], [PBB[sb_]])
                        return sb_, lo

                    pend = [S(0), S(1), S(2)]
                    for i in range(ntile):
                        sb_, lo = pend.pop(0)
                        pi = ptc[0] % 4
                        ptc[0] += 1
                        act(PT[pi][:, lo:512], PB[sb_][:, lo:512], AF.Exp, [PBB[sb_], B_nf], [B_pt[pi]],
                            bias=(None if PROBE == "nobias" else NF[:, i * 16 + head:i * 16 + head + 1]), scale=1.0)
                        if i + 3 < ntile:
                            pend.append(S(i + 3))
                        if i == 3:
                            while pend_norm:
                                flush_norm()
                        maybe_side()
                        mm([(PB[ob][:, lo:512], va[:, i, hh, :], PT[pi][:, lo:512], i == 0, i == ntile - 1)],
                           [B_vs[st][i // 4], B_vones_s[st], B_pt[pi]], [PBB[ob]])
                    pend_norm.append((ob, c, hh, qb, q0, 512))
                    if qb == 3:
                        sample_block(0, hh)
                    elif qb == 1:
                        sample_block(1, hh)

        B_vst = [Buf(), Buf()]
        for it in kv_items(0, 0):
            run_full(it)
        const_memsets()
        def tr(items, reads, writes):
            def fn(e, items=items):
                ins = None
                for (o, i_, idn) in items:
                    ins = e.transpose(o, i_, idn)
                return ins
            return P.op("tensor", fn, reads=reads, writes=writes)

        idn16 = ident_f[0:16, 0:16]
        tr([(PB[2][:, i * 16:(i + 1) * 16], FP[0:16, i * 128:(i + 1) * 128], idn16) for i in range(16)],
           [B_fp, B_const], [PBB[2]])
        act(NF[:, :], PB[2][:, 0:256], AF.Copy, [PBB[2]], [B_nf], scale=-1.0)
        tr([(PB[3][:, (s * 8 + i) * 16:(s * 8 + i + 1) * 16], FS[0:16, s, i * 128:(i + 1) * 128], idn16)
            for s in range(2) for i in range(8)] +
           [(PB[3][:, 256:272], FNEW[0:16, :], idn16)],
           [B_fs, B_fnew, B_const], [PBB[3]])
        act(NFS[:, :], PB[3][:, 0:272], AF.Copy, [PBB[3]], [B_nfs], scale=-1.0)

        alias_seed([x_ for l_ in B_kc for x_ in l_] + [B_kcrow], [B_fp])
        alias_seed([B_cy], [B_fs])
        split_act[0] = True
        qga_load(0)
        for c in range(8):
            st = c % 2
            side_q.extend(qga_items(c))
            cv = conv_items(c)
            if c + 1 < 8:
                side_q.extend((None, f) for f in kv_items(c + 1, 1 - st))
            side_q.append((None, cv[0]))
            if c + 1 < 8:
                side_q.append((None, lambda c=c: qga_load(c + 1)))
            side_q.extend((None, f) for f in cv[1:])
            if PROBE == "serial":
                while side_q or lag_q:
                    gstep[0] += 1
                    run_lag()
                    side(1)
            pair_main(c, st, len(side_q))
            while side_q or lag_q:
                gstep[0] += 1
                run_lag()
                side(1)
        while pend_norm:
            a_ = pend_norm.pop(0)
            normalize(*a_)
        out_toks.append(P.dma("sync", lambda e: e.dma_start(out=cvo, in_=CVO[:, :]), reads=[B_cvo]))

        alias_seed([WSB[5]], [B_stgm[2]])
        alias_seed(WSB[6:11], [x_ for l_ in B_ks[1] for x_ in l_] + [B_krows[1]])
        alias_seed(WSB[11:16], B_vs[1] + [B_vones_s[1]])
        slot_active[0] = list(range(NSLOT))
        slot_next[0] = 0

        stop_at("attn")
        bar = P.barrier()
        for tb, (t0, n) in enumerate(TBS):
            P.dma("sync", lambda e, t0=t0, n=n: e.dma_start(out=xv[:, :, t0:t0 + n], in_=xs3[:, :, t0:t0 + n]),
                  writes=[XB[k][tb] for k in range(8)], waits=[t for t in bar + spill if t is not None])
        so = [wload(w_out[k * 128:(k + 1) * 128, :], "flat") for k in range(8)]
        for tb, (t0, n) in enumerate(TBS):
            for mo in range(8):
                bank = 4 + (cnt["stg"] % 2)
                cnt["stg"] += 1
                mm([(PB[bank][:, 0:n], so[k][0][:, mo * 128:(mo + 1) * 128], mv[:, k, t0:t0 + n], k == 0, k == 7)
                    for k in range(8)],
                   [so[k][1] for k in range(8)] + [MB[k][tb] for k in range(8)], [PBB[bank]])
                for (lo, ln_, seq) in segs(tb):
                    P.op("vector", lambda e, bank=bank, mo=mo, t0=t0, lo=lo, ln_=ln_, seq=seq: e.scalar_tensor_tensor(
                        out=xv[:, mo, t0 + lo:t0 + lo + ln_], in0=PB[bank][:, lo:lo + ln_],
                        scalar=mod_ap(5, mo, seq), in1=xv[:, mo, t0 + lo:t0 + lo + ln_],
                        op0=ALU.mult, op1=ALU.add),
                        reads=[PBB[bank], B_mod[5], XB[mo][tb]], writes=[XB[mo][tb]])
                if tb > 0 and mo == 3:
                    norm_tb(tb - 1, 2, 6)
        norm_tb(4, 2, 6)

        stop_at("wout")
        ffn(w2i, w2o, 4, B_der[4], after_tb=lambda tb: norm_tb(tb, None, None, final=True))

        P.stopped = False
        P.wait_only("sync", out_toks + P.barrier())
        block = ctx.enter_context(nc.Block())
        P.emit(block)
    return nc


_NC_CACHE = {}


def _prep(x_prompt, x_sample, cache_k, cache_v, cache_logf, state_conv, c_prompt, c_sample,
          w_ada, b_ada, g_ffn1, w_ffn1_in, w_ffn1_out, g_mix, w_in, b_f, conv_w, w_out,
          g_ffn2, w_ffn2_in, w_ffn2_out, g_final):
    f = lambda a: np.ascontiguousarray(np.asarray(a, dtype=np.float32))
    x_prompt, x_sample = f(x_prompt), f(x_sample)
    cache_k, cache_v, cache_logf, state_conv = f(cache_k), f(cache_v), f(cache_logf), f(state_conv)
    c_prompt, c_sample = f(c_prompt), f(c_sample)
    def pk(vec):
        return vec.reshape(8, 128).T
    b_ada3 = np.repeat(f(b_ada)[0].reshape(72, 128).T[:, :, None], 3, axis=2).reshape(128, 216)
    gains = np.stack([pk(f(g_ffn1)[0]), pk(f(g_mix)[0]), pk(f(g_ffn2)[0]), pk(f(g_final))], axis=1)
    g3 = np.repeat(gains[:, :, :, None], 3, axis=3).reshape(128, 96)
    conv_wT = np.stack([pk(f(conv_w)[0, j]) for j in range(3)], axis=2).reshape(128, 24)
    shared = {
        "w_ada": f(w_ada)[0], "b_ada3": f(b_ada3), "g3": f(g3),
        "w1i": f(w_ffn1_in)[0], "w1o": f(w_ffn1_out)[0], "w_in": f(w_in)[0], "w_out": f(w_out)[0],
        "w2i": f(w_ffn2_in)[0], "w2o": f(w_ffn2_out)[0],
        "b_f": f(b_f)[0].reshape(16, 1), "conv_wT": f(conv_wT),
    }
    in_maps = []
    for c in range(NCORES):
        s0, s1 = 2 * c, 2 * c + 1
        xt = np.concatenate([x_prompt[c], x_sample[s0], x_sample[s1]], axis=0)
        cs = np.stack([c_prompt[c], c_sample[s0], c_sample[s1]], axis=0)
        cT = cs.reshape(3, 8, 128).transpose(2, 1, 0).reshape(128, 24)
        kcT = cache_k[0, s0:s1 + 1].transpose(0, 1, 3, 2)
        vcc = cache_v[0, s0:s1 + 1].transpose(0, 2, 1, 3).reshape(2, PAST, D)
        sc = state_conv[0, s0:s1 + 1]
        scT = sc.reshape(2, 2, 8, 128).transpose(3, 2, 0, 1).reshape(128, 32)
        m = dict(shared)
        m.update({"xT": f(xt.T), "cT": f(cT), "kcT": f(kcT), "vc": f(vcc),
                  "lfc": f(cache_logf[0, s0:s1 + 1]), "scT": f(scT)})
        in_maps.append(m)
    return in_maps


def _assemble(R):
    y_prompt = np.empty((8, NPR, D), np.float32)
    y_sample = np.empty((16, SQ, D), np.float32)
    k_prompt = np.empty((1, 8, 16, NPR, 64), np.float32)
    v_prompt = np.empty((1, 8, 16, NPR, 64), np.float32)
    logf_prompt = np.empty((1, 8, 16, NPR), np.float32)
    conv_prompt = np.empty((1, 8, 2, D), np.float32)
    k_sample = np.empty((1, 16, 16, SQ, 64), np.float32)
    v_sample = np.empty((1, 16, 16, SQ, 64), np.float32)
    logf_sample = np.empty((1, 16, 16, SQ), np.float32)
    conv_sample = np.empty((1, 16, 2, D), np.float32)
    for c in range(NCORES):
        r = R[c]
        y = np.asarray(r["yT"]).T
        kk = np.asarray(r["kT"]).reshape(16, 64, NT).transpose(0, 2, 1)
        vv = np.asarray(r["vo"]).reshape(NT, 16, 64).transpose(1, 0, 2)
        lf = np.asarray(r["lf"])
        cv = np.asarray(r["cvT"]).reshape(128, 8, 3, 2).transpose(2, 3, 1, 0).reshape(3, 2, D)
        y_prompt[c] = y[:NPR]
        k_prompt[0, c] = kk[:, :NPR]
        v_prompt[0, c] = vv[:, :NPR]
        logf_prompt[0, c] = lf[:, :NPR]
        conv_prompt[0, c] = cv[0]
        for s in range(2):
            sl = slice(NPR + SQ * s, NPR + SQ * (s + 1))
            y_sample[2 * c + s] = y[sl]
            k_sample[0, 2 * c + s] = kk[:, sl]
            v_sample[0, 2 * c + s] = vv[:, sl]
            logf_sample[0, 2 * c + s] = lf[:, sl]
            conv_sample[0, 2 * c + s] = cv[1 + s]
    return (y_prompt, y_sample, k_prompt, v_prompt, logf_prompt, conv_prompt,
            k_sample, v_sample, logf_sample, conv_sample)


def kernel(**inputs):
    in_maps = _prep(**inputs)
    if "nc" not in _NC_CACHE:
        _NC_CACHE["nc"] = build_program()
    nc = _NC_CACHE["nc"]
    res = run_bass_kernel_spmd(nc, in_maps, core_ids=list(range(NCORES)))
    return _assemble(res.results)
```
